# Optimizing a Trainium2 kernel written in Bass

```python
import jax, jax.numpy as jnp
from jax import lax
import numpy as np

D_MODEL = 1024
BATCH = 8
SEQ = 8192
DEPTH = 2

MIX_WIDTH = D_MODEL
N_MIXERS = 4
GROUP_WIDTH = MIX_WIDTH // N_MIXERS
HEAD_DIM = 64
HEADS_PER_GROUP = GROUP_WIDTH // HEAD_DIM
CONV_WIDTH = 3
POOL_WINDOWS = (2, 4, 8, 16)
POOL_CH = GROUP_WIDTH // len(POOL_WINDOWS)
CHUNK = 128
D_FF = -(-8 * D_MODEL // (3 * 256)) * 256
N_PROJ_SLICES = 7
PROJ_WIDTH = N_PROJ_SLICES * GROUP_WIDTH
EPS = 1e-6

kernel_name = "hybrid_parallel_mixer_encoder"


def rmsnorm(x, g):
    xf = x.astype(jnp.float32)
    y = xf * lax.rsqrt(jnp.mean(xf * xf, axis=-1, keepdims=True) + EPS)
    return (y * g.astype(jnp.float32)).astype(x.dtype)


def layernorm_noaffine(x):
    xf = x.astype(jnp.float32)
    mu = jnp.mean(xf, axis=-1, keepdims=True)
    xc = xf - mu
    y = xc * lax.rsqrt(jnp.mean(xc * xc, axis=-1, keepdims=True) + EPS)
    return y.astype(x.dtype)


def short_conv(z, w):
    zp = jnp.pad(z, ((0, 0), (1, 1), (0, 0)))
    return zp[:, :-2] * w[0] + zp[:, 1:-1] * w[1] + zp[:, 2:] * w[2]


def multiscale_pool(p, pool_w, pool_scale):
    B, S, _ = p.shape
    pf = p.astype(jnp.float32)
    cs = jnp.concatenate([jnp.zeros_like(pf[:, :1]), jnp.cumsum(pf, axis=1)], axis=1)
    t = jnp.arange(S)
    outs = []
    for g, w in enumerate(POOL_WINDOWS):
        lo = jnp.clip(t - w // 2, 0, S)
        hi = jnp.clip(t + w // 2, 0, S)
        sl = slice(g * POOL_CH, (g + 1) * POOL_CH)
        csg = cs[..., sl]
        cnt = (hi - lo).astype(jnp.float32)[None, :, None]
        mean = (jnp.take(csg, hi, axis=1) - jnp.take(csg, lo, axis=1)) / cnt
        outs.append(mean - pf[..., sl])
    d = jnp.stack(outs, axis=2).astype(p.dtype)
    y = jnp.einsum("bsgc,gcd->bsgd", d, pool_w).reshape(B, S, GROUP_WIDTH)
    return y * pool_scale


def fourier_mix(f, fourier_w):
    B, S, _ = f.shape
    fh = f.astype(jnp.float32).reshape(B, S, HEADS_PER_GROUP, HEAD_DIM)
    spec = jnp.fft.fft2(fh, axes=(1, 3), norm="ortho").real.astype(f.dtype)
    return jnp.einsum("bshc,hcd->bshd", spec, fourier_w).reshape(B, S, GROUP_WIDTH)


def spatial_gate(u, v, spatial_w, spatial_b):
    B, S, _ = u.shape
    n = S // CHUNK
    vh = layernorm_noaffine(v.reshape(B, S, HEADS_PER_GROUP, HEAD_DIM))
    vh = vh.reshape(B, n, CHUNK, HEADS_PER_GROUP, HEAD_DIM)
    s = jnp.einsum("hpq,bnqhc->bnphc", spatial_w, vh) + spatial_b.T[None, None, :, :, None]
    return u * s.reshape(B, S, GROUP_WIDTH)


def hybrid_mixer(h, w_in, conv_w, pool_w, pool_scale, fourier_w, spatial_w, spatial_b,
                 group_norm_gain, w_out):
    B, S, _ = h.shape
    proj = h @ w_in
    b_gate, c_gate, z, p, f, u, v = jnp.split(proj, N_PROJ_SLICES, axis=-1)
    y_conv = b_gate * short_conv(c_gate * z, conv_w)
    y_pool = multiscale_pool(p, pool_w, pool_scale)
    y_four = fourier_mix(f, fourier_w)
    y_gmlp = spatial_gate(u, v, spatial_w, spatial_b)
    y = jnp.stack([y_conv, y_pool, y_four, y_gmlp], axis=2)
    y = rmsnorm(y, group_norm_gain.reshape(N_MIXERS, GROUP_WIDTH)).reshape(B, S, MIX_WIDTH)
    return y @ w_out


def swiglu(h, w_gate, w_up, w_down):
    return (jax.nn.silu(h @ w_gate) * (h @ w_up)) @ w_down


def setup_inputs(seed: int = 0) -> dict:
    key = jax.random.key(seed)
    ks = jax.random.split(key, 17)

    def nrm(k, shape, scale):
        return jax.random.normal(k, shape, jnp.float32) * scale

    def gain(k, shape):
        return 1.0 + nrm(k, shape, 0.05)

    return {
        "x": nrm(ks[0], (BATCH, SEQ, D_MODEL), 1.0),
        "pre_mix_gain": gain(ks[1], (DEPTH, D_MODEL)),
        "post_mix_gain": gain(ks[2], (DEPTH, D_MODEL)),
        "pre_ffn_gain": gain(ks[3], (DEPTH, D_MODEL)),
        "post_ffn_gain": gain(ks[4], (DEPTH, D_MODEL)),
        "w_in": nrm(ks[5], (DEPTH, D_MODEL, PROJ_WIDTH), D_MODEL ** -0.5),
        "conv_w": nrm(ks[6], (DEPTH, CONV_WIDTH, GROUP_WIDTH), CONV_WIDTH ** -0.5),
        "pool_w": nrm(ks[7], (DEPTH, len(POOL_WINDOWS), POOL_CH, POOL_CH), POOL_CH ** -0.5),
        "pool_scale": gain(ks[8], (DEPTH, GROUP_WIDTH)),
        "fourier_w": nrm(ks[9], (DEPTH, HEADS_PER_GROUP, HEAD_DIM, HEAD_DIM), HEAD_DIM ** -0.5),
        "spatial_w": nrm(ks[10], (DEPTH, HEADS_PER_GROUP, CHUNK, CHUNK), CHUNK ** -0.5),
        "spatial_b": 1.0 + nrm(ks[11], (DEPTH, HEADS_PER_GROUP, CHUNK), 0.02),
        "group_norm_gain": gain(ks[12], (DEPTH, MIX_WIDTH)),
        "w_out": nrm(ks[13], (DEPTH, MIX_WIDTH, D_MODEL), MIX_WIDTH ** -0.5),
        "w_gate": nrm(ks[14], (DEPTH, D_MODEL, D_FF), D_MODEL ** -0.5),
        "w_up": nrm(ks[15], (DEPTH, D_MODEL, D_FF), D_MODEL ** -0.5),
        "w_down": nrm(ks[16], (DEPTH, D_FF, D_MODEL), D_FF ** -0.5),
    }


def reference(x, pre_mix_gain, post_mix_gain, pre_ffn_gain, post_ffn_gain, w_in, conv_w,
              pool_w, pool_scale, fourier_w, spatial_w, spatial_b, group_norm_gain, w_out,
              w_gate, w_up, w_down):
    for l in range(DEPTH):
        h = rmsnorm(x, pre_mix_gain[l])
        m = hybrid_mixer(h, w_in[l], conv_w[l], pool_w[l], pool_scale[l], fourier_w[l],
                         spatial_w[l], spatial_b[l], group_norm_gain[l], w_out[l])
        x = x + rmsnorm(m, post_mix_gain[l])
        h = rmsnorm(x, pre_ffn_gain[l])
        f = swiglu(h, w_gate[l], w_up[l], w_down[l])
        x = x + rmsnorm(f, post_ffn_gain[l])
    return x
```

```python
from contextlib import ExitStack
import numpy as np
import ml_dtypes
import concourse.bass as bass
import concourse.mybir as mybir
from concourse.bass_utils import run_bass_kernel_spmd

F32 = mybir.dt.float32
BF16 = mybir.dt.bfloat16
AF = mybir.ActivationFunctionType
ALU = mybir.AluOpType
AX = mybir.AxisListType

D = 1024
DFF = 2816
NDC = DFF // 128
PW = 1792
EPS = 1e-6
ENGS = ("pe", "act", "dve", "pool", "sp")


class Buf:
    __slots__ = ("name", "w", "r", "semkey", "n", "last")

    def __init__(self, name):
        self.name = name
        self.w = None
        self.r = []
        self.semkey = None
        self.n = 0


class _FakeIns:
    def then_inc(self, *a, **k):
        return self


class _FakeEng:
    def __init__(self):
        self.calls = []

    def __getattr__(self, name):
        def f(*a, **k):
            self.calls.append((name, a, k))
            return _FakeIns()
        return f


def _free_size(ap):
    n = 1
    for d in ap.shape[1:]:
        n *= d
    return n


def _est_cost(eng, fn):
    fe = _FakeEng()
    fn(fe)
    t = 0.0
    for name, a, k in fe.calls:
        out = k.get("out", a[0] if a else None)
        F = _free_size(out)
        if name == "matmul":
            t += max(F, 48) * 0.43 + 14.0
        elif eng == "act":
            t += F * 0.87 + 200.0
        elif eng == "dve":
            t += F * 1.15 + 120.0
        elif eng == "pool":
            t += F * 3.5 + 250.0
        else:
            t += 60.0
    return t


SYNC_LAT = 150.0
DMA_LAT = 2200.0
DMA_BW = 120.0


class _Op:
    __slots__ = ("id", "eng", "fn", "preds", "cost", "dma", "owner", "nbytes", "tok", "finish", "prio", "succ", "npend", "ready")

    def __init__(self, id_, eng, fn, preds, cost, dma=False, owner=None, nbytes=0):
        self.id = id_
        self.eng = eng
        self.fn = fn
        self.preds = preds
        self.cost = cost
        self.dma = dma
        self.owner = owner
        self.nbytes = nbytes
        self.tok = None
        self.finish = 0.0
        self.prio = 0.0
        self.succ = []
        self.npend = 0
        self.ready = 0.0


class Sched:
    def __init__(self, nc, stack, reorder=True):
        self.nc = nc
        self.stack = stack
        self.reorder = reorder
        self.sems = {}
        self.ops = []
        self.region = []
        self.regions = []
        self.dma_bufs = []
        for e in ENGS:
            self.sems[e] = stack.enter_context(nc.semaphore("c_" + e))

    def _preds(self, reads, writes):
        p = set()
        for b in reads:
            if b.w is not None:
                p.add(b.w)
        for b in writes:
            if b.w is not None:
                p.add(b.w)
            p.update(b.r)
        return p

    def _commit(self, op, reads, writes):
        self.ops.append(op)
        self.region.append(op.id)
        for b in reads:
            b.r.append(op.id)
        for b in writes:
            b.w = op.id
            b.r = []
        return op.id

    def op(self, eng, fn, reads=(), writes=()):
        o = _Op(len(self.ops), eng, fn, self._preds(reads, writes), _est_cost(eng, fn))
        return self._commit(o, reads, writes)

    def dma(self, out, in_, reads=(), writes=(), q="sp", **kw):
        owner = writes[0] if writes else reads[0]
        if owner.semkey is None:
            owner.semkey = f"d{len(self.dma_bufs)}_" + owner.name
            self.sems[owner.semkey] = self.stack.enter_context(self.nc.semaphore(owner.semkey))
            self.dma_bufs.append(owner)
            owner.n = 0
            owner.last = None
        preds = self._preds(reads, writes)
        if owner.last is not None:
            preds.add(owner.last)
        nbytes = _free_size(out) * out.shape[0] * (2 if out.dtype == BF16 else 4)
        o = _Op(len(self.ops), q, (lambda e: e.dma_start(out=out, in_=in_, **kw)), preds,
                60.0 if q == "sp" else 1000.0, dma=True, owner=owner, nbytes=nbytes)
        owner.last = o.id
        return self._commit(o, reads, writes)

    def barrier(self):
        self.regions.append(self.region)
        self.region = []

    def _schedule(self, ids):
        ops = self.ops
        inreg = set(ids)
        if not self.reorder:
            return list(ids)
        for i in ids:
            o = ops[i]
            o.succ = []
            o.npend = 0
            o.ready = 0.0
        for i in ids:
            o = ops[i]
            for p in o.preds:
                if p in inreg:
                    ops[p].succ.append(i)
                    o.npend += 1
        for i in reversed(ids):
            o = ops[i]
            m = 0.0
            for sidx in o.succ:
                if ops[sidx].prio > m:
                    m = ops[sidx].prio
            o.prio = m + o.cost + (DMA_LAT if o.dma else 0.0)
        free = {e: 0.0 for e in ENGS}
        ready = {e: [] for e in ENGS}
        for i in ids:
            if ops[i].npend == 0:
                ready[ops[i].eng].append(i)
        order = []
        n = len(ids)
        while len(order) < n:
            best = None
            bkey = None
            for e in ENGS:
                fe = free[e]
                for i in ready[e]:
                    o = ops[i]
                    st = o.ready if o.ready > fe else fe
                    key = (st, -o.prio, i)
                    if bkey is None or key < bkey:
                        bkey = key
                        best = i
            o = ops[best]
            ready[o.eng].remove(best)
            st = bkey[0]
            free[o.eng] = st + o.cost
            if o.dma:
                o.finish = st + o.cost + DMA_LAT + o.nbytes / DMA_BW
            else:
                o.finish = st + o.cost
            order.append(best)
            for sidx in o.succ:
                so = ops[sidx]
                t = o.finish + SYNC_LAT
                if t > so.ready:
                    so.ready = t
                so.npend -= 1
                if so.npend == 0:
                    ready[so.eng].append(sidx)
        self.makespan = max(free.values())
        return order

    def emit(self):
        if self.region:
            self.regions.append(self.region)
            self.region = []
        nc = self.nc
        sems = self.sems
        ops = self.ops
        cnt = {e: 0 for e in ENGS}
        seen = {e: {} for e in ENGS}
        prog = {e: [] for e in ENGS}
        dcount = {}
        self.makespans = []

        def waits_for(eng, toks):
            need = {}
            for k, v in toks:
                if k == eng and eng in ("pe", "sp"):
                    continue
                if seen[eng].get(k, 0) >= v:
                    continue
                if need.get(k, 0) < v:
                    need[k] = v
            for k, v in need.items():
                seen[eng][k] = v
            return list(need.items())

        for ids in self.regions:
            order = self._schedule(ids)
            self.makespans.append(getattr(self, "makespan", 0.0))
            for i in order:
                o = ops[i]
                w = waits_for(o.eng, [ops[p].tok for p in o.preds])
                if o.dma:
                    k = o.owner.semkey
                    dcount[k] = dcount.get(k, 0) + 1
                    o.tok = (k, 16 * dcount[k])
                    prog[o.eng].append((w, o.fn, (k, 16)))
                else:
                    cnt[o.eng] += 1
                    o.tok = (o.eng, cnt[o.eng])
                    prog[o.eng].append((w, o.fn, (o.eng, 1)))
            toks = [(k, 16 * v) for k, v in dcount.items()]
            toks += [(e, cnt[e]) for e in ENGS if e != "sp" and cnt[e] > 0]
            w = waits_for("sp", toks)
            cnt["sp"] += 1
            tsp = ("sp", cnt["sp"])
            prog["sp"].append((w, (lambda e: e.nop()), ("sp", 1)))
            for e in ENGS:
                if e == "sp":
                    continue
                w = waits_for(e, [tsp] + toks)
                if w:
                    prog[e].append((w, None, None))

        import os as _os
        if _os.environ.get("KDEBUG"):
            print("KDEBUG counts", cnt, "max dma", max(dcount.values()) * 16, "nsems", len(sems), "makespans_us", [round(m / 1e3) for m in self.makespans])
            print("KDEBUG nwaits", {e: sum(len(w) for w, _, _ in prog[e]) for e in ENGS}, {e: len(prog[e]) for e in ENGS})

        def run(e, lst):
            for waits, fn, inc in lst:
                for k, v in waits:
                    e.wait_ge(sems[k], v)
                if fn is not None:
                    ins = fn(e)
                    if inc is not None:
                        ins.then_inc(sems[inc[0]], inc[1])

        with nc.Block() as block:
            @block.sync
            def _(e):
                run(e, prog["sp"])

            @block.tensor
            def _(e):
                run(e, prog["pe"])

            @block.scalar
            def _(e):
                run(e, prog["act"])

            @block.vector
            def _(e):
                run(e, prog["dve"])

            @block.gpsimd
            def _(e):
                run(e, prog["pool"])


def host_consts(S):
    J = S // 128
    bf = ml_dtypes.bfloat16
    c = {}
    c["ident"] = np.eye(128, dtype=np.float32).astype(bf)
    c["ones"] = np.ones((128, 128), dtype=np.float32).astype(bf)
    a = np.arange(64)
    ang = 2 * np.pi * np.outer(a, a) / 64.0
    C64 = np.cos(ang) / 8.0
    S64 = np.sin(ang) / 8.0
    z = np.zeros((64, 64))
    c["c2"] = np.block([[C64, z], [z, C64]]).astype(np.float32)
    c["s2n"] = np.block([[-S64, z], [z, -S64]]).astype(np.float32)
    q = np.arange(128)[:, None, None]
    j = np.arange(J)[None, :, None]
    k1 = np.arange(128)[None, None, :]
    m = (k1 * (J * q + j)) % S
    th = 2 * np.pi * m / S
    sc = 1.0 / np.sqrt(128.0)
    c["tc"] = (np.cos(th) * sc).reshape(128, J * 128).astype(np.float32).astype(bf)
    c["ts"] = (np.sin(th) * sc).reshape(128, J * 128).astype(np.float32).astype(bf)
    c["tsn"] = (-np.sin(th) * sc).reshape(128, J * 128).astype(np.float32).astype(bf)
    jj = np.arange(J)[:, None]
    k2 = np.arange(J)[None, :]
    th3 = 2 * np.pi * jj * k2 / J
    cs3 = np.zeros((J, 2, J))
    cs3[:, 0, :] = np.cos(th3) / np.sqrt(J)
    cs3[:, 1, :] = np.sin(th3) / np.sqrt(J)
    c["cs3"] = cs3.reshape(2 * J, J).astype(np.float32).astype(bf)
    ic = np.zeros((128, 2, 2, 8), dtype=np.float32)
    for mch in range(2):
        for half in range(2):
            w = (2, 4, 8, 16)[mch * 2 + half]
            for t in range(8):
                lo = max(t - w // 2, 0)
                hi = min(t + w // 2, S)
                ic[half * 64:(half + 1) * 64, mch, 0, t] = 1.0 / (hi - lo)
                tt = S - 8 + t
                lo = max(tt - w // 2, 0)
                hi = min(tt + w // 2, S)
                ic[half * 64:(half + 1) * 64, mch, 1, t] = 1.0 / (hi - lo)
    c["icnt"] = ic.reshape(128, 32)
    return c


CONST_SPECS = lambda J: {
    "ident": ([128, 128], BF16), "ones": ([128, 128], BF16),
    "c2": ([128, 128], F32), "s2n": ([128, 128], F32),
    "tc": ([128, J * 128], BF16), "ts": ([128, J * 128], BF16), "tsn": ([128, J * 128], BF16),
    "cs3": ([2 * J, J], BF16), "icnt": ([128, 32], F32),
}

PARAM_SPECS = lambda L: {
    "pre_mix_gain": [L, D], "post_mix_gain": [L, D], "pre_ffn_gain": [L, D], "post_ffn_gain": [L, D],
    "w_in": [L, D, PW], "conv_w": [L, 3, 256], "pool_w": [L, 4, 64, 64], "pool_scale": [L, 256],
    "fourier_w": [L, 4, 64, 64], "spatial_w": [L, 4, 128, 128], "spatial_b": [L, 4, 128],
    "group_norm_gain": [L, D], "w_out": [L, D, D], "w_gate": [L, D, DFF], "w_up": [L, D, DFF],
    "w_down": [L, DFF, D],
}


def build(S=8192, L=2, dbg=None):
    J = S // 128
    NB = S // 512
    J2 = 2 * J
    nc = bass.Bass("TRN2", target_bir_lowering=False)
    stack = ExitStack()
    with stack:
        dram = {}
        dram["x"] = nc.dram_tensor("x", [S, D], F32, kind="ExternalInput").ap()
        for k, shp in PARAM_SPECS(L).items():
            dram[k] = nc.dram_tensor(k, shp, F32, kind="ExternalInput").ap()
        for k, (shp, dt_) in CONST_SPECS(J).items():
            dram[k] = nc.dram_tensor(k, shp, dt_, kind="ExternalInput").ap()
        out_d = nc.dram_tensor("out", [S, D], F32, kind="ExternalOutput").ap()
        xmid = [nc.dram_tensor(f"xmid{l}", [S, D], F32, kind="Internal").ap() for l in range(max(L - 1, 1))]
        ysc = nc.dram_tensor("ysc", [NB, 128, 8, 512], BF16, kind="Internal").ap()
        wgs = nc.dram_tensor("wgs", [L, NDC, 128, 1024], BF16, kind="Internal").ap()
        wus = nc.dram_tensor("wus", [L, NDC, 128, 1024], BF16, kind="Internal").ap()
        wds = nc.dram_tensor("wds", [L, NDC, 128, 1024], BF16, kind="Internal").ap()
        wos = nc.dram_tensor("wos", [L, 8, 128, 1024], BF16, kind="Internal").ap()
        dbg_out = {}
        if dbg:
            for k, shp, dt_ in dbg:
                dbg_out[k] = nc.dram_tensor(k, shp, dt_, kind="ExternalOutput").ap()

        sc = Sched(nc, stack)

        def sb(name, shape, dt_):
            return stack.enter_context(nc.sbuf_tensor("s_" + name, shape, dt_))

        ident = sb("ident", [128, 128], BF16); b_ident = Buf("ident")
        ones = sb("ones", [128, 128], BF16); b_ones = Buf("ones")
        epsc = sb("epsc", [128, 1], F32); b_eps = Buf("eps")
        cw = sb("cw", [128, L, 3, 2], F32); b_cw = Buf("cw")
        psc = sb("psc", [128, L, 2], F32); b_psc = Buf("psc")
        gng = sb("gng", [128, L, 8], F32); b_gng = Buf("gng")
        bsp = sb("bsp", [128, L, 4], F32); b_bsp = Buf("bsp")
        icnt = sb("icnt", [128, 2, 2, 8], F32); b_icnt = Buf("icnt")
        ggm = sb("ggm", [128, 256], F32); b_ggm = Buf("ggm")

        psum = [stack.enter_context(nc.psum_tensor(f"ps{i}", [128, 1024], F32)) for i in range(4)]
        b_ps = [[Buf(f"ps{i}a"), Buf(f"ps{i}b")] for i in range(4)]
        ps_rr = [0]

        def next_ps():
            i = ps_rr[0] % 4
            ps_rr[0] += 1
            return psum[i], b_ps[i]

        sc.dma(ident[:, :], dram["ident"], writes=[b_ident])
        sc.dma(ones[:, :], dram["ones"], writes=[b_ones])
        sc.dma(icnt[:, :, :, :].rearrange("p a b c -> p (a b c)"), dram["icnt"], writes=[b_icnt])
        sc.op("pool", lambda e: e.memset(epsc[:, :], EPS), writes=[b_eps])
        sc.dma(cw[:, :, :, :], dram["conv_w"].rearrange("l t (m p) -> p l t m", p=128), writes=[b_cw],
               allow_slow_non_contiguous=True)
        sc.dma(psc[:, :, :], dram["pool_scale"].rearrange("l (m p) -> p l m", p=128), writes=[b_psc],
               allow_slow_non_contiguous=True)
        sc.dma(gng[:, :, :], dram["group_norm_gain"].rearrange("l (k p) -> p l k", p=128), writes=[b_gng],
               allow_slow_non_contiguous=True)
        sc.dma(bsp[:, :, :], dram["spatial_b"].rearrange("l h p -> p l h"), writes=[b_bsp],
               allow_slow_non_contiguous=True)

        def convert_gate_up(l, stg, b_stg):
            n = 0
            for src, dst in ((dram["w_gate"], wgs), (dram["w_up"], wus)):
                for c0 in range(NDC):
                    s_ = n % len(stg)
                    n += 1
                    sc.dma(stg[s_][:, :].rearrange("p (k n) -> p k n", k=8),
                           src[l][:, c0 * 128:(c0 + 1) * 128].rearrange("(k p) n -> p k n", p=128),
                           writes=[b_stg[s_]], q="pool")
                    sc.dma(dst[l, c0], stg[s_][:, :], reads=[b_stg[s_]], q="pool")

        def convert_down_out(l, stg, b_stg):
            n = 0
            for src, dst, nch in ((dram["w_out"], wos, 8), (dram["w_down"], wds, NDC)):
                for c0 in range(nch):
                    s_ = n % len(stg)
                    n += 1
                    sc.dma(stg[s_][:, :], src[l][c0 * 128:(c0 + 1) * 128, :], writes=[b_stg[s_]], q="pool")
                    sc.dma(dst[l, c0], stg[s_][:, :], reads=[b_stg[s_]], q="pool")

        def rstd_small(out_ap, in_ap, n, b_in, b_out, b_tmp, tmp_ap, lnexp=True):
            if lnexp:
                sc.op("act", lambda e: e.activation(out=tmp_ap, in_=in_ap, func=AF.Ln, scale=1.0 / n, bias=epsc[:, 0:1]),
                      reads=[b_in, b_eps], writes=[b_tmp])
                sc.op("act", lambda e: e.activation(out=out_ap, in_=tmp_ap, func=AF.Exp, scale=-0.5), reads=[b_tmp], writes=[b_out])
            else:
                sc.op("act", lambda e: e.activation(out=tmp_ap, in_=in_ap, func=AF.Sqrt, scale=1.0 / n, bias=epsc[:, 0:1]),
                      reads=[b_in, b_eps], writes=[b_tmp])
                sc.op("dve", lambda e: e.reciprocal(out=out_ap, in_=tmp_ap), reads=[b_tmp], writes=[b_out])

        def layer(l):
            x_in = dram["x"] if l == 0 else xmid[l - 1]
            x_out = out_d if l == L - 1 else xmid[l]
            fstack = ExitStack()
            fT = fstack.enter_context(nc.sbuf_tensor(f"fT_{l}", [128, 2, J, 128], BF16)); b_fT = Buf(f"fT{l}")
            fTflat = fT[:, :, :, :].rearrange("p m j q -> p m (j q)")
            sc.dma(ggm[:, :], dram["group_norm_gain"][l:l + 1, 768:1024].partition_broadcast(128), writes=[b_ggm])

            with ExitStack() as pa:
                def sa(name, shape, dt_):
                    return pa.enter_context(nc.sbuf_tensor(f"a_{name}_{l}", shape, dt_))
                w_in_sb = sa("w_in", [128, 8, PW], BF16); b_win = Buf(f"win{l}")
                gA = sa("gA", [128, D], F32); b_gA = Buf(f"gA{l}")
                sc.dma(gA[:, :], dram["pre_mix_gain"][l:l + 1, :].partition_broadcast(128), writes=[b_gA])
                sc.dma(w_in_sb[:, :, :], dram["w_in"][l].rearrange("(k p) n -> p k n", p=128), writes=[b_win], q="pool")
                pwbd = sa("pwbd", [128, 2, 128], BF16); b_pwbd = Buf(f"pwbd{l}")
                sc.op("pool", lambda e: e.memset(pwbd[:, :, :], 0.0), writes=[b_pwbd])
                for g in range(4):
                    h0 = (g % 2) * 64
                    sc.dma(pwbd[h0:h0 + 64, g // 2, h0:h0 + 64], dram["pool_w"][l, g], writes=[b_pwbd], q="pool")
                wsn = sa("wsn", [128, 4, 128], BF16); b_wsn = Buf(f"wsn{l}")
                sc.dma(wsn[:, :, :], dram["spatial_w"][l].rearrange("h p q -> p h q"), writes=[b_wsn], q="pool")
                wsT = sa("wsT", [128, 4, 128], BF16); b_wsT = Buf(f"wsT{l}")
                pt_, bp_ = next_ps()

                def f_wsT(e, pt_=pt_):
                    for h in range(4):
                        ins = e.matmul(pt_[:, h * 128:(h + 1) * 128], lhsT=wsn[:, h, :], rhs=ident[:, :], start=True, stop=True)
                    return ins
                sc.op("pe", f_wsT, reads=[b_wsn, b_ident], writes=bp_[:1])
                sc.op("dve", lambda e, pt_=pt_: e.tensor_copy(out=wsT[:, :, :].rearrange("p h q -> p (h q)"), in_=pt_[:, 0:512]),
                      reads=bp_[:1], writes=[b_wsT])

                stg = [sa(f"stg{i}", [128, 1024], BF16) for i in range(2)]
                b_stg = [Buf(f"stgA{l}_{i}") for i in range(2)]
                convert_gate_up(l, stg, b_stg)

                xt = [sa(f"xt{i}", [128, 4, D], F32) for i in range(2)]; b_xt = [Buf(f"xtA{l}_{i}") for i in range(2)]
                ssx = sa("ssx", [128, 4], F32); b_ssx = Buf("ssx")
                ssx2 = sa("ssx2", [128, 4], F32); b_ssx2 = Buf("ssx2")
                rsx = sa("rsx", [128, 4], F32); b_rsx = Buf("rsx")
                hb = sa("hb", [128, 4, D], BF16); b_hb = Buf("hb")
                hTx = [sa(f"hTx{i}", [128, 8, 528], BF16) for i in range(3)]; b_hTx = [Buf(f"hTx{i}") for i in range(3)]
                z_sb = sa("z_sb", [128, 528], F32); b_z = Buf("z")
                cz = sa("cz", [128, 528], F32); b_cz = Buf("cz")
                acc = sa("acc", [128, 512], F32); b_acc = Buf("acc")
                acc2 = sa("acc2", [128, 512], F32); b_acc2 = Buf("acc2")
                yfm = sa("yfm", [128, 4, 512], F32); b_yfm = [Buf(f"yfm{i}") for i in range(4)]
                p_sb = sa("p_sb", [128, 528], F32); b_p = Buf("p")
                Ra = sa("Ra", [128, 528], F32); b_Ra = Buf("Ra")
                Rb = sa("Rb", [128, 528], F32); b_Rb = Buf("Rb")
                Rc = sa("Rc", [128, 528], F32); b_Rc = Buf("Rc")
                Rd = Ra; b_Rd = b_Ra
                etmp = sa("etmp", [128, 8], F32); b_etmp = Buf("etmp")
                dpl = sa("dpl", [128, 2, 512], BF16); b_dpl = [Buf("dpl0"), Buf("dpl1")]
                sq = sa("sq", [128, 2, 512], BF16); b_sq = [Buf("sq0"), Buf("sq1")]
                sd = sa("sd", [128, 512], F32); b_sd = Buf("sd")
                rs = sa("rs", [128, 512], F32); b_rs = Buf("rs")
                yst = [sa(f"yst{i}", [128, 6, 512], BF16) for i in range(2)]; b_yst = [Buf(f"yst{i}") for i in range(2)]
                uv = sa("uv", [128, 4, 512], F32); b_uv = Buf("uv")
                sqv = sa("sqv", [128, 4, 256], F32); b_sqv = Buf("sqv")
                vst = sa("vst", [128, 6, 16], F32); b_vst = [Buf(f"vst{i}") for i in range(6)]
                vh = sa("vh", [128, 4, 256], BF16); b_vh = Buf("vh")
                yg = sa("yg", [128, 4, 256], F32); b_yg = Buf("yg")
                ygn = sa("ygn", [128, 4, 256], BF16); b_ygn = Buf("ygn")
                gst = sa("gst", [128, 3, 4], F32); b_gst = [Buf(f"gst{i}") for i in range(3)]

                def load_x(i):
                    s_ = i % 2
                    sc.dma(xt[s_][:, :, :], x_in[i * 512:(i + 1) * 512, :].rearrange("(c p) d -> p c d", p=128),
                           writes=[b_xt[s_]])

                def stageN(i):
                    s_ = i % 2
                    xs = xt[s_]
                    hs = i % 3
                    for c in range(4):
                        sc.op("act", lambda e, c=c: e.activation(out=hb[:, c, :], in_=xs[:, c, :], func=AF.Square,
                                                                 accum_out=ssx[:, c:c + 1]),
                              reads=[b_xt[s_]], writes=[b_hb, b_ssx])
                    rstd_small(rsx[:, :], ssx[:, :], D, b_ssx, b_rsx, b_ssx2, ssx2[:, :])
                    for c in range(4):
                        eng = "dve"
                        sc.op(eng, lambda e, c=c: e.scalar_tensor_tensor(out=hb[:, c, :], in0=xs[:, c, :], scalar=rsx[:, c:c + 1],
                                                                         in1=gA[:, :], op0=ALU.mult, op1=ALU.mult),
                              reads=[b_xt[s_], b_rsx, b_gA], writes=[b_hb])
                    if i == 0:
                        sc.op("pool", lambda e: e.memset(hTx[hs][:, :, 0:8], 0.0), writes=[b_hTx[hs]])
                    if i == NB - 1:
                        sc.op("pool", lambda e: e.memset(hTx[hs][:, :, 520:528], 0.0), writes=[b_hTx[hs]])
                    for c in range(4):
                        pt, bp = next_ps()

                        def f_tr(e, c=c, pt=pt):
                            for k in range(8):
                                ins = e.matmul(pt[:, k * 128:(k + 1) * 128], lhsT=hb[:, c, k * 128:(k + 1) * 128], rhs=ident[:, :],
                                               start=True, stop=True)
                            return ins
                        sc.op("pe", f_tr, reads=[b_hb, b_ident], writes=bp)
                        eng = "act" if c % 2 == 0 else "dve"
                        if eng == "act":
                            sc.op("act", lambda e, c=c, pt=pt: e.activation(out=hTx[hs][:, :, 8 + c * 128:8 + (c + 1) * 128],
                                                                          in_=pt[:, :].rearrange("p (k n) -> p k n", k=8), func=AF.Copy),
                                  reads=bp, writes=[b_hTx[hs]])
                        else:
                            sc.op("dve", lambda e, c=c, pt=pt: e.tensor_copy(out=hTx[hs][:, :, 8 + c * 128:8 + (c + 1) * 128],
                                                                           in_=pt[:, :].rearrange("p (k n) -> p k n", k=8)),
                                  reads=bp, writes=[b_hTx[hs]])
                    if i >= 1:
                        hp = (i - 1) % 3
                        sc.op("pool", lambda e: e.tensor_copy(out=hTx[hp][:, :, 520:528], in_=hTx[hs][:, :, 8:16]),
                              reads=[b_hTx[hs]], writes=[b_hTx[hp]])
                    if i + 1 < NB:
                        hn = (i + 1) % 3
                        sc.op("pool", lambda e: e.tensor_copy(out=hTx[hn][:, :, 0:8], in_=hTx[hs][:, :, 512:520]),
                              reads=[b_hTx[hs]], writes=[b_hTx[hn]])

                def proj_fm(i, m, halo):
                    hs = i % 3
                    pt, bp = next_ps()
                    if halo:
                        def f(e, pt=pt):
                            for k in range(8):
                                e.matmul(pt[:, 0:512], lhsT=w_in_sb[:, k, m * 128:(m + 1) * 128], rhs=hTx[hs][:, k, 0:512],
                                         start=(k == 0), stop=(k == 7))
                            for k in range(8):
                                ins = e.matmul(pt[:, 512:528], lhsT=w_in_sb[:, k, m * 128:(m + 1) * 128], rhs=hTx[hs][:, k, 512:528],
                                               start=(k == 0), stop=(k == 7))
                            return ins
                        sc.op("pe", f, reads=[b_win, b_hTx[hs]], writes=bp)
                    else:
                        def f(e, pt=pt):
                            for k in range(8):
                                ins = e.matmul(pt[:, 0:512], lhsT=w_in_sb[:, k, m * 128:(m + 1) * 128], rhs=hTx[hs][:, k, 8:520],
                                               start=(k == 0), stop=(k == 7))
                            return ins
                        sc.op("pe", f, reads=[b_win, b_hTx[hs]], writes=bp[:1])
                    return pt, bp

                def group_norm_fm(i, gidx, ys):
                    for mm in range(2):
                        sc.op("act", lambda e, mm=mm: e.activation(out=sq[:, mm, :], in_=yfm[:, 2 * gidx + mm, :], func=AF.Square),
                              reads=[b_yfm[2 * gidx + mm]], writes=[b_sq[mm]])
                    pt, bp = next_ps()

                    def f(e, pt=pt):
                        for mm in range(2):
                            ins = e.matmul(pt[:, 0:512], lhsT=ones[:, :], rhs=sq[:, mm, :], start=(mm == 0), stop=(mm == 1))
                        return ins
                    sc.op("pe", f, reads=[b_ones] + b_sq, writes=bp[:1])
                    sc.op("act", lambda e, pt=pt: e.activation(out=sd[:, :], in_=pt[:, 0:512], func=AF.Ln, scale=1.0 / 256,
                                                             bias=epsc[:, 0:1]), reads=bp[:1] + [b_eps], writes=[b_sd])
                    sc.op("act", lambda e: e.activation(out=rs[:, :], in_=sd[:, :], func=AF.Exp, scale=-0.5), reads=[b_sd], writes=[b_rs])
                    for mm in range(2):
                        kc = 2 * gidx + mm
                        sc.op("dve",
                              lambda e, mm=mm, kc=kc: e.scalar_tensor_tensor(out=yst[ys][:, kc, :], in0=yfm[:, kc, :],
                                                                             scalar=gng[:, l, kc:kc + 1], in1=rs[:, :],
                                                                             op0=ALU.mult, op1=ALU.mult),
                              reads=[b_yfm[kc], b_gng, b_rs], writes=[b_yst[ys]])

                def stageP(i):
                    hs = i % 3
                    ys = i % 2
                    for mm in range(2):
                        pz, bz = proj_fm(i, 4 + mm, True)
                        sc.op("act", lambda e, pz=pz: e.activation(out=z_sb[:, :], in_=pz[:, 0:528], func=AF.Copy), reads=bz, writes=[b_z])
                        pc, bc = proj_fm(i, 2 + mm, True)
                        sc.op("dve", lambda e, pc=pc: e.tensor_tensor(out=cz[:, :], in0=pc[:, 0:528], in1=z_sb[:, :], op=ALU.mult),
                              reads=bc + [b_z], writes=[b_cz])
                        sc.op("act", lambda e, mm=mm: e.activation(out=acc[:, :], in_=cz[:, 8:520], func=AF.Copy, scale=cw[:, l, 1, mm:mm + 1]),
                              reads=[b_cz, b_cw], writes=[b_acc])
                        sc.op("dve", lambda e, mm=mm: e.scalar_tensor_tensor(out=acc2[:, :], in0=cz[:, 7:519], scalar=cw[:, l, 0, mm:mm + 1],
                                                                             in1=acc[:, :], op0=ALU.mult, op1=ALU.add),
                              reads=[b_cz, b_cw, b_acc], writes=[b_acc2])
                        sc.op("dve", lambda e, mm=mm: e.scalar_tensor_tensor(out=acc[:, :], in0=cz[:, 9:521], scalar=cw[:, l, 2, mm:mm + 1],
                                                                            in1=acc2[:, :], op0=ALU.mult, op1=ALU.add),
                              reads=[b_cz, b_cw, b_acc2], writes=[b_acc])
                        pb, bb = proj_fm(i, mm, False)
                        sc.op("dve", lambda e, mm=mm, pb=pb: e.tensor_tensor(out=yfm[:, mm, :], in0=pb[:, 0:512], in1=acc[:, :], op=ALU.mult),
                              reads=bb[:1] + [b_acc], writes=[b_yfm[mm]])
                    for mm in range(2):
                        pp, bpp = proj_fm(i, 6 + mm, True)
                        sc.op("act", lambda e, pp=pp: e.activation(out=p_sb[:, :], in_=pp[:, 0:528], func=AF.Copy), reads=bpp, writes=[b_p])
                        sc.op("pool" if mm == 0 else "dve", lambda e: e.tensor_tensor(out=Ra[:, 0:527], in0=p_sb[:, 0:527], in1=p_sb[:, 1:528], op=ALU.add),
                              reads=[b_p], writes=[b_Ra])
                        sc.op("pool" if mm == 0 else "dve", lambda e: e.tensor_tensor(out=Rb[:, 0:525], in0=Ra[:, 0:525], in1=Ra[:, 2:527], op=ALU.add),
                              reads=[b_Ra], writes=[b_Rb])
                        if mm == 0:
                            wins = ((0, 64, Ra, b_Ra, 2), (64, 128, Rb, b_Rb, 4))
                        else:
                            sc.op("dve", lambda e: e.tensor_tensor(out=Rc[:, 0:521], in0=Rb[:, 0:521], in1=Rb[:, 4:525], op=ALU.add),
                                  reads=[b_Rb], writes=[b_Rc])
                            sc.op("dve", lambda e: e.tensor_tensor(out=Rd[:, 0:513], in0=Rc[:, 0:513], in1=Rc[:, 8:521], op=ALU.add),
                                  reads=[b_Rc], writes=[b_Rd])
                            wins = ((0, 64, Rc, b_Rc, 8), (64, 128, Rd, b_Rd, 16))
                        for (p0, p1, R, bR, w) in wins:
                            o = 8 - w // 2
                            sc.op("dve", lambda e, p0=p0, p1=p1, R=R, w=w, o=o, mm=mm: e.scalar_tensor_tensor(
                                out=dpl[p0:p1, mm, :], in0=R[p0:p1, o:o + 512], scalar=1.0 / w, in1=p_sb[p0:p1, 8:520],
                                op0=ALU.mult, op1=ALU.subtract), reads=[bR, b_p], writes=[b_dpl[mm]])
                            for side, blk in ((0, 0), (1, NB - 1)):
                                if i != blk:
                                    continue
                                c0 = 0 if side == 0 else 504
                                sc.op("dve", lambda e, p0=p0, p1=p1, R=R, o=o, c0=c0, side=side, mm=mm: e.tensor_tensor(
                                    out=etmp[p0:p1, :], in0=R[p0:p1, o + c0:o + c0 + 8], in1=icnt[p0:p1, mm, side, :], op=ALU.mult),
                                    reads=[bR, b_icnt], writes=[b_etmp])
                                sc.op("dve", lambda e, p0=p0, p1=p1, c0=c0, mm=mm: e.tensor_tensor(
                                    out=dpl[p0:p1, mm, c0:c0 + 8], in0=etmp[p0:p1, :], in1=p_sb[p0:p1, 8 + c0:16 + c0], op=ALU.subtract),
                                    reads=[b_etmp, b_p], writes=[b_dpl[mm]])
                    pt, bp = next_ps()

                    def f_pw(e, pt=pt):
                        for mm in range(2):
                            ins = e.matmul(pt[:, mm * 512:(mm + 1) * 512], lhsT=pwbd[:, mm, :], rhs=dpl[:, mm, :], start=True, stop=True)
                        return ins
                    sc.op("pe", f_pw, reads=[b_pwbd] + b_dpl, writes=bp)
                    for mm in range(2):
                        sc.op("act", lambda e, mm=mm, pt=pt: e.activation(out=yfm[:, 2 + mm, :], in_=pt[:, mm * 512:(mm + 1) * 512],
                                                                        func=AF.Copy, scale=psc[:, l, mm:mm + 1]),
                              reads=[bp[mm], b_psc], writes=[b_yfm[2 + mm]])
                    for mm in range(2):
                        pf, bf_ = proj_fm(i, 8 + mm, False)
                        nq = 512 // J
                        sc.op("act", lambda e, mm=mm, pf=pf: e.activation(
                            out=fT[:, mm, :, i * nq:(i + 1) * nq], in_=pf[:, 0:512].rearrange("p (q j) -> p j q", j=J), func=AF.Copy),
                            reads=bf_[:1], writes=[b_fT])
                    for c in range(4):
                        pt, bp = next_ps()

                        def f_uv(e, c=c, pt=pt):
                            for k in range(8):
                                ins = e.matmul(pt[:, 0:512], lhsT=hTx[hs][:, k, 8 + c * 128:8 + (c + 1) * 128], rhs=w_in_sb[:, k, 1280:1792],
                                               start=(k == 0), stop=(k == 7))
                            return ins
                        sc.op("pe", f_uv, reads=[b_win, b_hTx[hs]], writes=bp[:1])
                        sc.op("act", lambda e, c=c, pt=pt: e.activation(out=uv[:, c, :], in_=pt[:, 0:512], func=AF.Copy),
                              reads=bp[:1], writes=[b_uv])
                    group_norm_fm(i, 0, ys)
                    group_norm_fm(i, 1, ys)
                    v4 = uv[:, :, 256:512].rearrange("p n (h c) -> p n h c", h=4)
                    sc.op("dve", lambda e: e.tensor_reduce(out=vst[:, 0, :].rearrange("p (n h) -> p n h", n=4), in_=v4, axis=AX.X, op=ALU.add),
                          reads=[b_uv], writes=[b_vst[0]])
                    sc.op("act", lambda e: e.activation(out=sqv[:, :, :], in_=uv[:, :, 256:512], func=AF.Square), reads=[b_uv], writes=[b_sqv])
                    sc.op("dve", lambda e: e.tensor_reduce(out=vst[:, 1, :].rearrange("p (n h) -> p n h", n=4),
                                                           in_=sqv[:, :, :].rearrange("p n (h c) -> p n h c", h=4), axis=AX.X, op=ALU.add),
                          reads=[b_sqv], writes=[b_vst[1]])
                    sc.op("pool", lambda e: e.tensor_scalar(out=vst[:, 2, :], in0=vst[:, 0, :], scalar1=1.0 / 64, scalar2=None, op0=ALU.mult),
                          reads=[b_vst[0]], writes=[b_vst[2]])
                    sc.op("pool", lambda e: e.tensor_tensor(out=vst[:, 3, :], in0=vst[:, 2, :], in1=vst[:, 2, :], op=ALU.mult),
                          reads=[b_vst[2]], writes=[b_vst[3]])
                    sc.op("dve", lambda e: e.scalar_tensor_tensor(out=vst[:, 4, :], in0=vst[:, 1, :], scalar=1.0 / 64, in1=vst[:, 3, :],
                                                                  op0=ALU.mult, op1=ALU.subtract), reads=[b_vst[1], b_vst[3]], writes=[b_vst[4]])
                    sc.op("act", lambda e: e.activation(out=vst[:, 3, :], in_=vst[:, 4, :], func=AF.Ln, scale=1.0, bias=epsc[:, 0:1]),
                          reads=[b_vst[4], b_eps], writes=[b_vst[3]])
                    sc.op("act", lambda e: e.activation(out=vst[:, 5, :], in_=vst[:, 3, :], func=AF.Exp, scale=-0.5), reads=[b_vst[3]], writes=[b_vst[5]])
                    mean_b = vst[:, 2, :].rearrange("p (n h) -> p n h", n=4).unsqueeze(3).to_broadcast([128, 4, 4, 64])
                    rstd_b = vst[:, 5, :].rearrange("p (n h) -> p n h", n=4).unsqueeze(3).to_broadcast([128, 4, 4, 64])
                    sqv4 = sqv[:, :, :].rearrange("p n (h c) -> p n h c", h=4)
                    sc.op("dve", lambda e: e.tensor_tensor(out=sqv4, in0=v4, in1=mean_b, op=ALU.subtract),
                          reads=[b_uv, b_vst[2]], writes=[b_sqv])
                    sc.op("dve", lambda e: e.tensor_tensor(out=vh[:, :, :].rearrange("p n (h c) -> p n h c", h=4), in0=sqv4, in1=rstd_b, op=ALU.mult),
                          reads=[b_sqv, b_vst[5]], writes=[b_vh])
                    pt, bp = next_ps()

                    def f_sp(e, pt=pt):
                        for h in range(4):
                            ins = e.matmul(pt[:, h * 256:(h + 1) * 256], lhsT=wsT[:, h, :], rhs=vh[:, :, h * 64:(h + 1) * 64], start=True, stop=True)
                        return ins
                    sc.op("pe", f_sp, reads=[b_wsT, b_vh], writes=bp)
                    for h in range(4):
                        sc.op("dve", lambda e, h=h, pt=pt: e.scalar_tensor_tensor(
                            out=yg[:, :, h * 64:(h + 1) * 64], in0=pt[:, h * 256:(h + 1) * 256].rearrange("p (n c) -> p n c", n=4),
                            scalar=bsp[:, l, h:h + 1], in1=uv[:, :, h * 64:(h + 1) * 64], op0=ALU.add, op1=ALU.mult),
                            reads=[bp[h // 2], b_bsp, b_uv], writes=[b_yg])
                    sc.op("act", lambda e: e.activation(out=sqv[:, :, :], in_=yg[:, :, :], func=AF.Square), reads=[b_yg], writes=[b_sqv])
                    sc.op("dve", lambda e: e.tensor_reduce(out=gst[:, 0, :], in_=sqv[:, :, :], axis=AX.X, op=ALU.add), reads=[b_sqv], writes=[b_gst[0]])
                    rstd_small(gst[:, 2, :], gst[:, 0, :], 256, b_gst[0], b_gst[2], b_gst[1], gst[:, 1, :])
                    for n in range(4):
                        sc.op("dve", lambda e, n=n: e.scalar_tensor_tensor(out=ygn[:, n, :], in0=yg[:, n, :], scalar=gst[:, 2, n:n + 1], in1=ggm[:, :],
                                                                          op0=ALU.mult, op1=ALU.mult), reads=[b_yg, b_gst[2], b_ggm], writes=[b_ygn])
                    pt, bp = next_ps()

                    def f_gt(e, pt=pt):
                        for mm in range(2):
                            for n in range(4):
                                ins = e.matmul(pt[:, mm * 512 + n * 128: mm * 512 + (n + 1) * 128], lhsT=ygn[:, n, mm * 128:(mm + 1) * 128],
                                               rhs=ident[:, :], start=True, stop=True)
                        return ins
                    sc.op("pe", f_gt, reads=[b_ygn, b_ident], writes=bp)
                    sc.op("act", lambda e, pt=pt: e.activation(out=yst[ys][:, 4:6, :], in_=pt[:, :].rearrange("p (m t) -> p m t", m=2), func=AF.Copy),
                          reads=bp, writes=[b_yst[ys]])
                    sc.dma(ysc[i, :, 0:6, :], yst[ys][:, :, :], reads=[b_yst[ys]])

                load_x(0)
                if NB > 1:
                    load_x(1)
                for i in range(NB + 1):
                    if i < NB:
                        stageN(i)
                        if i + 2 < NB:
                            load_x(i + 2)
                    if i >= 1:
                        stageP(i - 1)
                if dbg and "dbg_fT" in dbg_out and l == 0:
                    sc.barrier()
                    sc.dma(dbg_out["dbg_fT"], fTflat, reads=[b_fT])
                sc.barrier()
            with ExitStack() as pb_:
                def sbb(name, shape, dt_):
                    return pb_.enter_context(nc.sbuf_tensor(f"b_{name}_{l}", shape, dt_))
                tcs = sbb("tc", [128, J, 128], BF16); b_tc = Buf(f"tc{l}")
                tss = sbb("ts", [128, J, 128], BF16); b_ts = Buf(f"ts{l}")
                tsn = sbb("tsn", [128, J, 128], BF16); b_tsn = Buf(f"tsn{l}")
                cs3 = sbb("cs3", [J2, J], BF16); b_cs3 = Buf(f"cs3{l}")
                c2 = sbb("c2", [128, 128], F32); b_c2 = Buf(f"c2{l}")
                s2n = sbb("s2n", [128, 128], F32); b_s2n = Buf(f"s2n{l}")
                fw2 = sbb("fw2", [128, 2, 128], F32); b_fw2 = Buf(f"fw2{l}")
                Dh = sbb("Dh", [128, 2, 256], BF16); b_Dh = Buf(f"Dh{l}")
                G = sbb("G", [128, 2, J, 256], BF16); b_G = [Buf(f"G0{l}"), Buf(f"G1{l}")]
                A = sbb("A", [128, J, 2, 128], BF16); b_A = Buf(f"A{l}")
                Ap_t = [sbb(f"Ap{i}", [J2, 128, 128], BF16) for i in range(2)] if J * 256 < 128 * 128 else None
                sqB = sbb("sqB", [128, 2, 512], BF16); b_sqB = [Buf("sqB0"), Buf("sqB1")]
                sdB = sbb("sdB", [128, 512], F32); b_sdB = Buf("sdB")
                rsB = sbb("rsB", [128, 512], F32); b_rsB = Buf("rsB")
                yfo = [sbb(f"yfo{i}", [128, 2, 512], BF16) for i in range(2)]; b_yfo = [Buf(f"yfo{i}") for i in range(2)]
                stgB = [sbb(f"stgB{i}", [128, 1024], BF16) for i in range(3)]
                b_stgB = [Buf(f"stgB{l}_{i}") for i in range(3)]
                convert_down_out(l, stgB, b_stgB)
                sc.dma(tcs[:, :, :].rearrange("p j k -> p (j k)"), dram["tc"], writes=[b_tc])
                sc.dma(tss[:, :, :].rearrange("p j k -> p (j k)"), dram["ts"], writes=[b_ts])
                sc.dma(tsn[:, :, :].rearrange("p j k -> p (j k)"), dram["tsn"], writes=[b_tsn])
                sc.dma(cs3[:, :], dram["cs3"], writes=[b_cs3])
                sc.dma(c2[:, :], dram["c2"], writes=[b_c2])
                sc.dma(s2n[:, :], dram["s2n"], writes=[b_s2n])
                sc.op("pool", lambda e: e.memset(fw2[:, :, :], 0.0), writes=[b_fw2])
                for h in range(4):
                    h0 = (h % 2) * 64
                    sc.dma(fw2[h0:h0 + 64, h // 2, h0:h0 + 64], dram["fourier_w"][l, h], writes=[b_fw2])
                for half in range(2):
                    pt, bp = next_ps()

                    def f_D(e, pt=pt, half=half):
                        e.matmul(pt[:, 0:128], lhsT=c2[:, :], rhs=fw2[:, half, :], start=True, stop=True)
                        return e.matmul(pt[:, 128:256], lhsT=s2n[:, :], rhs=fw2[:, half, :], start=True, stop=True)
                    sc.op("pe", f_D, reads=[b_c2, b_s2n, b_fw2], writes=bp[:1])
                    sc.op("dve", lambda e, pt=pt, half=half: e.tensor_copy(out=Dh[:, half, :], in_=pt[:, 0:256]), reads=bp[:1], writes=[b_Dh])
                for j in range(J):
                    pt, bp = next_ps()

                    def f_s1(e, pt=pt, j=j):
                        for half in range(2):
                            ins = e.matmul(pt[:, half * 256:(half + 1) * 256], lhsT=fT[:, half, j, :], rhs=Dh[:, half, :], start=True, stop=True)
                        return ins
                    sc.op("pe", f_s1, reads=[b_fT, b_Dh], writes=bp[:1])
                    eng = "act" if j % 2 == 0 else "dve"
                    if eng == "act":
                        sc.op("act", lambda e, pt=pt, j=j: e.activation(out=G[:, :, j, :], in_=pt[:, 0:512].rearrange("p (h c) -> p h c", h=2), func=AF.Copy),
                              reads=bp[:1], writes=b_G)
                    else:
                        sc.op("dve", lambda e, pt=pt, j=j: e.tensor_copy(out=G[:, :, j, :], in_=pt[:, 0:512].rearrange("p (h c) -> p h c", h=2)),
                              reads=bp[:1], writes=b_G)
                yT4 = fTflat
                for half in range(2):
                    for j0 in range(0, J, 2):
                        pt, bp = next_ps()

                        def f_s2(e, pt=pt, j0=j0, half=half):
                            for jj in range(2):
                                j = j0 + jj
                                o = jj * 256
                                e.matmul(pt[:, o:o + 128], lhsT=tcs[:, j, :], rhs=G[:, half, j, 0:128], start=True, stop=False)
                                e.matmul(pt[:, o:o + 128], lhsT=tss[:, j, :], rhs=G[:, half, j, 128:256], start=False, stop=True)
                                e.matmul(pt[:, o + 128:o + 256], lhsT=tcs[:, j, :], rhs=G[:, half, j, 128:256], start=True, stop=False)
                                ins = e.matmul(pt[:, o + 128:o + 256], lhsT=tsn[:, j, :], rhs=G[:, half, j, 0:128], start=False, stop=True)
                            return ins
                        sc.op("pe", f_s2, reads=[b_tc, b_ts, b_tsn, b_G[half]], writes=bp[:1])
                        eng = "act" if (j0 // 2) % 2 == 0 else "dve"
                        o_ap = A[:, j0:j0 + 2, :, :]
                        i_ap = lambda pt: pt[:, 0:512].rearrange("p (j r c) -> p j r c", j=2, r=2)
                        if eng == "act":
                            sc.op("act", lambda e, pt=pt, o_ap=o_ap: e.activation(out=o_ap, in_=i_ap(pt), func=AF.Copy), reads=bp[:1], writes=[b_A])
                        else:
                            sc.op("dve", lambda e, pt=pt, o_ap=o_ap: e.tensor_copy(out=o_ap, in_=i_ap(pt)), reads=bp[:1], writes=[b_A])
                    if J * 256 >= 128 * 128:
                        Ap = G[0:J2, half, :, :].rearrange("p j c -> p (j c)")[:, 0:128 * 128].rearrange("p (c k) -> p c k", c=128)
                    else:
                        Ap = Ap_t[half][:, :, :]
                    for c0 in range(0, 128, 8):
                        pt, bp = next_ps()

                        def f_tr(e, pt=pt, c0=c0):
                            for cc in range(8):
                                ins = e.matmul(pt[0:J2, cc * 128:(cc + 1) * 128], lhsT=A[:, :, :, c0 + cc].rearrange("p j r -> p (j r)"), rhs=ident[:, :], start=True, stop=True)
                            return ins
                        sc.op("pe", f_tr, reads=[b_A, b_ident], writes=bp)
                        eng = "act" if (c0 // 8) % 2 == 0 else "dve"
                        o_ap = Ap[:, c0:c0 + 8, :]
                        if eng == "act":
                            sc.op("act", lambda e, pt=pt, o_ap=o_ap: e.activation(out=o_ap, in_=pt[0:J2, :].rearrange("p (c k) -> p c k", c=8), func=AF.Copy),
                                  reads=bp, writes=[b_G[half]])
                        else:
                            sc.op("dve", lambda e, pt=pt, o_ap=o_ap: e.tensor_copy(out=o_ap, in_=pt[0:J2, :].rearrange("p (c k) -> p c k", c=8)),
                                  reads=bp, writes=[b_G[half]])
                    KB = min(128, 512 // J)
                    for k0 in range(0, 128, KB):
                        pt, bp = next_ps()

                        def f_s3(e, pt=pt, k0=k0, Ap=Ap):
                            for kk in range(KB):
                                ins = e.matmul(pt[:, kk * J:(kk + 1) * J], lhsT=Ap[:, :, k0 + kk], rhs=cs3[:, :], start=True, stop=True)
                            return ins
                        sc.op("pe", f_s3, reads=[b_G[half], b_cs3], writes=bp[:1])
                        eng = "act" if (k0 // KB) % 2 == 0 else "dve"
                        o_ap = yT4[:, half, :].rearrange("p (k2 k1) -> p k1 k2", k1=128)[:, k0:k0 + KB, :]
                        if eng == "act":
                            sc.op("act", lambda e, pt=pt, o_ap=o_ap: e.activation(out=o_ap, in_=pt[:, 0:KB * J].rearrange("p (k a) -> p k a", k=KB), func=AF.Copy),
                                  reads=bp[:1], writes=[b_fT])
                        else:
                            sc.op("dve", lambda e, pt=pt, o_ap=o_ap: e.tensor_copy(out=o_ap, in_=pt[:, 0:KB * J].rearrange("p (k a) -> p k a", k=KB)),
                                  reads=bp[:1], writes=[b_fT])
                if dbg and "dbg_y4" in dbg_out and l == 0:
                    sc.barrier()
                    sc.dma(dbg_out["dbg_y4"], yT4, reads=[b_fT])
                    sc.barrier()
                for i in range(NB):
                    fs = i % 2
                    for mm in range(2):
                        sc.op("act", lambda e, mm=mm, i=i: e.activation(out=sqB[:, mm, :], in_=yT4[:, mm, i * 512:(i + 1) * 512], func=AF.Square),
                              reads=[b_fT], writes=[b_sqB[mm]])
                    pt, bp = next_ps()

                    def f_st(e, pt=pt):
                        for mm in range(2):
                            ins = e.matmul(pt[:, 0:512], lhsT=ones[:, :], rhs=sqB[:, mm, :], start=(mm == 0), stop=(mm == 1))
                        return ins
                    sc.op("pe", f_st, reads=[b_ones] + b_sqB, writes=bp[:1])
                    sc.op("act", lambda e, pt=pt: e.activation(out=sdB[:, :], in_=pt[:, 0:512], func=AF.Ln, scale=1.0 / 256, bias=epsc[:, 0:1]),
                          reads=bp[:1] + [b_eps], writes=[b_sdB])
                    sc.op("act", lambda e: e.activation(out=rsB[:, :], in_=sdB[:, :], func=AF.Exp, scale=-0.5), reads=[b_sdB], writes=[b_rsB])
                    for mm in range(2):
                        sc.op("dve",
                              lambda e, mm=mm, i=i, fs=fs: e.scalar_tensor_tensor(out=yfo[fs][:, mm, :], in0=yT4[:, mm, i * 512:(i + 1) * 512],
                                                                                 scalar=gng[:, l, 4 + mm:5 + mm], in1=rsB[:, :], op0=ALU.mult, op1=ALU.mult),
                              reads=[b_fT, b_gng, b_rsB], writes=[b_yfo[fs]])
                    sc.dma(ysc[i, :, 6:8, :], yfo[fs][:, :, :], reads=[b_yfo[fs]])
                sc.barrier()
            fstack.close()
            with ExitStack() as pc_:
                def sbc(name, shape, dt_):
                    return pc_.enter_context(nc.sbuf_tensor(f"c_{name}_{l}", shape, dt_))
                w_out_sb = sbc("w_out", [128, 8, D], BF16); b_wout = Buf(f"wout{l}")
                gC = sbc("gC", [128, 3, D], F32); b_gC = [Buf(f"gC{l}_{i}") for i in range(3)]
                for gi, gk in enumerate(("post_mix_gain", "pre_ffn_gain", "post_ffn_gain")):
                    sc.dma(gC[:, gi, :], dram[gk][l:l + 1, :].partition_broadcast(128), writes=[b_gC[gi]])
                w_dn_sb = sbc("w_dn", [128, NDC, D], BF16); b_wdn = Buf(f"wdn{l}")
                sc.dma(w_out_sb[:, :, :], wos[l].rearrange("k p n -> p k n"), writes=[b_wout])
                b_wdn4 = [Buf(f"wdn{l}_{i}") for i in range(2)]
                sc.dma(w_dn_sb[:, 0:NDC // 2, :], wds[l, 0:NDC // 2].rearrange("k p n -> p k n"), writes=[b_wdn4[0]])
                sc.dma(w_dn_sb[:, NDC // 2:NDC, :], wds[l, NDC // 2:NDC].rearrange("k p n -> p k n"), writes=[b_wdn4[1]])
                NW = 3
                wg_sb = [sbc(f"wg{i}", [128, 8, 128], BF16) for i in range(NW)]; b_wg = [Buf(f"wg{l}_{i}") for i in range(NW)]
                wu_sb = [sbc(f"wu{i}", [128, 8, 128], BF16) for i in range(NW)]; b_wu = [Buf(f"wu{l}_{i}") for i in range(NW)]
                xtc = [sbc(f"xtc{i}", [128, 4, D], F32) for i in range(2)]; b_xtc = [Buf(f"xtC{l}_{i}") for i in range(2)]
                yl = [sbc(f"yl{i}", [128, 8, 512], BF16) for i in range(2)]; b_yl = [Buf(f"yl{l}_{i}") for i in range(2)]
                junkf = sbc("junkf", [128, D], BF16); b_junkc = Buf("junkc")
                tmpf = [sbc(f"tmpf{i}", [128, D], F32) for i in range(2)]; b_tmpf = [Buf(f"tmpf{i}") for i in range(2)]
                st = sbc("st", [128, 9, 4], F32); b_st = [Buf(f"st{i}") for i in range(9)]
                h2 = sbc("h2", [128, 4, D], BF16); b_h2 = Buf("h2")
                h2Ts = [sbc(f"h2T{i}", [128, 8, 512], BF16) for i in range(2)]; b_h2Ts = [Buf(f"h2T{i}") for i in range(2)]
                actT = sbc("actT", [128, NDC, 512], BF16); b_act = Buf("actT")
                sg = [sbc(f"sg{i}", [128, 512], F32) for i in range(2)]; b_sg = [Buf(f"sg{i}") for i in range(2)]
                wctr = [0]

                def load_blk(i):
                    s_ = i % 2
                    sc.dma(xtc[s_][:, :, :], x_in[i * 512:(i + 1) * 512, :].rearrange("(c p) d -> p c d", p=128), writes=[b_xtc[s_]])
                    sc.dma(yl[s_][:, :, :], ysc[i], writes=[b_yl[s_]])

                def load_w(dc):
                    s_ = wctr[0] % NW
                    wctr[0] += 1
                    sc.dma(wg_sb[s_][:, :, :], wgs[l, dc].rearrange("p (k n) -> p k n", k=8), writes=[b_wg[s_]])
                    sc.dma(wu_sb[s_][:, :, :], wus[l, dc].rearrange("p (k n) -> p k n", k=8), writes=[b_wu[s_]])
                    return s_

                def resid_norm(i, c, pt, bp, gi, s0, final):
                    s_ = i % 2
                    xs = xtc[s_]
                    sc.op("act", lambda e: e.activation(out=junkf[:, :], in_=pt[:, :], func=AF.Square, accum_out=st[:, s0, c:c + 1]),
                          reads=bp, writes=[b_junkc, b_st[s0]])
                    rstd_small(st[:, s0 + 2, c:c + 1], st[:, s0, c:c + 1], D, b_st[s0], b_st[s0 + 2], b_st[s0 + 1], st[:, s0 + 1, c:c + 1], lnexp=False)
                    tf = tmpf[c % 2]
                    sc.op("dve", lambda e: e.scalar_tensor_tensor(out=tf[:, :], in0=pt[:, :], scalar=st[:, s0 + 2, c:c + 1], in1=gC[:, gi, :],
                                                                  op0=ALU.mult, op1=ALU.mult), reads=bp + [b_st[s0 + 2], b_gC[gi]], writes=[b_tmpf[c % 2]])
                    sc.op("pool" if final else "dve", lambda e: e.tensor_tensor(out=xs[:, c, :], in0=tf[:, :], in1=xs[:, c, :], op=ALU.add),
                          reads=[b_tmpf[c % 2]], writes=[b_xtc[s_]])

                def stageC1(i):
                    s_ = i % 2
                    h2T = h2Ts[i % 2]; b_h2T = b_h2Ts[i % 2]
                    xs = xtc[s_]
                    for c in range(4):
                        pt, bp = next_ps()

                        def f_o(e, pt=pt, c=c):
                            for hf in range(2):
                                for kc in range(8):
                                    wk = (0, 1, 2, 3, 6, 7, 4, 5)[kc]
                                    ins = e.matmul(pt[:, hf * 512:(hf + 1) * 512], lhsT=yl[s_][:, kc, c * 128:(c + 1) * 128],
                                                   rhs=w_out_sb[:, wk, hf * 512:(hf + 1) * 512], start=(kc == 0), stop=(kc == 7))
                            return ins
                        sc.op("pe", f_o, reads=[b_yl[s_], b_wout], writes=bp)
                        resid_norm(i, c, pt, bp, 0, 0, False)
                        sc.op("act", lambda e, c=c: e.activation(out=junkf[:, :], in_=xs[:, c, :], func=AF.Square, accum_out=st[:, 3, c:c + 1]),
                              reads=[b_xtc[s_]], writes=[b_junkc, b_st[3]])
                        rstd_small(st[:, 5, c:c + 1], st[:, 3, c:c + 1], D, b_st[3], b_st[5], b_st[4], st[:, 4, c:c + 1], lnexp=False)
                        sc.op("dve", lambda e, c=c: e.scalar_tensor_tensor(out=h2[:, c, :], in0=xs[:, c, :], scalar=st[:, 5, c:c + 1], in1=gC[:, 1, :],
                                                                          op0=ALU.mult, op1=ALU.mult), reads=[b_xtc[s_], b_st[5], b_gC[1]], writes=[b_h2])
                    for c in range(4):
                        pt, bp = next_ps()

                        def f_tr(e, pt=pt, c=c):
                            for k in range(8):
                                ins = e.matmul(pt[:, k * 128:(k + 1) * 128], lhsT=h2[:, c, k * 128:(k + 1) * 128], rhs=ident[:, :], start=True, stop=True)
                            return ins
                        sc.op("pe", f_tr, reads=[b_h2, b_ident], writes=bp)
                        if c % 2 == 0:
                            sc.op("act", lambda e, pt=pt, c=c: e.activation(out=h2T[:, :, c * 128:(c + 1) * 128], in_=pt[:, :].rearrange("p (k n) -> p k n", k=8), func=AF.Copy),
                                  reads=bp, writes=[b_h2T])
                        else:
                            sc.op("dve", lambda e, pt=pt, c=c: e.tensor_copy(out=h2T[:, :, c * 128:(c + 1) * 128], in_=pt[:, :].rearrange("p (k n) -> p k n", k=8)),
                                  reads=bp, writes=[b_h2T])

                def stageC2(i, slots):
                    h2T = h2Ts[i % 2]; b_h2T = b_h2Ts[i % 2]
                    for dc in range(NDC):
                        ws = slots.pop(0)
                        if dc + NW - 1 < NDC:
                            slots.append(load_w(dc + NW - 1))
                        pt, bp = next_ps()

                        def f_g(e, pt=pt, ws=ws):
                            for k in range(8):
                                e.matmul(pt[:, 0:512], lhsT=wg_sb[ws][:, k, :], rhs=h2T[:, k, :], start=(k == 0), stop=(k == 7))
                            for k in range(8):
                                ins = e.matmul(pt[:, 512:1024], lhsT=wu_sb[ws][:, k, :], rhs=h2T[:, k, :], start=(k == 0), stop=(k == 7))
                            return ins
                        sc.op("pe", f_g, reads=[b_wg[ws], b_wu[ws], b_h2T], writes=bp)
                        sc.op("act", lambda e, pt=pt, dc=dc: e.activation(out=sg[dc % 2][:, :], in_=pt[:, 0:512], func=AF.Silu),
                              reads=bp[:1], writes=[b_sg[dc % 2]])
                        sc.op("dve", lambda e, pt=pt, dc=dc: e.tensor_tensor(out=actT[:, dc, :], in0=pt[:, 512:1024], in1=sg[dc % 2][:, :], op=ALU.mult),
                              reads=[bp[1], b_sg[dc % 2]], writes=[b_act])

                def stageC3(i):
                    s_ = i % 2
                    for c in range(4):
                        pt, bp = next_ps()

                        def f_d(e, pt=pt, c=c):
                            for hf in range(2):
                                for dc in range(NDC):
                                    ins = e.matmul(pt[:, hf * 512:(hf + 1) * 512], lhsT=actT[:, dc, c * 128:(c + 1) * 128],
                                                   rhs=w_dn_sb[:, dc, hf * 512:(hf + 1) * 512], start=(dc == 0), stop=(dc == NDC - 1))
                            return ins
                        sc.op("pe", f_d, reads=[b_act] + b_wdn4, writes=bp)
                        resid_norm(i, c, pt, bp, 2, 6, True)
                    sc.dma(x_out[i * 512:(i + 1) * 512, :].rearrange("(c p) d -> p c d", p=128), xtc[s_][:, :, :], reads=[b_xtc[s_]])

                load_blk(0)
                if NB > 1:
                    load_blk(1)
                for i in range(NB):
                    slots = [load_w(dc) for dc in range(NW - 1)]
                    stageC1(i)
                    stageC2(i, slots)
                    stageC3(i)
                    if i + 2 < NB:
                        load_blk(i + 2)
                sc.barrier()
        for l_ in range(L):
            layer(l_)
        sc.emit()
    return nc


_CACHE = {}


def kernel(**inputs):
    S = inputs["x"].shape[1]
    L = inputs["w_in"].shape[0]
    B = inputs["x"].shape[0]
    key = (S, L)
    if key not in _CACHE:
        _CACHE[key] = (build(S, L), host_consts(S))
    nc, consts = _CACHE[key]
    shared = {k: np.ascontiguousarray(np.asarray(inputs[k], dtype=np.float32)) for k in PARAM_SPECS(L)}
    shared.update(consts)
    x = np.asarray(inputs["x"], dtype=np.float32)
    in_maps = []
    for b in range(B):
        m = dict(shared)
        m["x"] = np.ascontiguousarray(x[b])
        in_maps.append(m)
    res = run_bass_kernel_spmd(nc, in_maps, core_ids=list(range(B)))
    return np.stack([np.asarray(r["out"], dtype=np.float32) for r in res.results], axis=0)
```

```python
from contextlib import ExitStack
import numpy as np
import ml_dtypes
import concourse.bass as bass
import concourse.mybir as mybir
from concourse.bass_utils import run_bass_kernel_spmd

F32 = mybir.dt.float32
BF16 = mybir.dt.bfloat16
AF = mybir.ActivationFunctionType
ALU = mybir.AluOpType
AX = mybir.AxisListType

D = 1024
DFF = 2816
NDC = DFF // 128
PW = 1792
EPS = 1e-6
ENGS = ("pe", "act", "dve", "pool", "sp")


class Buf:
    __slots__ = ("name", "w", "r", "semkey", "n", "last")

    def __init__(self, name):
        self.name = name
        self.w = None
        self.r = []
        self.semkey = None
        self.n = 0


class _FakeIns:
    def then_inc(self, *a, **k):
        return self


class _FakeEng:
    def __init__(self):
        self.calls = []

    def __getattr__(self, name):
        def f(*a, **k):
            self.calls.append((name, a, k))
            return _FakeIns()
        return f


def _free_size(ap):
    n = 1
    for d in ap.shape[1:]:
        n *= d
    return n


def _est_cost(eng, fn):
    fe = _FakeEng()
    fn(fe)
    t = 0.0
    for name, a, k in fe.calls:
        out = k.get("out", a[0] if a else None)
        F = _free_size(out)
        if name == "matmul":
            t += max(F, 48) * 0.43 + 14.0
        elif eng == "act":
            t += F * 0.87 + 200.0
        elif eng == "dve":
            t += F * 1.15 + 120.0
        elif eng == "pool":
            t += F * 3.5 + 250.0
        else:
            t += 60.0
    return t


SYNC_LAT = 150.0
DMA_LAT = 2200.0
DMA_BW = 120.0


class _Op:
    __slots__ = ("id", "eng", "fn", "preds", "cost", "dma", "owner", "nbytes", "tok", "finish", "prio", "succ", "npend", "ready", "tag")

    def __init__(self, id_, eng, fn, preds, cost, dma=False, owner=None, nbytes=0):
        self.id = id_
        self.eng = eng
        self.fn = fn
        self.preds = preds
        self.cost = cost
        self.dma = dma
        self.owner = owner
        self.nbytes = nbytes
        self.tok = None
        self.finish = 0.0
        self.prio = 0.0
        self.succ = []
        self.npend = 0
        self.ready = 0.0


class Sched:
    def __init__(self, nc, stack, reorder=True):
        self.nc = nc
        self.stack = stack
        self.reorder = reorder
        self.sems = {}
        self.ops = []
        self.region = []
        self.regions = []
        self.dma_bufs = []
        for e in ENGS:
            self.sems[e] = stack.enter_context(nc.semaphore("c_" + e))

    def _preds(self, reads, writes):
        p = set()
        for b in reads:
            if b.w is not None:
                p.add(b.w)
        for b in writes:
            if b.w is not None:
                p.add(b.w)
            p.update(b.r)
        return p

    def _commit(self, op, reads, writes):
        self.ops.append(op)
        self.region.append(op.id)
        for b in reads:
            b.r.append(op.id)
        for b in writes:
            b.w = op.id
            b.r = []
        return op.id

    def op(self, eng, fn, reads=(), writes=()):
        o = _Op(len(self.ops), eng, fn, self._preds(reads, writes), _est_cost(eng, fn))
        import sys as _sys
        o.tag = _sys._getframe(1).f_lineno
        return self._commit(o, reads, writes)

    def dma(self, out, in_, reads=(), writes=(), q="sp", **kw):
        owner = writes[0] if writes else reads[0]
        if owner.semkey is None:
            owner.semkey = f"d{len(self.dma_bufs)}_" + owner.name
            self.sems[owner.semkey] = self.stack.enter_context(self.nc.semaphore(owner.semkey))
            self.dma_bufs.append(owner)
            owner.n = 0
            owner.last = None
        preds = self._preds(reads, writes)
        if owner.last is not None:
            preds.add(owner.last)
        nbytes = _free_size(out) * out.shape[0] * (2 if out.dtype == BF16 else 4)
        o = _Op(len(self.ops), q, (lambda e: e.dma_start(out=out, in_=in_, **kw)), preds,
                60.0 if q == "sp" else 1000.0, dma=True, owner=owner, nbytes=nbytes)
        owner.last = o.id
        return self._commit(o, reads, writes)

    def barrier(self):
        self.regions.append(self.region)
        self.region = []

    def _schedule(self, ids):
        ops = self.ops
        inreg = set(ids)
        if not self.reorder:
            return list(ids)
        for i in ids:
            o = ops[i]
            o.succ = []
            o.npend = 0
            o.ready = 0.0
        for i in ids:
            o = ops[i]
            for p in o.preds:
                if p in inreg:
                    ops[p].succ.append(i)
                    o.npend += 1
        for i in reversed(ids):
            o = ops[i]
            m = 0.0
            for sidx in o.succ:
                if ops[sidx].prio > m:
                    m = ops[sidx].prio
            o.prio = m + o.cost + (DMA_LAT if o.dma else 0.0)
        free = {e: 0.0 for e in ENGS}
        ready = {e: [] for e in ENGS}
        import os as _os
        self.trace = {} if _os.environ.get("KTRACE") else None
        self.last_on = {}
        for i in ids:
            if ops[i].npend == 0:
                ready[ops[i].eng].append(i)
        order = []
        n = len(ids)
        while len(order) < n:
            best = None
            bkey = None
            for e in ENGS:
                fe = free[e]
                for i in ready[e]:
                    o = ops[i]
                    st = o.ready if o.ready > fe else fe
                    key = (st, -o.prio, i)
                    if bkey is None or key < bkey:
                        bkey = key
                        best = i
            o = ops[best]
            ready[o.eng].remove(best)
            st = bkey[0]
            if self.trace is not None:
                self.trace[best] = (st, free[o.eng], o.ready, self.last_on.get(o.eng))
                self.last_on[o.eng] = best
            free[o.eng] = st + o.cost
            if o.dma:
                o.finish = st + o.cost + DMA_LAT + o.nbytes / DMA_BW
            else:
                o.finish = st + o.cost
            order.append(best)
            for sidx in o.succ:
                so = ops[sidx]
                t = o.finish + SYNC_LAT
                if t > so.ready:
                    so.ready = t
                so.npend -= 1
                if so.npend == 0:
                    ready[so.eng].append(sidx)
        self.makespan = max(free.values())
        if self.trace is not None and len(ids) > 1000 and not hasattr(self, "_traced"):
            self._traced = True
            cur = max(ids, key=lambda i: ops[i].finish)
            chain = []
            while cur is not None and len(chain) < 400:
                st, fe, rd, prev = self.trace[cur]
                o = ops[cur]
                if rd >= fe and o.preds:
                    cand = [p for p in o.preds if p in inreg]
                    if not cand:
                        break
                    pbest = max(cand, key=lambda p: ops[p].finish)
                    chain.append((cur, o.eng, round(st), round(o.cost), "dep", getattr(o, "tag", "")))
                    cur = pbest
                else:
                    chain.append((cur, o.eng, round(st), round(o.cost), "eng", getattr(o, "tag", "")))
                    cur = prev
            for c in chain[:int(_os.environ.get("KTRACE"))]:
                print("KCHAIN", c)
        return order

    def emit(self):
        if self.region:
            self.regions.append(self.region)
            self.region = []
        nc = self.nc
        sems = self.sems
        ops = self.ops
        cnt = {e: 0 for e in ENGS}
        seen = {e: {} for e in ENGS}
        prog = {e: [] for e in ENGS}
        dcount = {}
        self.makespans = []

        def waits_for(eng, toks):
            need = {}
            for k, v in toks:
                if k == eng and eng in ("pe", "sp"):
                    continue
                if seen[eng].get(k, 0) >= v:
                    continue
                if need.get(k, 0) < v:
                    need[k] = v
            for k, v in need.items():
                seen[eng][k] = v
            return list(need.items())

        for ids in self.regions:
            order = self._schedule(ids)
            self.makespans.append(getattr(self, "makespan", 0.0))
            if not hasattr(self, "busy"):
                self.busy = []
            bz = {e: 0.0 for e in ENGS}
            for i in ids:
                bz[ops[i].eng] += ops[i].cost
            self.busy.append({e: round(v / 1e3) for e, v in bz.items()})
            for i in order:
                o = ops[i]
                w = waits_for(o.eng, [ops[p].tok for p in o.preds])
                if o.dma:
                    k = o.owner.semkey
                    dcount[k] = dcount.get(k, 0) + 1
                    o.tok = (k, 16 * dcount[k])
                    prog[o.eng].append((w, o.fn, (k, 16)))
                else:
                    cnt[o.eng] += 1
                    o.tok = (o.eng, cnt[o.eng])
                    prog[o.eng].append((w, o.fn, (o.eng, 1)))
            toks = [(k, 16 * v) for k, v in dcount.items()]
            toks += [(e, cnt[e]) for e in ENGS if e != "sp" and cnt[e] > 0]
            w = waits_for("sp", toks)
            cnt["sp"] += 1
            tsp = ("sp", cnt["sp"])
            prog["sp"].append((w, (lambda e: e.nop()), ("sp", 1)))
            for e in ENGS:
                if e == "sp":
                    continue
                w = waits_for(e, [tsp] + toks)
                if w:
                    prog[e].append((w, None, None))

        import os as _os
        if _os.environ.get("KDEBUG"):
            print("KDEBUG counts", cnt, "max dma", max(dcount.values()) * 16, "nsems", len(sems), "makespans_us", [round(m / 1e3) for m in self.makespans])
            print("KDEBUG busy_us", self.busy)
            print("KDEBUG nwaits", {e: sum(len(w) for w, _, _ in prog[e]) for e in ENGS}, {e: len(prog[e]) for e in ENGS})

        def run(e, lst):
            for waits, fn, inc in lst:
                for k, v in waits:
                    e.wait_ge(sems[k], v)
                if fn is not None:
                    ins = fn(e)
                    if inc is not None:
                        ins.then_inc(sems[inc[0]], inc[1])

        with nc.Block() as block:
            @block.sync
            def _(e):
                run(e, prog["sp"])

            @block.tensor
            def _(e):
                run(e, prog["pe"])

            @block.scalar
            def _(e):
                run(e, prog["act"])

            @block.vector
            def _(e):
                run(e, prog["dve"])

            @block.gpsimd
            def _(e):
                run(e, prog["pool"])


def host_consts(S):
    J = S // 128
    bf = ml_dtypes.bfloat16
    c = {}
    c["ident"] = np.eye(128, dtype=np.float32).astype(bf)
    c["ones"] = np.ones((128, 128), dtype=np.float32).astype(bf)
    a = np.arange(64)
    ang = 2 * np.pi * np.outer(a, a) / 64.0
    C64 = np.cos(ang) / 8.0
    S64 = np.sin(ang) / 8.0
    z = np.zeros((64, 64))
    c["c2"] = np.block([[C64, z], [z, C64]]).astype(np.float32)
    c["s2n"] = np.block([[-S64, z], [z, -S64]]).astype(np.float32)
    q = np.arange(128)[:, None, None]
    j = np.arange(J)[None, :, None]
    k1 = np.arange(128)[None, None, :]
    m = (k1 * (J * q + j)) % S
    th = 2 * np.pi * m / S
    sc = 1.0 / np.sqrt(128.0)
    c["tc"] = (np.cos(th) * sc).reshape(128, J * 128).astype(np.float32).astype(bf)
    c["ts"] = (np.sin(th) * sc).reshape(128, J * 128).astype(np.float32).astype(bf)
    c["tsn"] = (-np.sin(th) * sc).reshape(128, J * 128).astype(np.float32).astype(bf)
    jj = np.arange(J)[:, None]
    k2 = np.arange(J)[None, :]
    th3 = 2 * np.pi * jj * k2 / J
    cs3 = np.zeros((J, 2, J))
    cs3[:, 0, :] = np.cos(th3) / np.sqrt(J)
    cs3[:, 1, :] = np.sin(th3) / np.sqrt(J)
    c["cs3"] = cs3.reshape(2 * J, J).astype(np.float32).astype(bf)
    ic = np.zeros((128, 2, 2, 8), dtype=np.float32)
    for mch in range(2):
        for half in range(2):
            w = (2, 4, 8, 16)[mch * 2 + half]
            for t in range(8):
                lo = max(t - w // 2, 0)
                hi = min(t + w // 2, S)
                ic[half * 64:(half + 1) * 64, mch, 0, t] = 1.0 / (hi - lo)
                tt = S - 8 + t
                lo = max(tt - w // 2, 0)
                hi = min(tt + w // 2, S)
                ic[half * 64:(half + 1) * 64, mch, 1, t] = 1.0 / (hi - lo)
    c["icnt"] = ic.reshape(128, 32)
    return c


CONST_SPECS = lambda J: {
    "ident": ([128, 128], BF16), "ones": ([128, 128], BF16),
    "c2": ([128, 128], F32), "s2n": ([128, 128], F32),
    "tc": ([128, J * 128], BF16), "ts": ([128, J * 128], BF16), "tsn": ([128, J * 128], BF16),
    "cs3": ([2 * J, J], BF16), "icnt": ([128, 32], F32),
}

PARAM_SPECS = lambda L: {
    "pre_mix_gain": [L, D], "post_mix_gain": [L, D], "pre_ffn_gain": [L, D], "post_ffn_gain": [L, D],
    "w_in": [L, D, PW], "conv_w": [L, 3, 256], "pool_w": [L, 4, 64, 64], "pool_scale": [L, 256],
    "fourier_w": [L, 4, 64, 64], "spatial_w": [L, 4, 128, 128], "spatial_b": [L, 4, 128],
    "group_norm_gain": [L, D], "w_out": [L, D, D], "w_gate": [L, D, DFF], "w_up": [L, D, DFF],
    "w_down": [L, DFF, D],
}


CFG = dict(xt=2, hb=1, cv=1, pl=1, yfm=1, gn=1, gm=1, pools=1)


def build(S=8192, L=2, dbg=None):
    J = S // 128
    NB = S // 512
    J2 = 2 * J
    nc = bass.Bass("TRN2", target_bir_lowering=False)
    stack = ExitStack()
    with stack:
        dram = {}
        dram["x"] = nc.dram_tensor("x", [S, D], F32, kind="ExternalInput").ap()
        for k, shp in PARAM_SPECS(L).items():
            dram[k] = nc.dram_tensor(k, shp, F32, kind="ExternalInput").ap()
        for k, (shp, dt_) in CONST_SPECS(J).items():
            dram[k] = nc.dram_tensor(k, shp, dt_, kind="ExternalInput").ap()
        out_d = nc.dram_tensor("out", [S, D], F32, kind="ExternalOutput").ap()
        xmid = [nc.dram_tensor(f"xmid{l}", [S, D], F32, kind="Internal").ap() for l in range(max(L - 1, 1))]
        ysc = nc.dram_tensor("ysc", [NB, 128, 8, 512], BF16, kind="Internal").ap()
        wgs = nc.dram_tensor("wgs", [L, NDC, 128, 1024], BF16, kind="Internal").ap()
        wus = nc.dram_tensor("wus", [L, NDC, 128, 1024], BF16, kind="Internal").ap()
        fsc = nc.dram_tensor("fsc", [128, 2, S], BF16, kind="Internal").ap()
        wds = nc.dram_tensor("wds", [L, NDC, 128, 1024], BF16, kind="Internal").ap()
        wos = nc.dram_tensor("wos", [L, 8, 128, 1024], BF16, kind="Internal").ap()
        dbg_out = {}
        if dbg:
            for k, shp, dt_ in dbg:
                dbg_out[k] = nc.dram_tensor(k, shp, dt_, kind="ExternalOutput").ap()

        sc = Sched(nc, stack)

        def sb(name, shape, dt_):
            return stack.enter_context(nc.sbuf_tensor("s_" + name, shape, dt_))

        ident = sb("ident", [128, 128], BF16); b_ident = Buf("ident")
        ones = sb("ones", [128, 128], BF16); b_ones = Buf("ones")
        epsc = sb("epsc", [128, 1], F32); b_eps = Buf("eps")
        cw = sb("cw", [128, L, 3, 2], F32); b_cw = Buf("cw")
        psc = sb("psc", [128, L, 2], F32); b_psc = Buf("psc")
        gng = sb("gng", [128, L, 8], F32); b_gng = Buf("gng")
        bsp = sb("bsp", [128, L, 4], F32); b_bsp = Buf("bsp")
        icnt = sb("icnt", [128, 2, 2, 8], F32); b_icnt = Buf("icnt")
        ggm = sb("ggm", [128, 256], F32); b_ggm = Buf("ggm")

        psum = [stack.enter_context(nc.psum_tensor(f"ps{i}", [128, 1024], F32)) for i in range(4)]
        b_ps = [[Buf(f"ps{i}a"), Buf(f"ps{i}b")] for i in range(4)]
        ps_rr = [0]

        ps_rr2 = {0: 0, 1: 0, "s0": 0, "s1": 0}

        class _Half:
            def __init__(self, t, off):
                self.t = t
                self.off = off

            def __getitem__(self, key):
                p, c = key
                assert c.start is not None and c.stop is not None and c.stop <= 512
                return self.t[p, c.start + self.off:c.stop + self.off]

        def next_ps(pool=0):
            if not CFG["pools"]:
                i = ps_rr[0] % 4
                ps_rr[0] += 1
                return psum[i], b_ps[i]
            i = pool * 2 + ps_rr2[pool] % 2
            ps_rr2[pool] += 1
            return psum[i], b_ps[i]

        def next_ps1(pool=0):
            if not CFG["pools"]:
                return next_ps()
            k = ps_rr2["s%d" % pool] % 4
            ps_rr2["s%d" % pool] += 1
            pi = pool * 2 + k // 2
            return _Half(psum[pi], (k % 2) * 512), [b_ps[pi][k % 2]]

        sc.dma(ident[:, :], dram["ident"], writes=[b_ident])
        sc.dma(ones[:, :], dram["ones"], writes=[b_ones])
        sc.dma(icnt[:, :, :, :].rearrange("p a b c -> p (a b c)"), dram["icnt"], writes=[b_icnt])
        sc.op("pool", lambda e: e.memset(epsc[:, :], EPS), writes=[b_eps])
        sc.dma(cw[:, :, :, :], dram["conv_w"].rearrange("l t (m p) -> p l t m", p=128), writes=[b_cw],
               allow_slow_non_contiguous=True)
        sc.dma(psc[:, :, :], dram["pool_scale"].rearrange("l (m p) -> p l m", p=128), writes=[b_psc],
               allow_slow_non_contiguous=True)
        sc.dma(gng[:, :, :], dram["group_norm_gain"].rearrange("l (k p) -> p l k", p=128), writes=[b_gng],
               allow_slow_non_contiguous=True)
        sc.dma(bsp[:, :, :], dram["spatial_b"].rearrange("l h p -> p l h"), writes=[b_bsp],
               allow_slow_non_contiguous=True)

        def convert_gate_up(l, stg, b_stg):
            n = 0
            for src, dst in ((dram["w_gate"], wgs), (dram["w_up"], wus)):
                for c0 in range(NDC):
                    s_ = n % len(stg)
                    n += 1
                    sc.dma(stg[s_][:, :].rearrange("p (k n) -> p k n", k=8),
                           src[l][:, c0 * 128:(c0 + 1) * 128].rearrange("(k p) n -> p k n", p=128),
                           writes=[b_stg[s_]], q="pool")
                    sc.dma(dst[l, c0], stg[s_][:, :], reads=[b_stg[s_]], q="pool")

        def convert_down_out(l, stg, b_stg):
            n = 0
            for src, dst, nch in ((dram["w_out"], wos, 8), (dram["w_down"], wds, NDC)):
                for c0 in range(nch):
                    s_ = n % len(stg)
                    n += 1
                    sc.dma(stg[s_][:, :], src[l][c0 * 128:(c0 + 1) * 128, :], writes=[b_stg[s_]], q="pool")
                    sc.dma(dst[l, c0], stg[s_][:, :], reads=[b_stg[s_]], q="pool")

        def rstd_small(out_ap, in_ap, n, b_in, b_out, b_tmp, tmp_ap, lnexp=True):
            if lnexp:
                sc.op("act", lambda e: e.activation(out=tmp_ap, in_=in_ap, func=AF.Ln, scale=1.0 / n, bias=epsc[:, 0:1]),
                      reads=[b_in, b_eps], writes=[b_tmp])
                sc.op("act", lambda e: e.activation(out=out_ap, in_=tmp_ap, func=AF.Exp, scale=-0.5), reads=[b_tmp], writes=[b_out])
            else:
                sc.op("act", lambda e: e.activation(out=tmp_ap, in_=in_ap, func=AF.Sqrt, scale=1.0 / n, bias=epsc[:, 0:1]),
                      reads=[b_in, b_eps], writes=[b_tmp])
                sc.op("dve", lambda e: e.reciprocal(out=out_ap, in_=tmp_ap), reads=[b_tmp], writes=[b_out])

        def layer(l):
            x_in = dram["x"] if l == 0 else xmid[l - 1]
            x_out = out_d if l == L - 1 else xmid[l]
            sc.dma(ggm[:, :], dram["group_norm_gain"][l:l + 1, 768:1024].partition_broadcast(128), writes=[b_ggm])

            with ExitStack() as pa:
                def sa(name, shape, dt_):
                    return pa.enter_context(nc.sbuf_tensor(f"a_{name}_{l}", shape, dt_))
                w_in_sb = sa("w_in", [128, 8, PW], BF16); b_win = Buf(f"win{l}")
                gA = sa("gA", [128, D], F32); b_gA = Buf(f"gA{l}")
                sc.dma(gA[:, :], dram["pre_mix_gain"][l:l + 1, :].partition_broadcast(128), writes=[b_gA])
                sc.dma(w_in_sb[:, :, :], dram["w_in"][l].rearrange("(k p) n -> p k n", p=128), writes=[b_win], q="pool")
                pwbd = sa("pwbd", [128, 2, 128], BF16); b_pwbd = Buf(f"pwbd{l}")
                sc.op("pool", lambda e: e.memset(pwbd[:, :, :], 0.0), writes=[b_pwbd])
                for g in range(4):
                    h0 = (g % 2) * 64
                    sc.dma(pwbd[h0:h0 + 64, g // 2, h0:h0 + 64], dram["pool_w"][l, g], writes=[b_pwbd], q="pool")
                wsn = sa("wsn", [128, 4, 128], BF16); b_wsn = Buf(f"wsn{l}")
                sc.dma(wsn[:, :, :], dram["spatial_w"][l].rearrange("h p q -> p h q"), writes=[b_wsn], q="pool")
                wsT = sa("wsT", [128, 4, 128], BF16); b_wsT = Buf(f"wsT{l}")
                pt_, bp_ = next_ps()

                def f_wsT(e, pt_=pt_):
                    for h in range(4):
                        ins = e.matmul(pt_[:, h * 128:(h + 1) * 128], lhsT=wsn[:, h, :], rhs=ident[:, :], start=True, stop=True)
                    return ins
                sc.op("pe", f_wsT, reads=[b_wsn, b_ident], writes=bp_[:1])
                sc.op("dve", lambda e, pt_=pt_: e.tensor_copy(out=wsT[:, :, :].rearrange("p h q -> p (h q)"), in_=pt_[:, 0:512]),
                      reads=bp_[:1], writes=[b_wsT])

                stg = [sa(f"stg{i}", [128, 1024], BF16) for i in range(2)]
                b_stg = [Buf(f"stgA{l}_{i}") for i in range(2)]
                convert_gate_up(l, stg, b_stg)

                def ring(name, shape, dt_, n):
                    return ([sa(f"{name}{k}", shape, dt_) for k in range(n)], [Buf(f"{name}{l}_{k}") for k in range(n)])

                def pick(r, idx):
                    return r[0][idx % len(r[0])], r[1][idx % len(r[1])]
                NX = CFG["xt"]
                xt, b_xt = ring("xt", [128, 4, D], F32, NX)
                ssx_r = ring("ssx", [128, 4], F32, 2); ssx2_r = ring("ssx2", [128, 4], F32, 2); rsx_r = ring("rsx", [128, 4], F32, 2)
                hb_r = ring("hb", [128, 4, D], BF16, CFG["hb"])
                hTx, b_hTx = ring("hTx", [128, 8, 528], BF16, 3)
                z_r = ring("z", [128, 528], F32, CFG["cv"]); cz_r = ring("cz", [128, 528], F32, CFG["cv"])
                bsb_r = ring("bsb", [128, 512], F32, CFG["cv"])
                acc_r = ring("acc", [128, 512], F32, CFG["cv"]); acc2_r = ring("acc2", [128, 512], F32, CFG["cv"])
                yfm_r = [ring(f"yfm{k}", [128, 512], F32, CFG["yfm"]) for k in range(4)]
                p_r = ring("p", [128, 528], F32, CFG["pl"]); Ra_r = ring("Ra", [128, 528], F32, CFG["pl"])
                Rb_r = ring("Rb", [128, 528], F32, CFG["pl"]); Rc_r = ring("Rc", [128, 528], F32, CFG["pl"])
                etmp = sa("etmp", [128, 8], F32); b_etmp = Buf("etmp")
                dpl_r = [ring(f"dpl{k}", [128, 512], BF16, CFG["pl"]) for k in range(2)]
                sq_r = [ring(f"sq{k}", [128, 512], BF16, CFG["gn"]) for k in range(2)]
                sd_r = ring("sd", [128, 512], F32, CFG["gn"]); rs_r = ring("rs", [128, 512], F32, CFG["gn"])
                yst, b_yst = ring("yst", [128, 6, 512], BF16, 2)
                fst_r = ring("fst", [128, 2, 512], BF16, 2)
                uv_r = ring("uv", [128, 4, 512], F32, CFG["gm"]); sqv_r = ring("sqv", [128, 4, 256], F32, CFG["gm"])
                vst_r = [ring(f"vst{k}", [128, 16], F32, 2) for k in range(6)]
                vh_r = ring("vh", [128, 4, 256], BF16, CFG["gm"]); yg_r = ring("yg", [128, 4, 256], F32, CFG["gm"])
                ygn_r = ring("ygn", [128, 4, 256], BF16, CFG["gm"])
                gst_r = [ring(f"gst{k}", [128, 4], F32, 2) for k in range(3)]

                def load_x(i):
                    s_ = i % NX
                    sc.dma(xt[s_][:, :, :], x_in[i * 512:(i + 1) * 512, :].rearrange("(c p) d -> p c d", p=128),
                           writes=[b_xt[s_]])

                def stageN(i):
                    s_ = i % NX
                    xs = xt[s_]
                    hs = i % 3
                    ssx, b_ssx = pick(ssx_r, i); ssx2, b_ssx2 = pick(ssx2_r, i); rsx, b_rsx = pick(rsx_r, i)
                    hb, b_hb = pick(hb_r, i)
                    for c in range(4):
                        sc.op("act", lambda e, c=c: e.activation(out=hb[:, c, :], in_=xs[:, c, :], func=AF.Square,
                                                                 accum_out=ssx[:, c:c + 1]),
                              reads=[b_xt[s_]], writes=[b_hb, b_ssx])
                    rstd_small(rsx[:, :], ssx[:, :], D, b_ssx, b_rsx, b_ssx2, ssx2[:, :])
                    for c in range(4):
                        sc.op("dve", lambda e, c=c: e.scalar_tensor_tensor(out=hb[:, c, :], in0=xs[:, c, :], scalar=rsx[:, c:c + 1],
                                                                         in1=gA[:, :], op0=ALU.mult, op1=ALU.mult),
                              reads=[b_xt[s_], b_rsx, b_gA], writes=[b_hb])
                    if i == 0:
                        sc.op("pool", lambda e: e.memset(hTx[hs][:, :, 0:8], 0.0), writes=[b_hTx[hs]])
                    if i == NB - 1:
                        sc.op("pool", lambda e: e.memset(hTx[hs][:, :, 520:528], 0.0), writes=[b_hTx[hs]])
                    for c in range(4):
                        pt, bp = next_ps()

                        def f_tr(e, c=c, pt=pt):
                            for k in range(8):
                                ins = e.matmul(pt[:, k * 128:(k + 1) * 128], lhsT=hb[:, c, k * 128:(k + 1) * 128], rhs=ident[:, :],
                                               start=True, stop=True)
                            return ins
                        sc.op("pe", f_tr, reads=[b_hb, b_ident], writes=bp)
                        sc.op("act", lambda e, c=c, pt=pt: e.activation(out=hTx[hs][:, :, 8 + c * 128:8 + (c + 1) * 128],
                                                                      in_=pt[:, :].rearrange("p (k n) -> p k n", k=8), func=AF.Copy),
                              reads=bp, writes=[b_hTx[hs]])
                    if i >= 1:
                        hp = (i - 1) % 3
                        sc.op("pool", lambda e: e.tensor_copy(out=hTx[hp][:, :, 520:528], in_=hTx[hs][:, :, 8:16]),
                              reads=[b_hTx[hs]], writes=[b_hTx[hp]])
                    if i + 1 < NB:
                        hn = (i + 1) % 3
                        sc.op("pool", lambda e: e.tensor_copy(out=hTx[hn][:, :, 0:8], in_=hTx[hs][:, :, 512:520]),
                              reads=[b_hTx[hs]], writes=[b_hTx[hn]])

                def proj_fm(i, m, halo):
                    hs = i % 3
                    pt, bp = next_ps() if halo else next_ps1()
                    if halo:
                        def f(e, pt=pt):
                            for k in range(8):
                                e.matmul(pt[:, 0:512], lhsT=w_in_sb[:, k, m * 128:(m + 1) * 128], rhs=hTx[hs][:, k, 0:512],
                                         start=(k == 0), stop=(k == 7))
                            for k in range(8):
                                ins = e.matmul(pt[:, 512:528], lhsT=w_in_sb[:, k, m * 128:(m + 1) * 128], rhs=hTx[hs][:, k, 512:528],
                                               start=(k == 0), stop=(k == 7))
                            return ins
                        sc.op("pe", f, reads=[b_win, b_hTx[hs]], writes=bp)
                    else:
                        def f(e, pt=pt):
                            for k in range(8):
                                ins = e.matmul(pt[:, 0:512], lhsT=w_in_sb[:, k, m * 128:(m + 1) * 128], rhs=hTx[hs][:, k, 8:520],
                                               start=(k == 0), stop=(k == 7))
                            return ins
                        sc.op("pe", f, reads=[b_win, b_hTx[hs]], writes=bp[:1])
                    return pt, bp

                def group_norm_fm(i, gidx, ys):
                    ri = 2 * i + gidx
                    sqs = [pick(sq_r[mm], ri) for mm in range(2)]
                    sd, b_sd = pick(sd_r, ri); rs, b_rs = pick(rs_r, ri)
                    yf = [pick(yfm_r[2 * gidx + mm], i) for mm in range(2)]
                    for mm in range(2):
                        sc.op("act", lambda e, mm=mm: e.activation(out=sqs[mm][0][:, :], in_=yf[mm][0][:, :], func=AF.Square),
                              reads=[yf[mm][1]], writes=[sqs[mm][1]])
                    pt, bp = next_ps1(1)

                    def f(e, pt=pt):
                        for mm in range(2):
                            ins = e.matmul(pt[:, 0:512], lhsT=ones[:, :], rhs=sqs[mm][0][:, :], start=(mm == 0), stop=(mm == 1))
                        return ins
                    sc.op("pe", f, reads=[b_ones, sqs[0][1], sqs[1][1]], writes=bp[:1])
                    sc.op("act", lambda e, pt=pt: e.activation(out=sd[:, :], in_=pt[:, 0:512], func=AF.Ln, scale=1.0 / 256,
                                                             bias=epsc[:, 0:1]), reads=bp[:1] + [b_eps], writes=[b_sd])
                    sc.op("act", lambda e: e.activation(out=rs[:, :], in_=sd[:, :], func=AF.Exp, scale=-0.5), reads=[b_sd], writes=[b_rs])
                    for mm in range(2):
                        kc = 2 * gidx + mm
                        sc.op("dve", lambda e, mm=mm, kc=kc: e.scalar_tensor_tensor(out=yst[ys][:, kc, :], in0=yf[mm][0][:, :],
                                                                                 scalar=gng[:, l, kc:kc + 1], in1=rs[:, :],
                                                                                 op0=ALU.mult, op1=ALU.mult),
                              reads=[yf[mm][1], b_gng, b_rs], writes=[b_yst[ys]])

                def conv_chunk(i, mm):
                    ri = 2 * i + mm
                    z_sb, b_z = pick(z_r, ri); cz, b_cz = pick(cz_r, ri); acc, b_acc = pick(acc_r, ri); acc2, b_acc2 = pick(acc2_r, ri)
                    yf, b_yf = pick(yfm_r[mm], i)
                    pz, bz = proj_fm(i, 4 + mm, True)
                    sc.op("act", lambda e: e.activation(out=z_sb[:, :], in_=pz[:, 0:528], func=AF.Copy), reads=bz, writes=[b_z])
                    pc, bc = proj_fm(i, 2 + mm, True)
                    sc.op("dve", lambda e: e.tensor_tensor(out=cz[:, :], in0=pc[:, 0:528], in1=z_sb[:, :], op=ALU.mult),
                          reads=bc + [b_z], writes=[b_cz])
                    bsb, b_bsb = pick(bsb_r, ri)
                    pb, bb = proj_fm(i, mm, False)
                    sc.op("act", lambda e: e.activation(out=bsb[:, :], in_=pb[:, 0:512], func=AF.Copy), reads=bb[:1], writes=[b_bsb])
                    sc.op("act", lambda e: e.activation(out=acc[:, :], in_=cz[:, 8:520], func=AF.Copy, scale=cw[:, l, 1, mm:mm + 1]),
                          reads=[b_cz, b_cw], writes=[b_acc])
                    sc.op("dve", lambda e: e.scalar_tensor_tensor(out=acc2[:, :], in0=cz[:, 7:519], scalar=cw[:, l, 0, mm:mm + 1],
                                                                  in1=acc[:, :], op0=ALU.mult, op1=ALU.add),
                          reads=[b_cz, b_cw, b_acc], writes=[b_acc2])
                    sc.op("dve", lambda e: e.scalar_tensor_tensor(out=acc[:, :], in0=cz[:, 9:521], scalar=cw[:, l, 2, mm:mm + 1],
                                                                  in1=acc2[:, :], op0=ALU.mult, op1=ALU.add),
                          reads=[b_cz, b_cw, b_acc2], writes=[b_acc])
                    sc.op("dve", lambda e: e.tensor_tensor(out=yf[:, :], in0=bsb[:, :], in1=acc[:, :], op=ALU.mult),
                          reads=[b_bsb, b_acc], writes=[b_yf])

                def pool_chunk(i, mm):
                    ri = 2 * i + mm
                    p_sb, b_p = pick(p_r, ri); Ra, b_Ra = pick(Ra_r, ri); Rb, b_Rb = pick(Rb_r, ri); Rc, b_Rc = pick(Rc_r, ri)
                    Rd, b_Rd = Ra, b_Ra
                    dpl, b_dpl = pick(dpl_r[mm], i)
                    pp, bpp = proj_fm(i, 6 + mm, True)
                    sc.op("act", lambda e: e.activation(out=p_sb[:, :], in_=pp[:, 0:528], func=AF.Copy), reads=bpp, writes=[b_p])
                    e0 = "pool" if mm == 0 else "dve"
                    sc.op(e0, lambda e: e.tensor_tensor(out=Ra[:, 0:527], in0=p_sb[:, 0:527], in1=p_sb[:, 1:528], op=ALU.add),
                          reads=[b_p], writes=[b_Ra])
                    sc.op(e0, lambda e: e.tensor_tensor(out=Rb[:, 0:525], in0=Ra[:, 0:525], in1=Ra[:, 2:527], op=ALU.add),
                          reads=[b_Ra], writes=[b_Rb])
                    if mm == 0:
                        wins = ((0, 64, Ra, b_Ra, 2), (64, 128, Rb, b_Rb, 4))
                    else:
                        sc.op("dve", lambda e: e.tensor_tensor(out=Rc[:, 0:521], in0=Rb[:, 0:521], in1=Rb[:, 4:525], op=ALU.add),
                              reads=[b_Rb], writes=[b_Rc])
                        sc.op("dve", lambda e: e.tensor_tensor(out=Rd[:, 0:513], in0=Rc[:, 0:513], in1=Rc[:, 8:521], op=ALU.add),
                              reads=[b_Rc], writes=[b_Rd])
                        wins = ((0, 64, Rc, b_Rc, 8), (64, 128, Rd, b_Rd, 16))
                    for (p0, p1, R, bR, w) in wins:
                        o = 8 - w // 2
                        sc.op("dve", lambda e, p0=p0, p1=p1, R=R, w=w, o=o: e.scalar_tensor_tensor(
                            out=dpl[p0:p1, :], in0=R[p0:p1, o:o + 512], scalar=1.0 / w, in1=p_sb[p0:p1, 8:520],
                            op0=ALU.mult, op1=ALU.subtract), reads=[bR, b_p], writes=[b_dpl])
                        for side, blk in ((0, 0), (1, NB - 1)):
                            if i != blk:
                                continue
                            c0 = 0 if side == 0 else 504
                            sc.op("dve", lambda e, p0=p0, p1=p1, R=R, o=o, c0=c0, side=side: e.tensor_tensor(
                                out=etmp[p0:p1, :], in0=R[p0:p1, o + c0:o + c0 + 8], in1=icnt[p0:p1, mm, side, :], op=ALU.mult),
                                reads=[bR, b_icnt], writes=[b_etmp])
                            sc.op("dve", lambda e, p0=p0, p1=p1, c0=c0: e.tensor_tensor(
                                out=dpl[p0:p1, c0:c0 + 8], in0=etmp[p0:p1, :], in1=p_sb[p0:p1, 8 + c0:16 + c0], op=ALU.subtract),
                                reads=[b_etmp, b_p], writes=[b_dpl])

                def stageP(i):
                    hs = i % 3
                    ys = i % 2
                    conv_chunk(i, 0)
                    conv_chunk(i, 1)
                    pool_chunk(i, 0)
                    pool_chunk(i, 1)
                    dp = [pick(dpl_r[mm], i) for mm in range(2)]
                    yfp = [pick(yfm_r[2 + mm], i) for mm in range(2)]
                    pt, bp = next_ps(1)

                    def f_pw(e, pt=pt):
                        for mm in range(2):
                            ins = e.matmul(pt[:, mm * 512:(mm + 1) * 512], lhsT=pwbd[:, mm, :], rhs=dp[mm][0][:, :], start=True, stop=True)
                        return ins
                    sc.op("pe", f_pw, reads=[b_pwbd, dp[0][1], dp[1][1]], writes=bp)
                    for mm in range(2):
                        sc.op("act", lambda e, mm=mm, pt=pt: e.activation(out=yfp[mm][0][:, :], in_=pt[:, mm * 512:(mm + 1) * 512],
                                                                        func=AF.Copy, scale=psc[:, l, mm:mm + 1]),
                              reads=[bp[mm], b_psc], writes=[yfp[mm][1]])
                    fst, b_fst = pick(fst_r, i)
                    for mm in range(2):
                        pf, bf_ = proj_fm(i, 8 + mm, False)
                        sc.op("act", lambda e, mm=mm, pf=pf: e.activation(out=fst[:, mm, :], in_=pf[:, 0:512], func=AF.Copy),
                              reads=bf_[:1], writes=[b_fst])
                    sc.dma(fsc[:, :, i * 512:(i + 1) * 512], fst[:, :, :], reads=[b_fst])
                    uv, b_uv = pick(uv_r, i); sqv, b_sqv = pick(sqv_r, i); vh, b_vh = pick(vh_r, i)
                    yg, b_yg = pick(yg_r, i); ygn, b_ygn = pick(ygn_r, i)
                    vstp = [pick(vst_r[k], i) for k in range(6)]
                    vst = [t[0] for t in vstp]; b_vst = [t[1] for t in vstp]
                    gstp = [pick(gst_r[k], i) for k in range(3)]
                    gst = [t[0] for t in gstp]; b_gst = [t[1] for t in gstp]
                    for c in range(4):
                        pt, bp = next_ps1()

                        def f_uv(e, c=c, pt=pt):
                            for k in range(8):
                                ins = e.matmul(pt[:, 0:512], lhsT=hTx[hs][:, k, 8 + c * 128:8 + (c + 1) * 128], rhs=w_in_sb[:, k, 1280:1792],
                                               start=(k == 0), stop=(k == 7))
                            return ins
                        sc.op("pe", f_uv, reads=[b_win, b_hTx[hs]], writes=bp[:1])
                        sc.op("act", lambda e, c=c, pt=pt: e.activation(out=uv[:, c, :], in_=pt[:, 0:512], func=AF.Copy),
                              reads=bp[:1], writes=[b_uv])
                    group_norm_fm(i, 0, ys)
                    group_norm_fm(i, 1, ys)
                    v4 = uv[:, :, 256:512].rearrange("p n (h c) -> p n h c", h=4)
                    nh = lambda t: t[:, :].rearrange("p (n h) -> p n h", n=4)
                    sc.op("dve", lambda e: e.tensor_reduce(out=nh(vst[0]), in_=v4, axis=AX.X, op=ALU.add),
                          reads=[b_uv], writes=[b_vst[0]])
                    sc.op("act", lambda e: e.activation(out=sqv[:, :, :], in_=uv[:, :, 256:512], func=AF.Square), reads=[b_uv], writes=[b_sqv])
                    sc.op("dve", lambda e: e.tensor_reduce(out=nh(vst[1]), in_=sqv[:, :, :].rearrange("p n (h c) -> p n h c", h=4), axis=AX.X, op=ALU.add),
                          reads=[b_sqv], writes=[b_vst[1]])
                    sc.op("pool", lambda e: e.tensor_scalar(out=vst[2][:, :], in0=vst[0][:, :], scalar1=1.0 / 64, scalar2=None, op0=ALU.mult),
                          reads=[b_vst[0]], writes=[b_vst[2]])
                    sc.op("pool", lambda e: e.tensor_tensor(out=vst[3][:, :], in0=vst[2][:, :], in1=vst[2][:, :], op=ALU.mult),
                          reads=[b_vst[2]], writes=[b_vst[3]])
                    sc.op("dve", lambda e: e.scalar_tensor_tensor(out=vst[4][:, :], in0=vst[1][:, :], scalar=1.0 / 64, in1=vst[3][:, :],
                                                                  op0=ALU.mult, op1=ALU.subtract), reads=[b_vst[1], b_vst[3]], writes=[b_vst[4]])
                    sc.op("act", lambda e: e.activation(out=vst[3][:, :], in_=vst[4][:, :], func=AF.Ln, scale=1.0, bias=epsc[:, 0:1]),
                          reads=[b_vst[4], b_eps], writes=[b_vst[3]])
                    sc.op("act", lambda e: e.activation(out=vst[5][:, :], in_=vst[3][:, :], func=AF.Exp, scale=-0.5), reads=[b_vst[3]], writes=[b_vst[5]])
                    mean_b = nh(vst[2]).unsqueeze(3).to_broadcast([128, 4, 4, 64])
                    rstd_b = nh(vst[5]).unsqueeze(3).to_broadcast([128, 4, 4, 64])
                    sqv4 = sqv[:, :, :].rearrange("p n (h c) -> p n h c", h=4)
                    sc.op("dve", lambda e: e.tensor_tensor(out=sqv4, in0=v4, in1=mean_b, op=ALU.subtract),
                          reads=[b_uv, b_vst[2]], writes=[b_sqv])
                    sc.op("dve", lambda e: e.tensor_tensor(out=vh[:, :, :].rearrange("p n (h c) -> p n h c", h=4), in0=sqv4, in1=rstd_b, op=ALU.mult),
                          reads=[b_sqv, b_vst[5]], writes=[b_vh])
                    pt, bp = next_ps(1)

                    def f_sp(e, pt=pt):
                        for h in range(4):
                            ins = e.matmul(pt[:, h * 256:(h + 1) * 256], lhsT=wsT[:, h, :], rhs=vh[:, :, h * 64:(h + 1) * 64], start=True, stop=True)
                        return ins
                    sc.op("pe", f_sp, reads=[b_wsT, b_vh], writes=bp)
                    for h in range(4):
                        sc.op("dve", lambda e, h=h, pt=pt: e.scalar_tensor_tensor(
                            out=yg[:, :, h * 64:(h + 1) * 64], in0=pt[:, h * 256:(h + 1) * 256].rearrange("p (n c) -> p n c", n=4),
                            scalar=bsp[:, l, h:h + 1], in1=uv[:, :, h * 64:(h + 1) * 64], op0=ALU.add, op1=ALU.mult),
                            reads=[bp[h // 2], b_bsp, b_uv], writes=[b_yg])
                    sc.op("act", lambda e: e.activation(out=sqv[:, :, :], in_=yg[:, :, :], func=AF.Square), reads=[b_yg], writes=[b_sqv])
                    sc.op("dve", lambda e: e.tensor_reduce(out=gst[0][:, :], in_=sqv[:, :, :], axis=AX.X, op=ALU.add), reads=[b_sqv], writes=[b_gst[0]])
                    rstd_small(gst[2][:, :], gst[0][:, :], 256, b_gst[0], b_gst[2], b_gst[1], gst[1][:, :])
                    for n in range(4):
                        sc.op("dve", lambda e, n=n: e.scalar_tensor_tensor(out=ygn[:, n, :], in0=yg[:, n, :], scalar=gst[2][:, n:n + 1], in1=ggm[:, :],
                                                                          op0=ALU.mult, op1=ALU.mult), reads=[b_yg, b_gst[2], b_ggm], writes=[b_ygn])
                    pt, bp = next_ps(1)

                    def f_gt(e, pt=pt):
                        for mm in range(2):
                            for n in range(4):
                                ins = e.matmul(pt[:, mm * 512 + n * 128: mm * 512 + (n + 1) * 128], lhsT=ygn[:, n, mm * 128:(mm + 1) * 128],
                                               rhs=ident[:, :], start=True, stop=True)
                        return ins
                    sc.op("pe", f_gt, reads=[b_ygn, b_ident], writes=bp)
                    sc.op("act", lambda e, pt=pt: e.activation(out=yst[ys][:, 4:6, :], in_=pt[:, :].rearrange("p (m t) -> p m t", m=2), func=AF.Copy),
                          reads=bp, writes=[b_yst[ys]])
                    sc.dma(ysc[i, :, 0:6, :], yst[ys][:, :, :], reads=[b_yst[ys]])

                for i0 in range(min(NX, NB)):
                    load_x(i0)
                for i in range(NB + 1):
                    if i < NB:
                        stageN(i)
                        if i + NX < NB:
                            load_x(i + NX)
                    if i >= 1:
                        stageP(i - 1)
                sc.barrier()
            with ExitStack() as pb_:
                def sbb(name, shape, dt_):
                    return pb_.enter_context(nc.sbuf_tensor(f"b_{name}_{l}", shape, dt_))
                tcs = sbb("tc", [128, J, 128], BF16); b_tc = Buf(f"tc{l}")
                tss = sbb("ts", [128, J, 128], BF16); b_ts = Buf(f"ts{l}")
                tsn = sbb("tsn", [128, J, 128], BF16); b_tsn = Buf(f"tsn{l}")
                cs3 = sbb("cs3", [J2, J], BF16); b_cs3 = Buf(f"cs3{l}")
                c2 = sbb("c2", [128, 128], F32); b_c2 = Buf(f"c2{l}")
                s2n = sbb("s2n", [128, 128], F32); b_s2n = Buf(f"s2n{l}")
                fw2 = sbb("fw2", [128, 2, 128], F32); b_fw2 = Buf(f"fw2{l}")
                Dh = sbb("Dh", [128, 2, 256], BF16); b_Dh = Buf(f"Dh{l}")
                G = sbb("G", [128, 2, J, 256], BF16); b_G = [Buf(f"G0{l}"), Buf(f"G1{l}")]
                A = sbb("A", [128, J, 2, 128], BF16); b_A = Buf(f"A{l}")
                Ap_t = [sbb(f"Ap{i}", [J2, 128, 128], BF16) for i in range(2)] if J * 256 < 128 * 128 else None
                sqB = sbb("sqB", [128, 2, 512], BF16); b_sqB = [Buf("sqB0"), Buf("sqB1")]
                sdB = sbb("sdB", [128, 512], F32); b_sdB = Buf("sdB")
                rsB = sbb("rsB", [128, 512], F32); b_rsB = Buf("rsB")
                yfo = [sbb(f"yfo{i}", [128, 2, 512], BF16) for i in range(2)]; b_yfo = [Buf(f"yfo{i}") for i in range(2)]
                fT = sbb("fT", [128, 2, S], BF16); b_fT = Buf(f"fT{l}")
                fTflat = fT
                sc.dma(fT[:, :, :], fsc, writes=[b_fT])
                stgB = [sbb(f"stgB{i}", [128, 1024], BF16) for i in range(3)]
                b_stgB = [Buf(f"stgB{l}_{i}") for i in range(3)]
                convert_down_out(l, stgB, b_stgB)
                sc.dma(tcs[:, :, :].rearrange("p j k -> p (j k)"), dram["tc"], writes=[b_tc])
                sc.dma(tss[:, :, :].rearrange("p j k -> p (j k)"), dram["ts"], writes=[b_ts])
                sc.dma(tsn[:, :, :].rearrange("p j k -> p (j k)"), dram["tsn"], writes=[b_tsn])
                sc.dma(cs3[:, :], dram["cs3"], writes=[b_cs3])
                sc.dma(c2[:, :], dram["c2"], writes=[b_c2])
                sc.dma(s2n[:, :], dram["s2n"], writes=[b_s2n])
                sc.op("pool", lambda e: e.memset(fw2[:, :, :], 0.0), writes=[b_fw2])
                for h in range(4):
                    h0 = (h % 2) * 64
                    sc.dma(fw2[h0:h0 + 64, h // 2, h0:h0 + 64], dram["fourier_w"][l, h], writes=[b_fw2])
                for half in range(2):
                    pt, bp = next_ps1()

                    def f_D(e, pt=pt, half=half):
                        e.matmul(pt[:, 0:128], lhsT=c2[:, :], rhs=fw2[:, half, :], start=True, stop=True)
                        return e.matmul(pt[:, 128:256], lhsT=s2n[:, :], rhs=fw2[:, half, :], start=True, stop=True)
                    sc.op("pe", f_D, reads=[b_c2, b_s2n, b_fw2], writes=bp[:1])
                    sc.op("dve", lambda e, pt=pt, half=half: e.tensor_copy(out=Dh[:, half, :], in_=pt[:, 0:256]), reads=bp[:1], writes=[b_Dh])
                for j in range(J):
                    pt, bp = next_ps1()

                    def f_s1(e, pt=pt, j=j):
                        for half in range(2):
                            ins = e.matmul(pt[:, half * 256:(half + 1) * 256], lhsT=fT[:, half, :].rearrange("p (q j) -> p j q", j=J)[:, j, :], rhs=Dh[:, half, :], start=True, stop=True)
                        return ins
                    sc.op("pe", f_s1, reads=[b_fT, b_Dh], writes=bp[:1])
                    eng = "act" if j % 2 == 0 else "dve"
                    if eng == "act":
                        sc.op("act", lambda e, pt=pt, j=j: e.activation(out=G[:, :, j, :], in_=pt[:, 0:512].rearrange("p (h c) -> p h c", h=2), func=AF.Copy),
                              reads=bp[:1], writes=b_G)
                    else:
                        sc.op("dve", lambda e, pt=pt, j=j: e.tensor_copy(out=G[:, :, j, :], in_=pt[:, 0:512].rearrange("p (h c) -> p h c", h=2)),
                              reads=bp[:1], writes=b_G)
                yT4 = fTflat
                for half in range(2):
                    for j0 in range(0, J, 2):
                        pt, bp = next_ps1()

                        def f_s2(e, pt=pt, j0=j0, half=half):
                            for jj in range(2):
                                j = j0 + jj
                                o = jj * 256
                                e.matmul(pt[:, o:o + 128], lhsT=tcs[:, j, :], rhs=G[:, half, j, 0:128], start=True, stop=False)
                                e.matmul(pt[:, o:o + 128], lhsT=tss[:, j, :], rhs=G[:, half, j, 128:256], start=False, stop=True)
                                e.matmul(pt[:, o + 128:o + 256], lhsT=tcs[:, j, :], rhs=G[:, half, j, 128:256], start=True, stop=False)
                                ins = e.matmul(pt[:, o + 128:o + 256], lhsT=tsn[:, j, :], rhs=G[:, half, j, 0:128], start=False, stop=True)
                            return ins
                        sc.op("pe", f_s2, reads=[b_tc, b_ts, b_tsn, b_G[half]], writes=bp[:1])
                        eng = "act" if (j0 // 2) % 2 == 0 else "dve"
                        o_ap = A[:, j0:j0 + 2, :, :]
                        i_ap = lambda pt: pt[:, 0:512].rearrange("p (j r c) -> p j r c", j=2, r=2)
                        if eng == "act":
                            sc.op("act", lambda e, pt=pt, o_ap=o_ap: e.activation(out=o_ap, in_=i_ap(pt), func=AF.Copy), reads=bp[:1], writes=[b_A])
                        else:
                            sc.op("dve", lambda e, pt=pt, o_ap=o_ap: e.tensor_copy(out=o_ap, in_=i_ap(pt)), reads=bp[:1], writes=[b_A])
                    if J * 256 >= 128 * 128:
                        Ap = G[0:J2, half, :, :].rearrange("p j c -> p (j c)")[:, 0:128 * 128].rearrange("p (c k) -> p c k", c=128)
                    else:
                        Ap = Ap_t[half][:, :, :]
                    for c0 in range(0, 128, 8):
                        pt, bp = next_ps()

                        def f_tr(e, pt=pt, c0=c0):
                            for cc in range(8):
                                ins = e.matmul(pt[0:J2, cc * 128:(cc + 1) * 128], lhsT=A[:, :, :, c0 + cc].rearrange("p j r -> p (j r)"), rhs=ident[:, :], start=True, stop=True)
                            return ins
                        sc.op("pe", f_tr, reads=[b_A, b_ident], writes=bp)
                        eng = "act" if (c0 // 8) % 2 == 0 else "dve"
                        o_ap = Ap[:, c0:c0 + 8, :]
                        if eng == "act":
                            sc.op("act", lambda e, pt=pt, o_ap=o_ap: e.activation(out=o_ap, in_=pt[0:J2, :].rearrange("p (c k) -> p c k", c=8), func=AF.Copy),
                                  reads=bp, writes=[b_G[half]])
                        else:
                            sc.op("dve", lambda e, pt=pt, o_ap=o_ap: e.tensor_copy(out=o_ap, in_=pt[0:J2, :].rearrange("p (c k) -> p c k", c=8)),
                                  reads=bp, writes=[b_G[half]])
                    KB = min(128, 512 // J)
                    for k0 in range(0, 128, KB):
                        pt, bp = next_ps1()

                        def f_s3(e, pt=pt, k0=k0, Ap=Ap):
                            for kk in range(KB):
                                ins = e.matmul(pt[:, kk * J:(kk + 1) * J], lhsT=Ap[:, :, k0 + kk], rhs=cs3[:, :], start=True, stop=True)
                            return ins
                        sc.op("pe", f_s3, reads=[b_G[half], b_cs3], writes=bp[:1])
                        eng = "act" if (k0 // KB) % 2 == 0 else "dve"
                        o_ap = yT4[:, half, :].rearrange("p (k2 k1) -> p k2 k1", k1=128)[:, :, k0:k0 + KB]
                        if eng == "act":
                            sc.op("act", lambda e, pt=pt, o_ap=o_ap: e.activation(out=o_ap, in_=pt[:, 0:KB * J].rearrange("p (k a) -> p a k", k=KB), func=AF.Copy),
                                  reads=bp[:1], writes=[b_fT])
                        else:
                            sc.op("dve", lambda e, pt=pt, o_ap=o_ap: e.tensor_copy(out=o_ap, in_=pt[:, 0:KB * J].rearrange("p (k a) -> p a k", k=KB)),
                                  reads=bp[:1], writes=[b_fT])
                if dbg and "dbg_y4" in dbg_out and l == 0:
                    sc.barrier()
                    sc.dma(dbg_out["dbg_y4"], yT4, reads=[b_fT])
                    sc.barrier()
                for i in range(NB):
                    fs = i % 2
                    for mm in range(2):
                        sc.op("act", lambda e, mm=mm, i=i: e.activation(out=sqB[:, mm, :], in_=yT4[:, mm, i * 512:(i + 1) * 512], func=AF.Square),
                              reads=[b_fT], writes=[b_sqB[mm]])
                    pt, bp = next_ps1()

                    def f_st(e, pt=pt):
                        for mm in range(2):
                            ins = e.matmul(pt[:, 0:512], lhsT=ones[:, :], rhs=sqB[:, mm, :], start=(mm == 0), stop=(mm == 1))
                        return ins
                    sc.op("pe", f_st, reads=[b_ones] + b_sqB, writes=bp[:1])
                    sc.op("act", lambda e, pt=pt: e.activation(out=sdB[:, :], in_=pt[:, 0:512], func=AF.Ln, scale=1.0 / 256, bias=epsc[:, 0:1]),
                          reads=bp[:1] + [b_eps], writes=[b_sdB])
                    sc.op("act", lambda e: e.activation(out=rsB[:, :], in_=sdB[:, :], func=AF.Exp, scale=-0.5), reads=[b_sdB], writes=[b_rsB])
                    for mm in range(2):
                        sc.op("dve",
                              lambda e, mm=mm, i=i, fs=fs: e.scalar_tensor_tensor(out=yfo[fs][:, mm, :], in0=yT4[:, mm, i * 512:(i + 1) * 512],
                                                                                 scalar=gng[:, l, 4 + mm:5 + mm], in1=rsB[:, :], op0=ALU.mult, op1=ALU.mult),
                              reads=[b_fT, b_gng, b_rsB], writes=[b_yfo[fs]])
                    sc.dma(ysc[i, :, 6:8, :], yfo[fs][:, :, :], reads=[b_yfo[fs]])
                sc.barrier()
            with ExitStack() as pc_:
                def sbc(name, shape, dt_):
                    return pc_.enter_context(nc.sbuf_tensor(f"c_{name}_{l}", shape, dt_))
                w_out_sb = sbc("w_out", [128, 8, D], BF16); b_wout = Buf(f"wout{l}")
                gC = sbc("gC", [128, 3, D], F32); b_gC = [Buf(f"gC{l}_{i}") for i in range(3)]
                for gi, gk in enumerate(("post_mix_gain", "pre_ffn_gain", "post_ffn_gain")):
                    sc.dma(gC[:, gi, :], dram[gk][l:l + 1, :].partition_broadcast(128), writes=[b_gC[gi]])
                w_dn_sb = sbc("w_dn", [128, NDC, D], BF16); b_wdn = Buf(f"wdn{l}")
                sc.dma(w_out_sb[:, :, :], wos[l].rearrange("k p n -> p k n"), writes=[b_wout])
                b_wdn4 = [Buf(f"wdn{l}_{i}") for i in range(2)]
                sc.dma(w_dn_sb[:, 0:NDC // 2, :], wds[l, 0:NDC // 2].rearrange("k p n -> p k n"), writes=[b_wdn4[0]])
                sc.dma(w_dn_sb[:, NDC // 2:NDC, :], wds[l, NDC // 2:NDC].rearrange("k p n -> p k n"), writes=[b_wdn4[1]])
                NW = 3
                wg_sb = [sbc(f"wg{i}", [128, 8, 128], BF16) for i in range(NW)]; b_wg = [Buf(f"wg{l}_{i}") for i in range(NW)]
                wu_sb = [sbc(f"wu{i}", [128, 8, 128], BF16) for i in range(NW)]; b_wu = [Buf(f"wu{l}_{i}") for i in range(NW)]
                xtc = [sbc(f"xtc{i}", [128, 4, D], F32) for i in range(2)]; b_xtc = [Buf(f"xtC{l}_{i}") for i in range(2)]
                yl = [sbc(f"yl{i}", [128, 8, 512], BF16) for i in range(2)]; b_yl = [Buf(f"yl{l}_{i}") for i in range(2)]
                junkf = sbc("junkf", [128, D], BF16); b_junkc = Buf("junkc")
                tmpf = [sbc(f"tmpf{i}", [128, D], F32) for i in range(2)]; b_tmpf = [Buf(f"tmpf{i}") for i in range(2)]
                st = sbc("st", [128, 9, 4], F32); b_st = [Buf(f"st{i}") for i in range(9)]
                h2 = sbc("h2", [128, 4, D], BF16); b_h2 = Buf("h2")
                h2Ts = [sbc(f"h2T{i}", [128, 8, 512], BF16) for i in range(2)]; b_h2Ts = [Buf(f"h2T{i}") for i in range(2)]
                actT = sbc("actT", [128, NDC, 512], BF16); b_act = Buf("actT")
                sg = [sbc(f"sg{i}", [128, 512], F32) for i in range(2)]; b_sg = [Buf(f"sg{i}") for i in range(2)]
                wctr = [0]

                def load_blk(i):
                    s_ = i % 2
                    sc.dma(xtc[s_][:, :, :], x_in[i * 512:(i + 1) * 512, :].rearrange("(c p) d -> p c d", p=128), writes=[b_xtc[s_]])
                    sc.dma(yl[s_][:, :, :], ysc[i], writes=[b_yl[s_]])

                def load_w(dc):
                    s_ = wctr[0] % NW
                    wctr[0] += 1
                    sc.dma(wg_sb[s_][:, :, :], wgs[l, dc].rearrange("p (k n) -> p k n", k=8), writes=[b_wg[s_]])
                    sc.dma(wu_sb[s_][:, :, :], wus[l, dc].rearrange("p (k n) -> p k n", k=8), writes=[b_wu[s_]])
                    return s_

                def resid_norm(i, c, pt, bp, gi, s0, final):
                    s_ = i % 2
                    xs = xtc[s_]
                    sc.op("act", lambda e: e.activation(out=junkf[:, :], in_=pt[:, :], func=AF.Square, accum_out=st[:, s0, c:c + 1]),
                          reads=bp, writes=[b_junkc, b_st[s0]])
                    rstd_small(st[:, s0 + 2, c:c + 1], st[:, s0, c:c + 1], D, b_st[s0], b_st[s0 + 2], b_st[s0 + 1], st[:, s0 + 1, c:c + 1], lnexp=False)
                    tf = tmpf[c % 2]
                    sc.op("dve", lambda e: e.scalar_tensor_tensor(out=tf[:, :], in0=pt[:, :], scalar=st[:, s0 + 2, c:c + 1], in1=gC[:, gi, :],
                                                                  op0=ALU.mult, op1=ALU.mult), reads=bp + [b_st[s0 + 2], b_gC[gi]], writes=[b_tmpf[c % 2]])
                    sc.op("pool" if final else "dve", lambda e: e.tensor_tensor(out=xs[:, c, :], in0=tf[:, :], in1=xs[:, c, :], op=ALU.add),
                          reads=[b_tmpf[c % 2]], writes=[b_xtc[s_]])

                def stageC1(i):
                    s_ = i % 2
                    h2T = h2Ts[i % 2]; b_h2T = b_h2Ts[i % 2]
                    xs = xtc[s_]
                    for c in range(4):
                        pt, bp = next_ps(1)

                        def f_o(e, pt=pt, c=c):
                            for hf in range(2):
                                for kc in range(8):
                                    wk = (0, 1, 2, 3, 6, 7, 4, 5)[kc]
                                    ins = e.matmul(pt[:, hf * 512:(hf + 1) * 512], lhsT=yl[s_][:, kc, c * 128:(c + 1) * 128],
                                                   rhs=w_out_sb[:, wk, hf * 512:(hf + 1) * 512], start=(kc == 0), stop=(kc == 7))
                            return ins
                        sc.op("pe", f_o, reads=[b_yl[s_], b_wout], writes=bp)
                        resid_norm(i, c, pt, bp, 0, 0, False)
                        sc.op("act", lambda e, c=c: e.activation(out=junkf[:, :], in_=xs[:, c, :], func=AF.Square, accum_out=st[:, 3, c:c + 1]),
                              reads=[b_xtc[s_]], writes=[b_junkc, b_st[3]])
                        rstd_small(st[:, 5, c:c + 1], st[:, 3, c:c + 1], D, b_st[3], b_st[5], b_st[4], st[:, 4, c:c + 1], lnexp=False)
                        sc.op("dve", lambda e, c=c: e.scalar_tensor_tensor(out=h2[:, c, :], in0=xs[:, c, :], scalar=st[:, 5, c:c + 1], in1=gC[:, 1, :],
                                                                          op0=ALU.mult, op1=ALU.mult), reads=[b_xtc[s_], b_st[5], b_gC[1]], writes=[b_h2])
                    for c in range(4):
                        pt, bp = next_ps()

                        def f_tr(e, pt=pt, c=c):
                            for k in range(8):
                                ins = e.matmul(pt[:, k * 128:(k + 1) * 128], lhsT=h2[:, c, k * 128:(k + 1) * 128], rhs=ident[:, :], start=True, stop=True)
                            return ins
                        sc.op("pe", f_tr, reads=[b_h2, b_ident], writes=bp)
                        if c % 2 == 0:
                            sc.op("act", lambda e, pt=pt, c=c: e.activation(out=h2T[:, :, c * 128:(c + 1) * 128], in_=pt[:, :].rearrange("p (k n) -> p k n", k=8), func=AF.Copy),
                                  reads=bp, writes=[b_h2T])
                        else:
                            sc.op("dve", lambda e, pt=pt, c=c: e.tensor_copy(out=h2T[:, :, c * 128:(c + 1) * 128], in_=pt[:, :].rearrange("p (k n) -> p k n", k=8)),
                                  reads=bp, writes=[b_h2T])

                def stageC2(i, slots):
                    h2T = h2Ts[i % 2]; b_h2T = b_h2Ts[i % 2]
                    for dc in range(NDC):
                        ws = slots.pop(0)
                        if dc + NW - 1 < NDC:
                            slots.append(load_w(dc + NW - 1))
                        pt, bp = next_ps()

                        def f_g(e, pt=pt, ws=ws):
                            for k in range(8):
                                e.matmul(pt[:, 0:512], lhsT=wg_sb[ws][:, k, :], rhs=h2T[:, k, :], start=(k == 0), stop=(k == 7))
                            for k in range(8):
                                ins = e.matmul(pt[:, 512:1024], lhsT=wu_sb[ws][:, k, :], rhs=h2T[:, k, :], start=(k == 0), stop=(k == 7))
                            return ins
                        sc.op("pe", f_g, reads=[b_wg[ws], b_wu[ws], b_h2T], writes=bp)
                        sc.op("act", lambda e, pt=pt, dc=dc: e.activation(out=sg[dc % 2][:, :], in_=pt[:, 0:512], func=AF.Silu),
                              reads=bp[:1], writes=[b_sg[dc % 2]])
                        sc.op("dve", lambda e, pt=pt, dc=dc: e.tensor_tensor(out=actT[:, dc, :], in0=pt[:, 512:1024], in1=sg[dc % 2][:, :], op=ALU.mult),
                              reads=[bp[1], b_sg[dc % 2]], writes=[b_act])

                def stageC3(i):
                    s_ = i % 2
                    for c in range(4):
                        pt, bp = next_ps(1)

                        def f_d(e, pt=pt, c=c):
                            for hf in range(2):
                                for dc in range(NDC):
                                    ins = e.matmul(pt[:, hf * 512:(hf + 1) * 512], lhsT=actT[:, dc, c * 128:(c + 1) * 128],
                                                   rhs=w_dn_sb[:, dc, hf * 512:(hf + 1) * 512], start=(dc == 0), stop=(dc == NDC - 1))
                            return ins
                        sc.op("pe", f_d, reads=[b_act] + b_wdn4, writes=bp)
                        resid_norm(i, c, pt, bp, 2, 6, True)
                    sc.dma(x_out[i * 512:(i + 1) * 512, :].rearrange("(c p) d -> p c d", p=128), xtc[s_][:, :, :], reads=[b_xtc[s_]])

                load_blk(0)
                if NB > 1:
                    load_blk(1)
                for i in range(NB):
                    slots = [load_w(dc) for dc in range(NW - 1)]
                    stageC1(i)
                    stageC2(i, slots)
                    stageC3(i)
                    if i + 2 < NB:
                        load_blk(i + 2)
                sc.barrier()
        for l_ in range(L):
            layer(l_)
        sc.emit()
    return nc


_CACHE = {}


def kernel(**inputs):
    S = inputs["x"].shape[1]
    L = inputs["w_in"].shape[0]
    B = inputs["x"].shape[0]
    key = (S, L)
    if key not in _CACHE:
        _CACHE[key] = (build(S, L), host_consts(S))
    nc, consts = _CACHE[key]
    shared = {k: np.ascontiguousarray(np.asarray(inputs[k], dtype=np.float32)) for k in PARAM_SPECS(L)}
    shared.update(consts)
    x = np.asarray(inputs["x"], dtype=np.float32)
    in_maps = []
    for b in range(B):
        m = dict(shared)
        m["x"] = np.ascontiguousarray(x[b])
        in_maps.append(m)
    res = run_bass_kernel_spmd(nc, in_maps, core_ids=list(range(B)))
    return np.stack([np.asarray(r["out"], dtype=np.float32) for r in res.results], axis=0)
```

```python
from contextlib import ExitStack
import numpy as np
import ml_dtypes
import concourse.bass as bass
import concourse.mybir as mybir
from concourse.bass_utils import run_bass_kernel_spmd

F32 = mybir.dt.float32
BF16 = mybir.dt.bfloat16
AF = mybir.ActivationFunctionType
ALU = mybir.AluOpType
AX = mybir.AxisListType

D = 1024
DFF = 2816
NDC = DFF // 128
PW = 1792
EPS = 1e-6
ENGS = ("pe", "act", "dve", "pool", "sp")


class Buf:
    __slots__ = ("name", "w", "r", "semkey", "n", "last")

    def __init__(self, name):
        self.name = name
        self.w = None
        self.r = []
        self.semkey = None
        self.n = 0


class _FakeIns:
    def then_inc(self, *a, **k):
        return self


class _FakeEng:
    def __init__(self):
        self.calls = []

    def __getattr__(self, name):
        def f(*a, **k):
            self.calls.append((name, a, k))
            return _FakeIns()
        return f


def _free_size(ap):
    n = 1
    for d in ap.shape[1:]:
        n *= d
    return n


def _est_cost(eng, fn):
    fe = _FakeEng()
    fn(fe)
    t = 0.0
    for name, a, k in fe.calls:
        out = k.get("out", a[0] if a else None)
        F = _free_size(out)
        if name == "matmul":
            t += max(F, 48) * 0.43 + 14.0
        elif eng == "act":
            t += F * 0.87 + 200.0
        elif eng == "dve":
            t += F * 1.15 + 120.0
        elif eng == "pool":
            t += F * 3.5 + 250.0
        else:
            t += 60.0
    return t


SYNC_LAT = 150.0
DMA_LAT = 2200.0
DMA_BW = 120.0


class _Op:
    __slots__ = ("id", "eng", "fn", "preds", "cost", "dma", "owner", "nbytes", "tok", "finish", "prio", "succ", "npend", "ready", "tag")

    def __init__(self, id_, eng, fn, preds, cost, dma=False, owner=None, nbytes=0):
        self.id = id_
        self.eng = eng
        self.fn = fn
        self.preds = preds
        self.cost = cost
        self.dma = dma
        self.owner = owner
        self.nbytes = nbytes
        self.tok = None
        self.finish = 0.0
        self.prio = 0.0
        self.succ = []
        self.npend = 0
        self.ready = 0.0


class Sched:
    def __init__(self, nc, stack, reorder=True):
        self.nc = nc
        self.stack = stack
        self.reorder = reorder
        self.sems = {}
        self.ops = []
        self.region = []
        self.regions = []
        self.dma_bufs = []
        for e in ENGS:
            self.sems[e] = stack.enter_context(nc.semaphore("c_" + e))

    def _preds(self, reads, writes):
        p = set()
        for b in reads:
            if b.w is not None:
                p.add(b.w)
        for b in writes:
            if b.w is not None:
                p.add(b.w)
            p.update(b.r)
        return p

    def _commit(self, op, reads, writes):
        self.ops.append(op)
        self.region.append(op.id)
        for b in reads:
            b.r.append(op.id)
        for b in writes:
            b.w = op.id
            b.r = []
        return op.id

    def op(self, eng, fn, reads=(), writes=()):
        o = _Op(len(self.ops), eng, fn, self._preds(reads, writes), _est_cost(eng, fn))
        import sys as _sys
        o.tag = _sys._getframe(1).f_lineno
        return self._commit(o, reads, writes)

    def dma(self, out, in_, reads=(), writes=(), q="sp", **kw):
        owner = writes[0] if writes else reads[0]
        if owner.semkey is None:
            owner.semkey = f"d{len(self.dma_bufs)}_" + owner.name
            self.sems[owner.semkey] = self.stack.enter_context(self.nc.semaphore(owner.semkey))
            self.dma_bufs.append(owner)
            owner.n = 0
            owner.last = None
        preds = self._preds(reads, writes)
        if owner.last is not None:
            preds.add(owner.last)
        nbytes = _free_size(out) * out.shape[0] * (2 if out.dtype == BF16 else 4)
        o = _Op(len(self.ops), q, (lambda e: e.dma_start(out=out, in_=in_, **kw)), preds,
                60.0 if q == "sp" else 1000.0, dma=True, owner=owner, nbytes=nbytes)
        owner.last = o.id
        return self._commit(o, reads, writes)

    def barrier(self):
        self.regions.append(self.region)
        self.region = []

    def _schedule(self, ids):
        ops = self.ops
        inreg = set(ids)
        if not self.reorder:
            return list(ids)
        for i in ids:
            o = ops[i]
            o.succ = []
            o.npend = 0
            o.ready = 0.0
        for i in ids:
            o = ops[i]
            for p in o.preds:
                if p in inreg:
                    ops[p].succ.append(i)
                    o.npend += 1
        for i in reversed(ids):
            o = ops[i]
            m = 0.0
            for sidx in o.succ:
                if ops[sidx].prio > m:
                    m = ops[sidx].prio
            o.prio = m + o.cost + (DMA_LAT if o.dma else 0.0)
        free = {e: 0.0 for e in ENGS}
        ready = {e: [] for e in ENGS}
        import os as _os
        self.trace = {} if _os.environ.get("KTRACE") else None
        self.last_on = {}
        for i in ids:
            if ops[i].npend == 0:
                ready[ops[i].eng].append(i)
        order = []
        n = len(ids)
        while len(order) < n:
            best = None
            bkey = None
            for e in ENGS:
                fe = free[e]
                for i in ready[e]:
                    o = ops[i]
                    st = o.ready if o.ready > fe else fe
                    key = (st, -o.prio, i)
                    if bkey is None or key < bkey:
                        bkey = key
                        best = i
            o = ops[best]
            ready[o.eng].remove(best)
            st = bkey[0]
            if self.trace is not None:
                self.trace[best] = (st, free[o.eng], o.ready, self.last_on.get(o.eng))
                self.last_on[o.eng] = best
            free[o.eng] = st + o.cost
            if o.dma:
                o.finish = st + o.cost + DMA_LAT + o.nbytes / DMA_BW
            else:
                o.finish = st + o.cost
            order.append(best)
            for sidx in o.succ:
                so = ops[sidx]
                t = o.finish + SYNC_LAT
                if t > so.ready:
                    so.ready = t
                so.npend -= 1
                if so.npend == 0:
                    ready[so.eng].append(sidx)
        self.makespan = max(free.values())
        self._nreg = getattr(self, "_nreg", -1) + 1
        if self.trace is not None and self._nreg == int(_os.environ.get("KREGION", "0")):
            cur = max(ids, key=lambda i: ops[i].finish)
            chain = []
            while cur is not None and len(chain) < 400:
                st, fe, rd, prev = self.trace[cur]
                o = ops[cur]
                if rd >= fe and o.preds:
                    cand = [p for p in o.preds if p in inreg]
                    if not cand:
                        break
                    pbest = max(cand, key=lambda p: ops[p].finish)
                    chain.append((cur, o.eng, round(st), round(o.cost), "dep", getattr(o, "tag", "")))
                    cur = pbest
                else:
                    chain.append((cur, o.eng, round(st), round(o.cost), "eng", getattr(o, "tag", "")))
                    cur = prev
            for c in chain[:int(_os.environ.get("KTRACE"))]:
                print("KCHAIN", c)
        return order

    def emit(self):
        if self.region:
            self.regions.append(self.region)
            self.region = []
        nc = self.nc
        sems = self.sems
        ops = self.ops
        cnt = {e: 0 for e in ENGS}
        seen = {e: {} for e in ENGS}
        prog = {e: [] for e in ENGS}
        dcount = {}
        self.makespans = []

        def waits_for(eng, toks):
            need = {}
            for k, v in toks:
                if k == eng and eng in ("pe", "sp"):
                    continue
                if seen[eng].get(k, 0) >= v:
                    continue
                if need.get(k, 0) < v:
                    need[k] = v
            for k, v in need.items():
                seen[eng][k] = v
            return list(need.items())

        for ids in self.regions:
            order = self._schedule(ids)
            self.makespans.append(getattr(self, "makespan", 0.0))
            if not hasattr(self, "busy"):
                self.busy = []
            bz = {e: 0.0 for e in ENGS}
            for i in ids:
                bz[ops[i].eng] += ops[i].cost
            self.busy.append({e: round(v / 1e3) for e, v in bz.items()})
            for i in order:
                o = ops[i]
                w = waits_for(o.eng, [ops[p].tok for p in o.preds])
                if o.dma:
                    k = o.owner.semkey
                    dcount[k] = dcount.get(k, 0) + 1
                    o.tok = (k, 16 * dcount[k])
                    prog[o.eng].append((w, o.fn, (k, 16)))
                else:
                    cnt[o.eng] += 1
                    o.tok = (o.eng, cnt[o.eng])
                    prog[o.eng].append((w, o.fn, (o.eng, 1)))
            toks = [(k, 16 * v) for k, v in dcount.items()]
            toks += [(e, cnt[e]) for e in ENGS if e != "sp" and cnt[e] > 0]
            w = waits_for("sp", toks)
            cnt["sp"] += 1
            tsp = ("sp", cnt["sp"])
            prog["sp"].append((w, (lambda e: e.nop()), ("sp", 1)))
            for e in ENGS:
                if e == "sp":
                    continue
                w = waits_for(e, [tsp] + toks)
                if w:
                    prog[e].append((w, None, None))

        import os as _os
        if _os.environ.get("KDEBUG"):
            print("KDEBUG counts", cnt, "max dma", max(dcount.values()) * 16, "nsems", len(sems), "makespans_us", [round(m / 1e3) for m in self.makespans])
            print("KDEBUG busy_us", self.busy)
            print("KDEBUG nwaits", {e: sum(len(w) for w, _, _ in prog[e]) for e in ENGS}, {e: len(prog[e]) for e in ENGS})

        def run(e, lst):
            for waits, fn, inc in lst:
                for k, v in waits:
                    e.wait_ge(sems[k], v)
                if fn is not None:
                    ins = fn(e)
                    if inc is not None:
                        ins.then_inc(sems[inc[0]], inc[1])

        with nc.Block() as block:
            @block.sync
            def _(e):
                run(e, prog["sp"])

            @block.tensor
            def _(e):
                run(e, prog["pe"])

            @block.scalar
            def _(e):
                run(e, prog["act"])

            @block.vector
            def _(e):
                run(e, prog["dve"])

            @block.gpsimd
            def _(e):
                run(e, prog["pool"])


def host_consts(S):
    J = S // 128
    bf = ml_dtypes.bfloat16
    c = {}
    c["ident"] = np.eye(128, dtype=np.float32).astype(bf)
    c["ones"] = np.ones((128, 128), dtype=np.float32).astype(bf)
    a = np.arange(64)
    ang = 2 * np.pi * np.outer(a, a) / 64.0
    C64 = np.cos(ang) / 8.0
    S64 = np.sin(ang) / 8.0
    z = np.zeros((64, 64))
    c["c2"] = np.block([[C64, z], [z, C64]]).astype(np.float32)
    c["s2n"] = np.block([[-S64, z], [z, -S64]]).astype(np.float32)
    q = np.arange(128)[:, None, None]
    j = np.arange(J)[None, :, None]
    k1 = np.arange(128)[None, None, :]
    m = (k1 * (J * q + j)) % S
    th = 2 * np.pi * m / S
    sc = 1.0 / np.sqrt(128.0)
    c["tc"] = (np.cos(th) * sc).reshape(128, J * 128).astype(np.float32).astype(bf)
    c["ts"] = (np.sin(th) * sc).reshape(128, J * 128).astype(np.float32).astype(bf)
    c["tsn"] = (-np.sin(th) * sc).reshape(128, J * 128).astype(np.float32).astype(bf)
    jj = np.arange(J)[:, None]
    k2 = np.arange(J)[None, :]
    th3 = 2 * np.pi * jj * k2 / J
    cs3 = np.zeros((J, 2, J))
    cs3[:, 0, :] = np.cos(th3) / np.sqrt(J)
    cs3[:, 1, :] = np.sin(th3) / np.sqrt(J)
    c["cs3"] = cs3.reshape(2 * J, J).astype(np.float32).astype(bf)
    ic = np.zeros((128, 2, 2, 8), dtype=np.float32)
    for mch in range(2):
        for half in range(2):
            w = (2, 4, 8, 16)[mch * 2 + half]
            for t in range(8):
                lo = max(t - w // 2, 0)
                hi = min(t + w // 2, S)
                ic[half * 64:(half + 1) * 64, mch, 0, t] = 1.0 / (hi - lo)
                tt = S - 8 + t
                lo = max(tt - w // 2, 0)
                hi = min(tt + w // 2, S)
                ic[half * 64:(half + 1) * 64, mch, 1, t] = 1.0 / (hi - lo)
    c["icnt"] = ic.reshape(128, 32)
    return c


CONST_SPECS = lambda J: {
    "ident": ([128, 128], BF16), "ones": ([128, 128], BF16),
    "c2": ([128, 128], F32), "s2n": ([128, 128], F32),
    "tc": ([128, J * 128], BF16), "ts": ([128, J * 128], BF16), "tsn": ([128, J * 128], BF16),
    "cs3": ([2 * J, J], BF16), "icnt": ([128, 32], F32),
}

PARAM_SPECS = lambda L: {
    "pre_mix_gain": [L, D], "post_mix_gain": [L, D], "pre_ffn_gain": [L, D], "post_ffn_gain": [L, D],
    "w_in": [L, D, PW], "conv_w": [L, 3, 256], "pool_w": [L, 4, 64, 64], "pool_scale": [L, 256],
    "fourier_w": [L, 4, 64, 64], "spatial_w": [L, 4, 128, 128], "spatial_b": [L, 4, 128],
    "group_norm_gain": [L, D], "w_out": [L, D, D], "w_gate": [L, D, DFF], "w_up": [L, D, DFF],
    "w_down": [L, DFF, D],
}


CFG = dict(xt=2, hb=1, cv=1, pl=1, yfm=1, gn=1, gm=1, ntf=2, cpools={0: [0, 1], 1: [2], 2: [3], 3: [0, 1]})


def build(S=8192, L=2, dbg=None):
    J = S // 128
    NB = S // 512
    J2 = 2 * J
    nc = bass.Bass("TRN2", target_bir_lowering=False)
    stack = ExitStack()
    with stack:
        dram = {}
        dram["x"] = nc.dram_tensor("x", [S, D], F32, kind="ExternalInput").ap()
        for k, shp in PARAM_SPECS(L).items():
            dram[k] = nc.dram_tensor(k, shp, F32, kind="ExternalInput").ap()
        for k, (shp, dt_) in CONST_SPECS(J).items():
            dram[k] = nc.dram_tensor(k, shp, dt_, kind="ExternalInput").ap()
        out_d = nc.dram_tensor("out", [S, D], F32, kind="ExternalOutput").ap()
        xmid = [nc.dram_tensor(f"xmid{l}", [S, D], F32, kind="Internal").ap() for l in range(max(L - 1, 1))]
        ysc = nc.dram_tensor("ysc", [NB, 128, 8, 512], BF16, kind="Internal").ap()
        wgs = nc.dram_tensor("wgs", [L, NDC, 128, 1024], BF16, kind="Internal").ap()
        wus = nc.dram_tensor("wus", [L, NDC, 128, 1024], BF16, kind="Internal").ap()
        fsc = nc.dram_tensor("fsc", [128, 2, S], BF16, kind="Internal").ap()
        wds = nc.dram_tensor("wds", [L, NDC, 128, 1024], BF16, kind="Internal").ap()
        wos = nc.dram_tensor("wos", [L, 8, 128, 1024], BF16, kind="Internal").ap()
        dbg_out = {}
        if dbg:
            for k, shp, dt_ in dbg:
                dbg_out[k] = nc.dram_tensor(k, shp, dt_, kind="ExternalOutput").ap()

        sc = Sched(nc, stack)

        def sb(name, shape, dt_):
            return stack.enter_context(nc.sbuf_tensor("s_" + name, shape, dt_))

        ident = sb("ident", [128, 128], BF16); b_ident = Buf("ident")
        ones = sb("ones", [128, 128], BF16); b_ones = Buf("ones")
        epsc = sb("epsc", [128, 1], F32); b_eps = Buf("eps")
        cw = sb("cw", [128, L, 3, 2], F32); b_cw = Buf("cw")
        psc = sb("psc", [128, L, 2], F32); b_psc = Buf("psc")
        gng = sb("gng", [128, L, 8], F32); b_gng = Buf("gng")
        bsp = sb("bsp", [128, L, 4], F32); b_bsp = Buf("bsp")
        icnt = sb("icnt", [128, 2, 2, 8], F32); b_icnt = Buf("icnt")
        ggm = sb("ggm", [128, 256], F32); b_ggm = Buf("ggm")

        psum = [stack.enter_context(nc.psum_tensor(f"ps{i}", [128, 1024], F32)) for i in range(4)]
        b_ps = [[Buf(f"ps{i}a"), Buf(f"ps{i}b")] for i in range(4)]
        ps_rr = [0]

        ps_rr2 = {0: 0, 1: 0, "s0": 0, "s1": 0}

        class _Half:
            def __init__(self, t, off):
                self.t = t
                self.off = off

            def __getitem__(self, key):
                p, c = key
                assert c.start is not None and c.stop is not None and c.stop <= 512
                return self.t[p, c.start + self.off:c.stop + self.off]

        pools = {"cur": {0: [0, 1], 1: [2, 3]}}

        def set_pools(d):
            pools["cur"] = d
            for k in list(ps_rr2.keys()):
                ps_rr2[k] = 0

        def next_ps(pool=0):
            lst = pools["cur"][pool]
            k = ps_rr2.get(pool, 0)
            ps_rr2[pool] = k + 1
            i = lst[k % len(lst)]
            return psum[i], b_ps[i]

        def next_ps1(pool=0):
            lst = pools["cur"][pool]
            key = "s%d" % pool
            k = ps_rr2.get(key, 0)
            ps_rr2[key] = k + 1
            k = k % (2 * len(lst))
            pi = lst[k // 2]
            return _Half(psum[pi], (k % 2) * 512), [b_ps[pi][k % 2]]

        sc.dma(ident[:, :], dram["ident"], writes=[b_ident])
        sc.dma(ones[:, :], dram["ones"], writes=[b_ones])
        sc.dma(icnt[:, :, :, :].rearrange("p a b c -> p (a b c)"), dram["icnt"], writes=[b_icnt])
        sc.op("pool", lambda e: e.memset(epsc[:, :], EPS), writes=[b_eps])
        sc.dma(cw[:, :, :, :], dram["conv_w"].rearrange("l t (m p) -> p l t m", p=128), writes=[b_cw],
               allow_slow_non_contiguous=True)
        sc.dma(psc[:, :, :], dram["pool_scale"].rearrange("l (m p) -> p l m", p=128), writes=[b_psc],
               allow_slow_non_contiguous=True)
        sc.dma(gng[:, :, :], dram["group_norm_gain"].rearrange("l (k p) -> p l k", p=128), writes=[b_gng],
               allow_slow_non_contiguous=True)
        sc.dma(bsp[:, :, :], dram["spatial_b"].rearrange("l h p -> p l h"), writes=[b_bsp],
               allow_slow_non_contiguous=True)

        def convert_gate_up(l, stg, b_stg):
            n = 0
            for src, dst in ((dram["w_gate"], wgs), (dram["w_up"], wus)):
                for c0 in range(NDC):
                    s_ = n % len(stg)
                    n += 1
                    sc.dma(stg[s_][:, :].rearrange("p (k n) -> p k n", k=8),
                           src[l][:, c0 * 128:(c0 + 1) * 128].rearrange("(k p) n -> p k n", p=128),
                           writes=[b_stg[s_]], q="pool")
                    sc.dma(dst[l, c0], stg[s_][:, :], reads=[b_stg[s_]], q="pool")

        def convert_down_out(l, stg, b_stg):
            n = 0
            for src, dst, nch in ((dram["w_out"], wos, 8), (dram["w_down"], wds, NDC)):
                for c0 in range(nch):
                    s_ = n % len(stg)
                    n += 1
                    sc.dma(stg[s_][:, :], src[l][c0 * 128:(c0 + 1) * 128, :], writes=[b_stg[s_]], q="pool")
                    sc.dma(dst[l, c0], stg[s_][:, :], reads=[b_stg[s_]], q="pool")

        def rstd_small(out_ap, in_ap, n, b_in, b_out, b_tmp, tmp_ap, lnexp=True):
            if lnexp:
                sc.op("act", lambda e: e.activation(out=tmp_ap, in_=in_ap, func=AF.Ln, scale=1.0 / n, bias=epsc[:, 0:1]),
                      reads=[b_in, b_eps], writes=[b_tmp])
                sc.op("act", lambda e: e.activation(out=out_ap, in_=tmp_ap, func=AF.Exp, scale=-0.5), reads=[b_tmp], writes=[b_out])
            else:
                sc.op("act", lambda e: e.activation(out=tmp_ap, in_=in_ap, func=AF.Sqrt, scale=1.0 / n, bias=epsc[:, 0:1]),
                      reads=[b_in, b_eps], writes=[b_tmp])
                sc.op("dve", lambda e: e.reciprocal(out=out_ap, in_=tmp_ap), reads=[b_tmp], writes=[b_out])

        def layer(l):
            x_in = dram["x"] if l == 0 else xmid[l - 1]
            x_out = out_d if l == L - 1 else xmid[l]
            sc.dma(ggm[:, :], dram["group_norm_gain"][l:l + 1, 768:1024].partition_broadcast(128), writes=[b_ggm])

            set_pools({0: [0, 1], 1: [2, 3]})
            with ExitStack() as pa:
                def sa(name, shape, dt_):
                    return pa.enter_context(nc.sbuf_tensor(f"a_{name}_{l}", shape, dt_))
                w_in_sb = sa("w_in", [128, 8, PW], BF16); b_win = Buf(f"win{l}")
                gA = sa("gA", [128, D], F32); b_gA = Buf(f"gA{l}")
                sc.dma(gA[:, :], dram["pre_mix_gain"][l:l + 1, :].partition_broadcast(128), writes=[b_gA])
                sc.dma(w_in_sb[:, :, :], dram["w_in"][l].rearrange("(k p) n -> p k n", p=128), writes=[b_win], q="pool")
                pwbd = sa("pwbd", [128, 2, 128], BF16); b_pwbd = Buf(f"pwbd{l}")
                sc.op("pool", lambda e: e.memset(pwbd[:, :, :], 0.0), writes=[b_pwbd])
                for g in range(4):
                    h0 = (g % 2) * 64
                    sc.dma(pwbd[h0:h0 + 64, g // 2, h0:h0 + 64], dram["pool_w"][l, g], writes=[b_pwbd], q="pool")
                wsn = sa("wsn", [128, 4, 128], BF16); b_wsn = Buf(f"wsn{l}")
                sc.dma(wsn[:, :, :], dram["spatial_w"][l].rearrange("h p q -> p h q"), writes=[b_wsn], q="pool")
                wsT = sa("wsT", [128, 4, 128], BF16); b_wsT = Buf(f"wsT{l}")
                pt_, bp_ = next_ps()

                def f_wsT(e, pt_=pt_):
                    for h in range(4):
                        ins = e.matmul(pt_[:, h * 128:(h + 1) * 128], lhsT=wsn[:, h, :], rhs=ident[:, :], start=True, stop=True)
                    return ins
                sc.op("pe", f_wsT, reads=[b_wsn, b_ident], writes=bp_[:1])
                sc.op("dve", lambda e, pt_=pt_: e.tensor_copy(out=wsT[:, :, :].rearrange("p h q -> p (h q)"), in_=pt_[:, 0:512]),
                      reads=bp_[:1], writes=[b_wsT])

                stg = [sa(f"stg{i}", [128, 1024], BF16) for i in range(2)]
                b_stg = [Buf(f"stgA{l}_{i}") for i in range(2)]
                convert_gate_up(l, stg, b_stg)

                def ring(name, shape, dt_, n):
                    return ([sa(f"{name}{k}", shape, dt_) for k in range(n)], [Buf(f"{name}{l}_{k}") for k in range(n)])

                def pick(r, idx):
                    return r[0][idx % len(r[0])], r[1][idx % len(r[1])]
                NX = CFG["xt"]
                xt, b_xt = ring("xt", [128, 4, D], F32, NX)
                ssx_r = ring("ssx", [128, 4], F32, 2); ssx2_r = ring("ssx2", [128, 4], F32, 2); rsx_r = ring("rsx", [128, 4], F32, 2)
                hb_r = ring("hb", [128, 4, D], BF16, CFG["hb"])
                hTx, b_hTx = ring("hTx", [128, 8, 528], BF16, 3)
                z_r = ring("z", [128, 528], F32, CFG["cv"]); cz_r = ring("cz", [128, 528], F32, CFG["cv"])
                bsb_r = ring("bsb", [128, 512], F32, CFG["cv"])
                acc_r = ring("acc", [128, 512], F32, CFG["cv"]); acc2_r = ring("acc2", [128, 512], F32, CFG["cv"])
                yfm_r = [ring(f"yfm{k}", [128, 512], F32, CFG["yfm"]) for k in range(4)]
                p_r = ring("p", [128, 528], F32, CFG["pl"]); Ra_r = ring("Ra", [128, 528], F32, CFG["pl"])
                Rb_r = ring("Rb", [128, 528], F32, CFG["pl"]); Rc_r = ring("Rc", [128, 528], F32, CFG["pl"])
                etmp = sa("etmp", [128, 8], F32); b_etmp = Buf("etmp")
                dpl_r = [ring(f"dpl{k}", [128, 512], BF16, CFG["pl"]) for k in range(2)]
                sq_r = [ring(f"sq{k}", [128, 512], BF16, CFG["gn"]) for k in range(2)]
                sd_r = ring("sd", [128, 512], F32, CFG["gn"]); rs_r = ring("rs", [128, 512], F32, CFG["gn"])
                yst, b_yst = ring("yst", [128, 6, 512], BF16, 2)
                fst_r = ring("fst", [128, 2, 512], BF16, 2)
                uv_r = ring("uv", [128, 4, 512], F32, CFG["gm"]); sqv_r = ring("sqv", [128, 4, 256], F32, CFG["gm"])
                vst_r = [ring(f"vst{k}", [128, 16], F32, 2) for k in range(6)]
                vh_r = ring("vh", [128, 4, 256], BF16, CFG["gm"]); yg_r = ring("yg", [128, 4, 256], F32, CFG["gm"])
                ygn_r = ring("ygn", [128, 4, 256], BF16, CFG["gm"])
                gst_r = [ring(f"gst{k}", [128, 4], F32, 2) for k in range(3)]

                def load_x(i):
                    s_ = i % NX
                    sc.dma(xt[s_][:, :, :], x_in[i * 512:(i + 1) * 512, :].rearrange("(c p) d -> p c d", p=128),
                           writes=[b_xt[s_]])

                def stageN(i):
                    s_ = i % NX
                    xs = xt[s_]
                    hs = i % 3
                    ssx, b_ssx = pick(ssx_r, i); ssx2, b_ssx2 = pick(ssx2_r, i); rsx, b_rsx = pick(rsx_r, i)
                    hb, b_hb = pick(hb_r, i)
                    for c in range(4):
                        sc.op("act", lambda e, c=c: e.activation(out=hb[:, c, :], in_=xs[:, c, :], func=AF.Square,
                                                                 accum_out=ssx[:, c:c + 1]),
                              reads=[b_xt[s_]], writes=[b_hb, b_ssx])
                    rstd_small(rsx[:, :], ssx[:, :], D, b_ssx, b_rsx, b_ssx2, ssx2[:, :])
                    for c in range(4):
                        sc.op("dve", lambda e, c=c: e.scalar_tensor_tensor(out=hb[:, c, :], in0=xs[:, c, :], scalar=rsx[:, c:c + 1],
                                                                         in1=gA[:, :], op0=ALU.mult, op1=ALU.mult),
                              reads=[b_xt[s_], b_rsx, b_gA], writes=[b_hb])
                    if i == 0:
                        sc.op("pool", lambda e: e.memset(hTx[hs][:, :, 0:8], 0.0), writes=[b_hTx[hs]])
                    if i == NB - 1:
                        sc.op("pool", lambda e: e.memset(hTx[hs][:, :, 520:528], 0.0), writes=[b_hTx[hs]])
                    for c in range(4):
                        pt, bp = next_ps()

                        def f_tr(e, c=c, pt=pt):
                            for k in range(8):
                                ins = e.matmul(pt[:, k * 128:(k + 1) * 128], lhsT=hb[:, c, k * 128:(k + 1) * 128], rhs=ident[:, :],
                                               start=True, stop=True)
                            return ins
                        sc.op("pe", f_tr, reads=[b_hb, b_ident], writes=bp)
                        sc.op("act", lambda e, c=c, pt=pt: e.activation(out=hTx[hs][:, :, 8 + c * 128:8 + (c + 1) * 128],
                                                                      in_=pt[:, :].rearrange("p (k n) -> p k n", k=8), func=AF.Copy),
                              reads=bp, writes=[b_hTx[hs]])
                    if i >= 1:
                        hp = (i - 1) % 3
                        sc.op("pool", lambda e: e.tensor_copy(out=hTx[hp][:, :, 520:528], in_=hTx[hs][:, :, 8:16]),
                              reads=[b_hTx[hs]], writes=[b_hTx[hp]])
                    if i + 1 < NB:
                        hn = (i + 1) % 3
                        sc.op("pool", lambda e: e.tensor_copy(out=hTx[hn][:, :, 0:8], in_=hTx[hs][:, :, 512:520]),
                              reads=[b_hTx[hs]], writes=[b_hTx[hn]])

                def proj_fm(i, m, halo):
                    hs = i % 3
                    pt, bp = next_ps() if halo else next_ps1()
                    if halo:
                        def f(e, pt=pt):
                            for k in range(8):
                                e.matmul(pt[:, 0:512], lhsT=w_in_sb[:, k, m * 128:(m + 1) * 128], rhs=hTx[hs][:, k, 0:512],
                                         start=(k == 0), stop=(k == 7))
                            for k in range(8):
                                ins = e.matmul(pt[:, 512:528], lhsT=w_in_sb[:, k, m * 128:(m + 1) * 128], rhs=hTx[hs][:, k, 512:528],
                                               start=(k == 0), stop=(k == 7))
                            return ins
                        sc.op("pe", f, reads=[b_win, b_hTx[hs]], writes=bp)
                    else:
                        def f(e, pt=pt):
                            for k in range(8):
                                ins = e.matmul(pt[:, 0:512], lhsT=w_in_sb[:, k, m * 128:(m + 1) * 128], rhs=hTx[hs][:, k, 8:520],
                                               start=(k == 0), stop=(k == 7))
                            return ins
                        sc.op("pe", f, reads=[b_win, b_hTx[hs]], writes=bp[:1])
                    return pt, bp

                def group_norm_fm(i, gidx, ys):
                    ri = 2 * i + gidx
                    sqs = [pick(sq_r[mm], ri) for mm in range(2)]
                    sd, b_sd = pick(sd_r, ri); rs, b_rs = pick(rs_r, ri)
                    yf = [pick(yfm_r[2 * gidx + mm], i) for mm in range(2)]
                    for mm in range(2):
                        sc.op("act", lambda e, mm=mm: e.activation(out=sqs[mm][0][:, :], in_=yf[mm][0][:, :], func=AF.Square),
                              reads=[yf[mm][1]], writes=[sqs[mm][1]])
                    pt, bp = next_ps1(1)

                    def f(e, pt=pt):
                        for mm in range(2):
                            ins = e.matmul(pt[:, 0:512], lhsT=ones[:, :], rhs=sqs[mm][0][:, :], start=(mm == 0), stop=(mm == 1))
                        return ins
                    sc.op("pe", f, reads=[b_ones, sqs[0][1], sqs[1][1]], writes=bp[:1])
                    sc.op("act", lambda e, pt=pt: e.activation(out=sd[:, :], in_=pt[:, 0:512], func=AF.Ln, scale=1.0 / 256,
                                                             bias=epsc[:, 0:1]), reads=bp[:1] + [b_eps], writes=[b_sd])
                    sc.op("act", lambda e: e.activation(out=rs[:, :], in_=sd[:, :], func=AF.Exp, scale=-0.5), reads=[b_sd], writes=[b_rs])
                    for mm in range(2):
                        kc = 2 * gidx + mm
                        sc.op("dve", lambda e, mm=mm, kc=kc: e.scalar_tensor_tensor(out=yst[ys][:, kc, :], in0=yf[mm][0][:, :],
                                                                                 scalar=gng[:, l, kc:kc + 1], in1=rs[:, :],
                                                                                 op0=ALU.mult, op1=ALU.mult),
                              reads=[yf[mm][1], b_gng, b_rs], writes=[b_yst[ys]])

                def conv_chunk(i, mm):
                    ri = 2 * i + mm
                    z_sb, b_z = pick(z_r, ri); cz, b_cz = pick(cz_r, ri); acc, b_acc = pick(acc_r, ri); acc2, b_acc2 = pick(acc2_r, ri)
                    yf, b_yf = pick(yfm_r[mm], i)
                    pz, bz = proj_fm(i, 4 + mm, True)
                    sc.op("act", lambda e: e.activation(out=z_sb[:, :], in_=pz[:, 0:528], func=AF.Copy), reads=bz, writes=[b_z])
                    pc, bc = proj_fm(i, 2 + mm, True)
                    sc.op("dve", lambda e: e.tensor_tensor(out=cz[:, :], in0=pc[:, 0:528], in1=z_sb[:, :], op=ALU.mult),
                          reads=bc + [b_z], writes=[b_cz])
                    bsb, b_bsb = pick(bsb_r, ri)
                    pb, bb = proj_fm(i, mm, False)
                    sc.op("act", lambda e: e.activation(out=bsb[:, :], in_=pb[:, 0:512], func=AF.Copy), reads=bb[:1], writes=[b_bsb])
                    sc.op("act", lambda e: e.activation(out=acc[:, :], in_=cz[:, 8:520], func=AF.Copy, scale=cw[:, l, 1, mm:mm + 1]),
                          reads=[b_cz, b_cw], writes=[b_acc])
                    sc.op("dve", lambda e: e.scalar_tensor_tensor(out=acc2[:, :], in0=cz[:, 7:519], scalar=cw[:, l, 0, mm:mm + 1],
                                                                  in1=acc[:, :], op0=ALU.mult, op1=ALU.add),
                          reads=[b_cz, b_cw, b_acc], writes=[b_acc2])
                    sc.op("dve", lambda e: e.scalar_tensor_tensor(out=acc[:, :], in0=cz[:, 9:521], scalar=cw[:, l, 2, mm:mm + 1],
                                                                  in1=acc2[:, :], op0=ALU.mult, op1=ALU.add),
                          reads=[b_cz, b_cw, b_acc2], writes=[b_acc])
                    sc.op("dve", lambda e: e.tensor_tensor(out=yf[:, :], in0=bsb[:, :], in1=acc[:, :], op=ALU.mult),
                          reads=[b_bsb, b_acc], writes=[b_yf])

                def pool_chunk(i, mm):
                    ri = 2 * i + mm
                    p_sb, b_p = pick(p_r, ri); Ra, b_Ra = pick(Ra_r, ri); Rb, b_Rb = pick(Rb_r, ri); Rc, b_Rc = pick(Rc_r, ri)
                    Rd, b_Rd = Ra, b_Ra
                    dpl, b_dpl = pick(dpl_r[mm], i)
                    pp, bpp = proj_fm(i, 6 + mm, True)
                    sc.op("act", lambda e: e.activation(out=p_sb[:, :], in_=pp[:, 0:528], func=AF.Copy), reads=bpp, writes=[b_p])
                    e0 = "pool" if mm == 0 else "dve"
                    sc.op(e0, lambda e: e.tensor_tensor(out=Ra[:, 0:527], in0=p_sb[:, 0:527], in1=p_sb[:, 1:528], op=ALU.add),
                          reads=[b_p], writes=[b_Ra])
                    sc.op(e0, lambda e: e.tensor_tensor(out=Rb[:, 0:525], in0=Ra[:, 0:525], in1=Ra[:, 2:527], op=ALU.add),
                          reads=[b_Ra], writes=[b_Rb])
                    if mm == 0:
                        wins = ((0, 64, Ra, b_Ra, 2), (64, 128, Rb, b_Rb, 4))
                    else:
                        sc.op("dve", lambda e: e.tensor_tensor(out=Rc[:, 0:521], in0=Rb[:, 0:521], in1=Rb[:, 4:525], op=ALU.add),
                              reads=[b_Rb], writes=[b_Rc])
                        sc.op("dve", lambda e: e.tensor_tensor(out=Rd[:, 0:513], in0=Rc[:, 0:513], in1=Rc[:, 8:521], op=ALU.add),
                              reads=[b_Rc], writes=[b_Rd])
                        wins = ((0, 64, Rc, b_Rc, 8), (64, 128, Rd, b_Rd, 16))
                    for (p0, p1, R, bR, w) in wins:
                        o = 8 - w // 2
                        sc.op("dve", lambda e, p0=p0, p1=p1, R=R, w=w, o=o: e.scalar_tensor_tensor(
                            out=dpl[p0:p1, :], in0=R[p0:p1, o:o + 512], scalar=1.0 / w, in1=p_sb[p0:p1, 8:520],
                            op0=ALU.mult, op1=ALU.subtract), reads=[bR, b_p], writes=[b_dpl])
                        for side, blk in ((0, 0), (1, NB - 1)):
                            if i != blk:
                                continue
                            c0 = 0 if side == 0 else 504
                            sc.op("dve", lambda e, p0=p0, p1=p1, R=R, o=o, c0=c0, side=side: e.tensor_tensor(
                                out=etmp[p0:p1, :], in0=R[p0:p1, o + c0:o + c0 + 8], in1=icnt[p0:p1, mm, side, :], op=ALU.mult),
                                reads=[bR, b_icnt], writes=[b_etmp])
                            sc.op("dve", lambda e, p0=p0, p1=p1, c0=c0: e.tensor_tensor(
                                out=dpl[p0:p1, c0:c0 + 8], in0=etmp[p0:p1, :], in1=p_sb[p0:p1, 8 + c0:16 + c0], op=ALU.subtract),
                                reads=[b_etmp, b_p], writes=[b_dpl])

                def stageP(i):
                    hs = i % 3
                    ys = i % 2
                    conv_chunk(i, 0)
                    conv_chunk(i, 1)
                    pool_chunk(i, 0)
                    pool_chunk(i, 1)
                    dp = [pick(dpl_r[mm], i) for mm in range(2)]
                    yfp = [pick(yfm_r[2 + mm], i) for mm in range(2)]
                    pt, bp = next_ps(1)

                    def f_pw(e, pt=pt):
                        for mm in range(2):
                            ins = e.matmul(pt[:, mm * 512:(mm + 1) * 512], lhsT=pwbd[:, mm, :], rhs=dp[mm][0][:, :], start=True, stop=True)
                        return ins
                    sc.op("pe", f_pw, reads=[b_pwbd, dp[0][1], dp[1][1]], writes=bp)
                    for mm in range(2):
                        sc.op("act", lambda e, mm=mm, pt=pt: e.activation(out=yfp[mm][0][:, :], in_=pt[:, mm * 512:(mm + 1) * 512],
                                                                        func=AF.Copy, scale=psc[:, l, mm:mm + 1]),
                              reads=[bp[mm], b_psc], writes=[yfp[mm][1]])
                    fst, b_fst = pick(fst_r, i)
                    for mm in range(2):
                        pf, bf_ = proj_fm(i, 8 + mm, False)
                        sc.op("act", lambda e, mm=mm, pf=pf: e.activation(out=fst[:, mm, :], in_=pf[:, 0:512], func=AF.Copy),
                              reads=bf_[:1], writes=[b_fst])
                    sc.dma(fsc[:, :, i * 512:(i + 1) * 512], fst[:, :, :], reads=[b_fst])
                    uv, b_uv = pick(uv_r, i); sqv, b_sqv = pick(sqv_r, i); vh, b_vh = pick(vh_r, i)
                    yg, b_yg = pick(yg_r, i); ygn, b_ygn = pick(ygn_r, i)
                    vstp = [pick(vst_r[k], i) for k in range(6)]
                    vst = [t[0] for t in vstp]; b_vst = [t[1] for t in vstp]
                    gstp = [pick(gst_r[k], i) for k in range(3)]
                    gst = [t[0] for t in gstp]; b_gst = [t[1] for t in gstp]
                    for c in range(4):
                        pt, bp = next_ps1()

                        def f_uv(e, c=c, pt=pt):
                            for k in range(8):
                                ins = e.matmul(pt[:, 0:512], lhsT=hTx[hs][:, k, 8 + c * 128:8 + (c + 1) * 128], rhs=w_in_sb[:, k, 1280:1792],
                                               start=(k == 0), stop=(k == 7))
                            return ins
                        sc.op("pe", f_uv, reads=[b_win, b_hTx[hs]], writes=bp[:1])
                        sc.op("act", lambda e, c=c, pt=pt: e.activation(out=uv[:, c, :], in_=pt[:, 0:512], func=AF.Copy),
                              reads=bp[:1], writes=[b_uv])
                    group_norm_fm(i, 0, ys)
                    group_norm_fm(i, 1, ys)
                    v4 = uv[:, :, 256:512].rearrange("p n (h c) -> p n h c", h=4)
                    nh = lambda t: t[:, :].rearrange("p (n h) -> p n h", n=4)
                    sc.op("dve", lambda e: e.tensor_reduce(out=nh(vst[0]), in_=v4, axis=AX.X, op=ALU.add),
                          reads=[b_uv], writes=[b_vst[0]])
                    sc.op("act", lambda e: e.activation(out=sqv[:, :, :], in_=uv[:, :, 256:512], func=AF.Square), reads=[b_uv], writes=[b_sqv])
                    sc.op("dve", lambda e: e.tensor_reduce(out=nh(vst[1]), in_=sqv[:, :, :].rearrange("p n (h c) -> p n h c", h=4), axis=AX.X, op=ALU.add),
                          reads=[b_sqv], writes=[b_vst[1]])
                    sc.op("pool", lambda e: e.tensor_scalar(out=vst[2][:, :], in0=vst[0][:, :], scalar1=1.0 / 64, scalar2=None, op0=ALU.mult),
                          reads=[b_vst[0]], writes=[b_vst[2]])
                    sc.op("pool", lambda e: e.tensor_tensor(out=vst[3][:, :], in0=vst[2][:, :], in1=vst[2][:, :], op=ALU.mult),
                          reads=[b_vst[2]], writes=[b_vst[3]])
                    sc.op("dve", lambda e: e.scalar_tensor_tensor(out=vst[4][:, :], in0=vst[1][:, :], scalar=1.0 / 64, in1=vst[3][:, :],
                                                                  op0=ALU.mult, op1=ALU.subtract), reads=[b_vst[1], b_vst[3]], writes=[b_vst[4]])
                    sc.op("act", lambda e: e.activation(out=vst[3][:, :], in_=vst[4][:, :], func=AF.Ln, scale=1.0, bias=epsc[:, 0:1]),
                          reads=[b_vst[4], b_eps], writes=[b_vst[3]])
                    sc.op("act", lambda e: e.activation(out=vst[5][:, :], in_=vst[3][:, :], func=AF.Exp, scale=-0.5), reads=[b_vst[3]], writes=[b_vst[5]])
                    mean_b = nh(vst[2]).unsqueeze(3).to_broadcast([128, 4, 4, 64])
                    rstd_b = nh(vst[5]).unsqueeze(3).to_broadcast([128, 4, 4, 64])
                    sqv4 = sqv[:, :, :].rearrange("p n (h c) -> p n h c", h=4)
                    sc.op("dve", lambda e: e.tensor_tensor(out=sqv4, in0=v4, in1=mean_b, op=ALU.subtract),
                          reads=[b_uv, b_vst[2]], writes=[b_sqv])
                    sc.op("dve", lambda e: e.tensor_tensor(out=vh[:, :, :].rearrange("p n (h c) -> p n h c", h=4), in0=sqv4, in1=rstd_b, op=ALU.mult),
                          reads=[b_sqv, b_vst[5]], writes=[b_vh])
                    pt, bp = next_ps(1)

                    def f_sp(e, pt=pt):
                        for h in range(4):
                            ins = e.matmul(pt[:, h * 256:(h + 1) * 256], lhsT=wsT[:, h, :], rhs=vh[:, :, h * 64:(h + 1) * 64], start=True, stop=True)
                        return ins
                    sc.op("pe", f_sp, reads=[b_wsT, b_vh], writes=bp)
                    for h in range(4):
                        sc.op("dve", lambda e, h=h, pt=pt: e.scalar_tensor_tensor(
                            out=yg[:, :, h * 64:(h + 1) * 64], in0=pt[:, h * 256:(h + 1) * 256].rearrange("p (n c) -> p n c", n=4),
                            scalar=bsp[:, l, h:h + 1], in1=uv[:, :, h * 64:(h + 1) * 64], op0=ALU.add, op1=ALU.mult),
                            reads=[bp[h // 2], b_bsp, b_uv], writes=[b_yg])
                    sc.op("act", lambda e: e.activation(out=sqv[:, :, :], in_=yg[:, :, :], func=AF.Square), reads=[b_yg], writes=[b_sqv])
                    sc.op("dve", lambda e: e.tensor_reduce(out=gst[0][:, :], in_=sqv[:, :, :], axis=AX.X, op=ALU.add), reads=[b_sqv], writes=[b_gst[0]])
                    rstd_small(gst[2][:, :], gst[0][:, :], 256, b_gst[0], b_gst[2], b_gst[1], gst[1][:, :])
                    for n in range(4):
                        sc.op("dve", lambda e, n=n: e.scalar_tensor_tensor(out=ygn[:, n, :], in0=yg[:, n, :], scalar=gst[2][:, n:n + 1], in1=ggm[:, :],
                                                                          op0=ALU.mult, op1=ALU.mult), reads=[b_yg, b_gst[2], b_ggm], writes=[b_ygn])
                    pt, bp = next_ps(1)

                    def f_gt(e, pt=pt):
                        for mm in range(2):
                            for n in range(4):
                                ins = e.matmul(pt[:, mm * 512 + n * 128: mm * 512 + (n + 1) * 128], lhsT=ygn[:, n, mm * 128:(mm + 1) * 128],
                                               rhs=ident[:, :], start=True, stop=True)
                        return ins
                    sc.op("pe", f_gt, reads=[b_ygn, b_ident], writes=bp)
                    sc.op("act", lambda e, pt=pt: e.activation(out=yst[ys][:, 4:6, :], in_=pt[:, :].rearrange("p (m t) -> p m t", m=2), func=AF.Copy),
                          reads=bp, writes=[b_yst[ys]])
                    sc.dma(ysc[i, :, 0:6, :], yst[ys][:, :, :], reads=[b_yst[ys]])

                for i0 in range(min(NX, NB)):
                    load_x(i0)
                for i in range(NB + 1):
                    if i < NB:
                        stageN(i)
                        if i + NX < NB:
                            load_x(i + NX)
                    if i >= 1:
                        stageP(i - 1)
                sc.barrier()
            set_pools({0: [0, 1, 2, 3], 1: [0, 1, 2, 3]})
            with ExitStack() as pb_:
                def sbb(name, shape, dt_):
                    return pb_.enter_context(nc.sbuf_tensor(f"b_{name}_{l}", shape, dt_))
                tcs = sbb("tc", [128, J, 128], BF16); b_tc = Buf(f"tc{l}")
                tss = sbb("ts", [128, J, 128], BF16); b_ts = Buf(f"ts{l}")
                tsn = sbb("tsn", [128, J, 128], BF16); b_tsn = Buf(f"tsn{l}")
                cs3 = sbb("cs3", [J2, J], BF16); b_cs3 = Buf(f"cs3{l}")
                c2 = sbb("c2", [128, 128], F32); b_c2 = Buf(f"c2{l}")
                s2n = sbb("s2n", [128, 128], F32); b_s2n = Buf(f"s2n{l}")
                fw2 = sbb("fw2", [128, 2, 128], F32); b_fw2 = Buf(f"fw2{l}")
                Dh = sbb("Dh", [128, 2, 256], BF16); b_Dh = Buf(f"Dh{l}")
                G = sbb("G", [128, 2, J, 256], BF16); b_G = [Buf(f"G0{l}"), Buf(f"G1{l}")]
                A = sbb("A", [128, J, 2, 128], BF16); b_A = Buf(f"A{l}")
                Ap_t = [sbb(f"Ap{i}", [J2, 128, 128], BF16) for i in range(2)] if J * 256 < 128 * 128 else None
                sqB = sbb("sqB", [128, 2, 512], BF16); b_sqB = [Buf("sqB0"), Buf("sqB1")]
                sdB = sbb("sdB", [128, 512], F32); b_sdB = Buf("sdB")
                rsB = sbb("rsB", [128, 512], F32); b_rsB = Buf("rsB")
                yfo = [sbb(f"yfo{i}", [128, 2, 512], BF16) for i in range(2)]; b_yfo = [Buf(f"yfo{i}") for i in range(2)]
                fT = sbb("fT", [128, 2, S], BF16); b_fT = Buf(f"fT{l}")
                fTflat = fT
                sc.dma(fT[:, :, :], fsc, writes=[b_fT])
                stgB = [sbb(f"stgB{i}", [128, 1024], BF16) for i in range(3)]
                b_stgB = [Buf(f"stgB{l}_{i}") for i in range(3)]
                convert_down_out(l, stgB, b_stgB)
                sc.dma(tcs[:, :, :].rearrange("p j k -> p (j k)"), dram["tc"], writes=[b_tc])
                sc.dma(tss[:, :, :].rearrange("p j k -> p (j k)"), dram["ts"], writes=[b_ts])
                sc.dma(tsn[:, :, :].rearrange("p j k -> p (j k)"), dram["tsn"], writes=[b_tsn])
                sc.dma(cs3[:, :], dram["cs3"], writes=[b_cs3])
                sc.dma(c2[:, :], dram["c2"], writes=[b_c2])
                sc.dma(s2n[:, :], dram["s2n"], writes=[b_s2n])
                sc.op("pool", lambda e: e.memset(fw2[:, :, :], 0.0), writes=[b_fw2])
                for h in range(4):
                    h0 = (h % 2) * 64
                    sc.dma(fw2[h0:h0 + 64, h // 2, h0:h0 + 64], dram["fourier_w"][l, h], writes=[b_fw2])
                for half in range(2):
                    pt, bp = next_ps1()

                    def f_D(e, pt=pt, half=half):
                        e.matmul(pt[:, 0:128], lhsT=c2[:, :], rhs=fw2[:, half, :], start=True, stop=True)
                        return e.matmul(pt[:, 128:256], lhsT=s2n[:, :], rhs=fw2[:, half, :], start=True, stop=True)
                    sc.op("pe", f_D, reads=[b_c2, b_s2n, b_fw2], writes=bp[:1])
                    sc.op("dve", lambda e, pt=pt, half=half: e.tensor_copy(out=Dh[:, half, :], in_=pt[:, 0:256]), reads=bp[:1], writes=[b_Dh])
                for j in range(J):
                    pt, bp = next_ps1()

                    def f_s1(e, pt=pt, j=j):
                        for half in range(2):
                            ins = e.matmul(pt[:, half * 256:(half + 1) * 256], lhsT=fT[:, half, :].rearrange("p (q j) -> p j q", j=J)[:, j, :], rhs=Dh[:, half, :], start=True, stop=True)
                        return ins
                    sc.op("pe", f_s1, reads=[b_fT, b_Dh], writes=bp[:1])
                    eng = "act" if j % 2 == 0 else "dve"
                    if eng == "act":
                        sc.op("act", lambda e, pt=pt, j=j: e.activation(out=G[:, :, j, :], in_=pt[:, 0:512].rearrange("p (h c) -> p h c", h=2), func=AF.Copy),
                              reads=bp[:1], writes=b_G)
                    else:
                        sc.op("dve", lambda e, pt=pt, j=j: e.tensor_copy(out=G[:, :, j, :], in_=pt[:, 0:512].rearrange("p (h c) -> p h c", h=2)),
                              reads=bp[:1], writes=b_G)
                yT4 = fTflat
                for half in range(2):
                    for j0 in range(0, J, 2):
                        pt, bp = next_ps1()

                        def f_s2(e, pt=pt, j0=j0, half=half):
                            for jj in range(2):
                                j = j0 + jj
                                o = jj * 256
                                e.matmul(pt[:, o:o + 128], lhsT=tcs[:, j, :], rhs=G[:, half, j, 0:128], start=True, stop=False)
                                e.matmul(pt[:, o:o + 128], lhsT=tss[:, j, :], rhs=G[:, half, j, 128:256], start=False, stop=True)
                                e.matmul(pt[:, o + 128:o + 256], lhsT=tcs[:, j, :], rhs=G[:, half, j, 128:256], start=True, stop=False)
                                ins = e.matmul(pt[:, o + 128:o + 256], lhsT=tsn[:, j, :], rhs=G[:, half, j, 0:128], start=False, stop=True)
                            return ins
                        sc.op("pe", f_s2, reads=[b_tc, b_ts, b_tsn, b_G[half]], writes=bp[:1])
                        eng = "act" if (j0 // 2) % 2 == 0 else "dve"
                        o_ap = A[:, j0:j0 + 2, :, :]
                        i_ap = lambda pt: pt[:, 0:512].rearrange("p (j r c) -> p j r c", j=2, r=2)
                        if eng == "act":
                            sc.op("act", lambda e, pt=pt, o_ap=o_ap: e.activation(out=o_ap, in_=i_ap(pt), func=AF.Copy), reads=bp[:1], writes=[b_A])
                        else:
                            sc.op("dve", lambda e, pt=pt, o_ap=o_ap: e.tensor_copy(out=o_ap, in_=i_ap(pt)), reads=bp[:1], writes=[b_A])
                    if J * 256 >= 128 * 128:
                        Ap = G[0:J2, half, :, :].rearrange("p j c -> p (j c)")[:, 0:128 * 128].rearrange("p (c k) -> p c k", c=128)
                    else:
                        Ap = Ap_t[half][:, :, :]
                    for c0 in range(0, 128, 8):
                        pt, bp = next_ps()

                        def f_tr(e, pt=pt, c0=c0):
                            for cc in range(8):
                                ins = e.matmul(pt[0:J2, cc * 128:(cc + 1) * 128], lhsT=A[:, :, :, c0 + cc].rearrange("p j r -> p (j r)"), rhs=ident[:, :], start=True, stop=True)
                            return ins
                        sc.op("pe", f_tr, reads=[b_A, b_ident], writes=bp)
                        eng = "act" if (c0 // 8) % 2 == 0 else "dve"
                        o_ap = Ap[:, c0:c0 + 8, :]
                        if eng == "act":
                            sc.op("act", lambda e, pt=pt, o_ap=o_ap: e.activation(out=o_ap, in_=pt[0:J2, :].rearrange("p (c k) -> p c k", c=8), func=AF.Copy),
                                  reads=bp, writes=[b_G[half]])
                        else:
                            sc.op("dve", lambda e, pt=pt, o_ap=o_ap: e.tensor_copy(out=o_ap, in_=pt[0:J2, :].rearrange("p (c k) -> p c k", c=8)),
                                  reads=bp, writes=[b_G[half]])
                    KB = min(128, 512 // J)
                    for k0 in range(0, 128, KB):
                        pt, bp = next_ps1()

                        def f_s3(e, pt=pt, k0=k0, Ap=Ap):
                            for kk in range(KB):
                                ins = e.matmul(pt[:, kk * J:(kk + 1) * J], lhsT=Ap[:, :, k0 + kk], rhs=cs3[:, :], start=True, stop=True)
                            return ins
                        sc.op("pe", f_s3, reads=[b_G[half], b_cs3], writes=bp[:1])
                        eng = "act" if (k0 // KB) % 2 == 0 else "dve"
                        o_ap = yT4[:, half, :].rearrange("p (k2 k1) -> p k2 k1", k1=128)[:, :, k0:k0 + KB]
                        if eng == "act":
                            sc.op("act", lambda e, pt=pt, o_ap=o_ap: e.activation(out=o_ap, in_=pt[:, 0:KB * J].rearrange("p (k a) -> p a k", k=KB), func=AF.Copy),
                                  reads=bp[:1], writes=[b_fT])
                        else:
                            sc.op("dve", lambda e, pt=pt, o_ap=o_ap: e.tensor_copy(out=o_ap, in_=pt[:, 0:KB * J].rearrange("p (k a) -> p a k", k=KB)),
                                  reads=bp[:1], writes=[b_fT])
                if dbg and "dbg_y4" in dbg_out and l == 0:
                    sc.barrier()
                    sc.dma(dbg_out["dbg_y4"], yT4, reads=[b_fT])
                    sc.barrier()
                for i in range(NB):
                    fs = i % 2
                    for mm in range(2):
                        sc.op("act", lambda e, mm=mm, i=i: e.activation(out=sqB[:, mm, :], in_=yT4[:, mm, i * 512:(i + 1) * 512], func=AF.Square),
                              reads=[b_fT], writes=[b_sqB[mm]])
                    pt, bp = next_ps1()

                    def f_st(e, pt=pt):
                        for mm in range(2):
                            ins = e.matmul(pt[:, 0:512], lhsT=ones[:, :], rhs=sqB[:, mm, :], start=(mm == 0), stop=(mm == 1))
                        return ins
                    sc.op("pe", f_st, reads=[b_ones] + b_sqB, writes=bp[:1])
                    sc.op("act", lambda e, pt=pt: e.activation(out=sdB[:, :], in_=pt[:, 0:512], func=AF.Ln, scale=1.0 / 256, bias=epsc[:, 0:1]),
                          reads=bp[:1] + [b_eps], writes=[b_sdB])
                    sc.op("act", lambda e: e.activation(out=rsB[:, :], in_=sdB[:, :], func=AF.Exp, scale=-0.5), reads=[b_sdB], writes=[b_rsB])
                    for mm in range(2):
                        sc.op("dve",
                              lambda e, mm=mm, i=i, fs=fs: e.scalar_tensor_tensor(out=yfo[fs][:, mm, :], in0=yT4[:, mm, i * 512:(i + 1) * 512],
                                                                                 scalar=gng[:, l, 4 + mm:5 + mm], in1=rsB[:, :], op0=ALU.mult, op1=ALU.mult),
                              reads=[b_fT, b_gng, b_rsB], writes=[b_yfo[fs]])
                    sc.dma(ysc[i, :, 6:8, :], yfo[fs][:, :, :], reads=[b_yfo[fs]])
                sc.barrier()
            set_pools(CFG["cpools"])
            with ExitStack() as pc_:
                def sbc(name, shape, dt_):
                    return pc_.enter_context(nc.sbuf_tensor(f"c_{name}_{l}", shape, dt_))
                w_out_sb = sbc("w_out", [128, 8, D], BF16); b_wout = Buf(f"wout{l}")
                gC = sbc("gC", [128, 3, D], F32); b_gC = [Buf(f"gC{l}_{i}") for i in range(3)]
                for gi, gk in enumerate(("post_mix_gain", "pre_ffn_gain", "post_ffn_gain")):
                    sc.dma(gC[:, gi, :], dram[gk][l:l + 1, :].partition_broadcast(128), writes=[b_gC[gi]])
                w_dn_sb = sbc("w_dn", [128, NDC, D], BF16); b_wdn = Buf(f"wdn{l}")
                sc.dma(w_out_sb[:, :, :], wos[l].rearrange("k p n -> p k n"), writes=[b_wout])
                b_wdn4 = [Buf(f"wdn{l}_{i}") for i in range(2)]
                sc.dma(w_dn_sb[:, 0:NDC // 2, :], wds[l, 0:NDC // 2].rearrange("k p n -> p k n"), writes=[b_wdn4[0]])
                sc.dma(w_dn_sb[:, NDC // 2:NDC, :], wds[l, NDC // 2:NDC].rearrange("k p n -> p k n"), writes=[b_wdn4[1]])
                NW = 3
                wg_sb = [sbc(f"wg{i}", [128, 8, 128], BF16) for i in range(NW)]; b_wg = [Buf(f"wg{l}_{i}") for i in range(NW)]
                wu_sb = [sbc(f"wu{i}", [128, 8, 128], BF16) for i in range(NW)]; b_wu = [Buf(f"wu{l}_{i}") for i in range(NW)]
                xtc = [sbc(f"xtc{i}", [128, 4, D], F32) for i in range(2)]; b_xtc = [Buf(f"xtC{l}_{i}") for i in range(2)]
                yl = [sbc(f"yl{i}", [128, 8, 512], BF16) for i in range(2)]; b_yl = [Buf(f"yl{l}_{i}") for i in range(2)]
                junkf = sbc("junkf", [128, D], BF16); b_junkc = Buf("junkc")
                NTF = CFG["ntf"]
                tctr = [0, 0]
                tmpf = [sbc(f"tmpf{i}", [128, D], F32) for i in range(2 * NTF)]; b_tmpf = [Buf(f"tmpf{i}") for i in range(2 * NTF)]
                st = sbc("st", [128, 9, 4], F32); b_st = [[Buf(f"st{i}_{c}") for c in range(4)] for i in range(9)]
                h2 = sbc("h2", [128, 4, D], BF16); b_h2 = Buf("h2")
                h2Ts = [sbc(f"h2T{i}", [128, 8, 512], BF16) for i in range(2)]; b_h2Ts = [Buf(f"h2T{i}") for i in range(2)]
                actT = sbc("actT", [128, NDC, 512], BF16); b_act = Buf("actT")
                sg = [sbc(f"sg{i}", [128, 512], F32) for i in range(2)]; b_sg = [Buf(f"sg{i}") for i in range(2)]
                wctr = [0]

                def load_blk(i):
                    s_ = i % 2
                    sc.dma(xtc[s_][:, :, :], x_in[i * 512:(i + 1) * 512, :].rearrange("(c p) d -> p c d", p=128), writes=[b_xtc[s_]])
                    sc.dma(yl[s_][:, :, :], ysc[i], writes=[b_yl[s_]])

                def load_w(dc):
                    s_ = wctr[0] % NW
                    wctr[0] += 1
                    sc.dma(wg_sb[s_][:, :, :], wgs[l, dc].rearrange("p (k n) -> p k n", k=8), writes=[b_wg[s_]])
                    sc.dma(wu_sb[s_][:, :, :], wus[l, dc].rearrange("p (k n) -> p k n", k=8), writes=[b_wu[s_]])
                    return s_

                def resid_norm(i, c, pt, bp, gi, s0, final):
                    s_ = i % 2
                    xs = xtc[s_]
                    ring_id = 1 if final else 0
                    k = ring_id * NTF + tctr[ring_id] % NTF
                    tctr[ring_id] += 1
                    tf = tmpf[k]
                    b_tf = b_tmpf[k]
                    sc.op("act", lambda e: e.activation(out=tf[:, :], in_=pt[:, :], func=AF.Copy), reads=bp, writes=[b_tf])
                    sc.op("act", lambda e: e.activation(out=junkf[:, :], in_=tf[:, :], func=AF.Square, accum_out=st[:, s0, c:c + 1]),
                          reads=[b_tf], writes=[b_st[s0][c]])
                    rstd_small(st[:, s0 + 2, c:c + 1], st[:, s0, c:c + 1], D, b_st[s0][c], b_st[s0 + 2][c], b_st[s0 + 1][c], st[:, s0 + 1, c:c + 1], lnexp=False)
                    sc.op("dve", lambda e: e.scalar_tensor_tensor(out=tf[:, :], in0=tf[:, :], scalar=st[:, s0 + 2, c:c + 1], in1=gC[:, gi, :],
                                                                  op0=ALU.mult, op1=ALU.mult), reads=[b_st[s0 + 2][c], b_gC[gi]], writes=[b_tf])
                    sc.op("pool" if final else "dve", lambda e: e.tensor_tensor(out=xs[:, c, :], in0=tf[:, :], in1=xs[:, c, :], op=ALU.add),
                          reads=[b_tf], writes=[b_xtc[s_]])

                def stageC1(i):
                    s_ = i % 2
                    h2T = h2Ts[i % 2]; b_h2T = b_h2Ts[i % 2]
                    xs = xtc[s_]
                    for c in range(4):
                        pt, bp = next_ps(1)

                        def f_o(e, pt=pt, c=c):
                            for hf in range(2):
                                for kc in range(8):
                                    wk = (0, 1, 2, 3, 6, 7, 4, 5)[kc]
                                    ins = e.matmul(pt[:, hf * 512:(hf + 1) * 512], lhsT=yl[s_][:, kc, c * 128:(c + 1) * 128],
                                                   rhs=w_out_sb[:, wk, hf * 512:(hf + 1) * 512], start=(kc == 0), stop=(kc == 7))
                            return ins
                        sc.op("pe", f_o, reads=[b_yl[s_], b_wout], writes=bp)
                        resid_norm(i, c, pt, bp, 0, 0, False)
                        sc.op("act", lambda e, c=c: e.activation(out=junkf[:, :], in_=xs[:, c, :], func=AF.Square, accum_out=st[:, 3, c:c + 1]),
                              reads=[b_xtc[s_]], writes=[b_st[3][c]])
                        rstd_small(st[:, 5, c:c + 1], st[:, 3, c:c + 1], D, b_st[3][c], b_st[5][c], b_st[4][c], st[:, 4, c:c + 1], lnexp=False)
                        sc.op("dve", lambda e, c=c: e.scalar_tensor_tensor(out=h2[:, c, :], in0=xs[:, c, :], scalar=st[:, 5, c:c + 1], in1=gC[:, 1, :],
                                                                          op0=ALU.mult, op1=ALU.mult), reads=[b_xtc[s_], b_st[5][c], b_gC[1]], writes=[b_h2])
                    for c in range(4):
                        pt, bp = next_ps(3)

                        def f_tr(e, pt=pt, c=c):
                            for k in range(8):
                                ins = e.matmul(pt[:, k * 128:(k + 1) * 128], lhsT=h2[:, c, k * 128:(k + 1) * 128], rhs=ident[:, :], start=True, stop=True)
                            return ins
                        sc.op("pe", f_tr, reads=[b_h2, b_ident], writes=bp)
                        if c % 2 == 0:
                            sc.op("act", lambda e, pt=pt, c=c: e.activation(out=h2T[:, :, c * 128:(c + 1) * 128], in_=pt[:, :].rearrange("p (k n) -> p k n", k=8), func=AF.Copy),
                                  reads=bp, writes=[b_h2T])
                        else:
                            sc.op("dve", lambda e, pt=pt, c=c: e.tensor_copy(out=h2T[:, :, c * 128:(c + 1) * 128], in_=pt[:, :].rearrange("p (k n) -> p k n", k=8)),
                                  reads=bp, writes=[b_h2T])

                def stageC2(i, slots):
                    h2T = h2Ts[i % 2]; b_h2T = b_h2Ts[i % 2]
                    for dc in range(NDC):
                        ws = slots.pop(0)
                        if dc + NW - 1 < NDC:
                            slots.append(load_w(dc + NW - 1))
                        pt, bp = next_ps()

                        def f_g(e, pt=pt, ws=ws):
                            for k in range(8):
                                e.matmul(pt[:, 0:512], lhsT=wg_sb[ws][:, k, :], rhs=h2T[:, k, :], start=(k == 0), stop=(k == 7))
                            for k in range(8):
                                ins = e.matmul(pt[:, 512:1024], lhsT=wu_sb[ws][:, k, :], rhs=h2T[:, k, :], start=(k == 0), stop=(k == 7))
                            return ins
                        sc.op("pe", f_g, reads=[b_wg[ws], b_wu[ws], b_h2T], writes=bp)
                        sc.op("act", lambda e, pt=pt, dc=dc: e.activation(out=sg[dc % 2][:, :], in_=pt[:, 0:512], func=AF.Silu),
                              reads=bp[:1], writes=[b_sg[dc % 2]])
                        sc.op("dve", lambda e, pt=pt, dc=dc: e.tensor_tensor(out=actT[:, dc, :], in0=pt[:, 512:1024], in1=sg[dc % 2][:, :], op=ALU.mult),
                              reads=[bp[1], b_sg[dc % 2]], writes=[b_act])

                def stageC3(i):
                    s_ = i % 2
                    for c in range(4):
                        pt, bp = next_ps(2)

                        def f_d(e, pt=pt, c=c):
                            for hf in range(2):
                                for dc in range(NDC):
                                    ins = e.matmul(pt[:, hf * 512:(hf + 1) * 512], lhsT=actT[:, dc, c * 128:(c + 1) * 128],
                                                   rhs=w_dn_sb[:, dc, hf * 512:(hf + 1) * 512], start=(dc == 0), stop=(dc == NDC - 1))
                            return ins
                        sc.op("pe", f_d, reads=[b_act] + b_wdn4, writes=bp)
                        resid_norm(i, c, pt, bp, 2, 6, True)
                    sc.dma(x_out[i * 512:(i + 1) * 512, :].rearrange("(c p) d -> p c d", p=128), xtc[s_][:, :, :], reads=[b_xtc[s_]])

                load_blk(0)
                if NB > 1:
                    load_blk(1)
                for i in range(NB):
                    slots = [load_w(dc) for dc in range(NW - 1)]
                    stageC1(i)
                    stageC2(i, slots)
                    stageC3(i)
                    if i + 2 < NB:
                        load_blk(i + 2)
                sc.barrier()
        for l_ in range(L):
            layer(l_)
        sc.emit()
    return nc


_CACHE = {}


def kernel(**inputs):
    S = inputs["x"].shape[1]
    L = inputs["w_in"].shape[0]
    B = inputs["x"].shape[0]
    key = (S, L)
    if key not in _CACHE:
        _CACHE[key] = (build(S, L), host_consts(S))
    nc, consts = _CACHE[key]
    shared = {k: np.ascontiguousarray(np.asarray(inputs[k], dtype=np.float32)) for k in PARAM_SPECS(L)}
    shared.update(consts)
    x = np.asarray(inputs["x"], dtype=np.float32)
    in_maps = []
    for b in range(B):
        m = dict(shared)
        m["x"] = np.ascontiguousarray(x[b])
        in_maps.append(m)
    res = run_bass_kernel_spmd(nc, in_maps, core_ids=list(range(B)))
    return np.stack([np.asarray(r["out"], dtype=np.float32) for r in res.results], axis=0)
```

```python
from contextlib import ExitStack
import numpy as np
import ml_dtypes
import concourse.bass as bass
import concourse.mybir as mybir
from concourse.bass_utils import run_bass_kernel_spmd

F32 = mybir.dt.float32
BF16 = mybir.dt.bfloat16
AF = mybir.ActivationFunctionType
ALU = mybir.AluOpType
AX = mybir.AxisListType

D = 1024
DFF = 2816
NDC = DFF // 128
PW = 1792
EPS = 1e-6
ENGS = ("pe", "act", "dve", "pool", "sp")


class Buf:
    __slots__ = ("name", "w", "r", "semkey", "n", "last")

    def __init__(self, name):
        self.name = name
        self.w = None
        self.r = []
        self.semkey = None
        self.n = 0


class _FakeIns:
    def then_inc(self, *a, **k):
        return self


class _FakeEng:
    def __init__(self):
        self.calls = []

    def __getattr__(self, name):
        def f(*a, **k):
            self.calls.append((name, a, k))
            return _FakeIns()
        return f


def _free_size(ap):
    n = 1
    for d in ap.shape[1:]:
        n *= d
    return n


def _est_cost(eng, fn):
    fe = _FakeEng()
    fn(fe)
    t = 0.0
    for name, a, k in fe.calls:
        out = k.get("out", a[0] if a else None)
        F = _free_size(out)
        if name == "matmul":
            t += max(F, 48) * 0.43 + 14.0
        elif eng == "act":
            t += F * 0.87 + 200.0
        elif eng == "dve":
            t += F * 1.15 + 120.0
        elif eng == "pool":
            t += F * 3.5 + 250.0
        else:
            t += 60.0
    return t


SYNC_LAT = 150.0
DMA_LAT = 2200.0
DMA_BW = 120.0


class _Op:
    __slots__ = ("id", "eng", "fn", "preds", "cost", "dma", "owner", "nbytes", "tok", "finish", "prio", "succ", "npend", "ready", "tag")

    def __init__(self, id_, eng, fn, preds, cost, dma=False, owner=None, nbytes=0):
        self.id = id_
        self.eng = eng
        self.fn = fn
        self.preds = preds
        self.cost = cost
        self.dma = dma
        self.owner = owner
        self.nbytes = nbytes
        self.tok = None
        self.finish = 0.0
        self.prio = 0.0
        self.succ = []
        self.npend = 0
        self.ready = 0.0


class Sched:
    def __init__(self, nc, stack, reorder=True):
        self.nc = nc
        self.stack = stack
        self.reorder = reorder
        self.sems = {}
        self.ops = []
        self.region = []
        self.regions = []
        self.dma_bufs = []
        for e in ENGS:
            self.sems[e] = stack.enter_context(nc.semaphore("c_" + e))

    def _preds(self, reads, writes):
        p = set()
        for b in reads:
            if b.w is not None:
                p.add(b.w)
        for b in writes:
            if b.w is not None:
                p.add(b.w)
            p.update(b.r)
        return p

    def _commit(self, op, reads, writes):
        self.ops.append(op)
        self.region.append(op.id)
        for b in reads:
            b.r.append(op.id)
        for b in writes:
            b.w = op.id
            b.r = []
        return op.id

    def op(self, eng, fn, reads=(), writes=()):
        o = _Op(len(self.ops), eng, fn, self._preds(reads, writes), _est_cost(eng, fn))
        import sys as _sys
        o.tag = _sys._getframe(1).f_lineno
        return self._commit(o, reads, writes)

    def dma(self, out, in_, reads=(), writes=(), q="sp", **kw):
        owner = writes[0] if writes else reads[0]
        if owner.semkey is None:
            owner.semkey = f"d{len(self.dma_bufs)}_" + owner.name
            self.sems[owner.semkey] = self.stack.enter_context(self.nc.semaphore(owner.semkey))
            self.dma_bufs.append(owner)
            owner.n = 0
            owner.last = None
        preds = self._preds(reads, writes)
        if owner.last is not None:
            preds.add(owner.last)
        nbytes = _free_size(out) * out.shape[0] * (2 if out.dtype == BF16 else 4)
        o = _Op(len(self.ops), q, (lambda e: e.dma_start(out=out, in_=in_, **kw)), preds,
                60.0 if q == "sp" else 1000.0, dma=True, owner=owner, nbytes=nbytes)
        owner.last = o.id
        return self._commit(o, reads, writes)

    def barrier(self):
        self.regions.append(self.region)
        self.region = []

    def _schedule(self, ids):
        ops = self.ops
        inreg = set(ids)
        if not self.reorder:
            return list(ids)
        for i in ids:
            o = ops[i]
            o.succ = []
            o.npend = 0
            o.ready = 0.0
        for i in ids:
            o = ops[i]
            for p in o.preds:
                if p in inreg:
                    ops[p].succ.append(i)
                    o.npend += 1
        for i in reversed(ids):
            o = ops[i]
            m = 0.0
            for sidx in o.succ:
                if ops[sidx].prio > m:
                    m = ops[sidx].prio
            o.prio = m + o.cost + (DMA_LAT if o.dma else 0.0)
        free = {e: 0.0 for e in ENGS}
        ready = {e: [] for e in ENGS}
        import os as _os
        self.trace = {} if _os.environ.get("KTRACE") else None
        self.last_on = {}
        for i in ids:
            if ops[i].npend == 0:
                ready[ops[i].eng].append(i)
        order = []
        n = len(ids)
        while len(order) < n:
            best = None
            bkey = None
            for e in ENGS:
                fe = free[e]
                for i in ready[e]:
                    o = ops[i]
                    st = o.ready if o.ready > fe else fe
                    key = (st, -o.prio, i)
                    if bkey is None or key < bkey:
                        bkey = key
                        best = i
            o = ops[best]
            ready[o.eng].remove(best)
            st = bkey[0]
            if self.trace is not None:
                self.trace[best] = (st, free[o.eng], o.ready, self.last_on.get(o.eng))
                self.last_on[o.eng] = best
            free[o.eng] = st + o.cost
            if o.dma:
                o.finish = st + o.cost + DMA_LAT + o.nbytes / DMA_BW
            else:
                o.finish = st + o.cost
            order.append(best)
            for sidx in o.succ:
                so = ops[sidx]
                t = o.finish + SYNC_LAT
                if t > so.ready:
                    so.ready = t
                so.npend -= 1
                if so.npend == 0:
                    ready[so.eng].append(sidx)
        self.makespan = max(free.values())
        self._nreg = getattr(self, "_nreg", -1) + 1
        if self.trace is not None and self._nreg == int(_os.environ.get("KREGION", "0")):
            cur = max(ids, key=lambda i: ops[i].finish)
            chain = []
            while cur is not None and len(chain) < 400:
                st, fe, rd, prev = self.trace[cur]
                o = ops[cur]
                if rd >= fe and o.preds:
                    cand = [p for p in o.preds if p in inreg]
                    if not cand:
                        break
                    pbest = max(cand, key=lambda p: ops[p].finish)
                    chain.append((cur, o.eng, round(st), round(o.cost), "dep", getattr(o, "tag", "")))
                    cur = pbest
                else:
                    chain.append((cur, o.eng, round(st), round(o.cost), "eng", getattr(o, "tag", "")))
                    cur = prev
            for c in chain[:int(_os.environ.get("KTRACE"))]:
                print("KCHAIN", c)
        return order

    def emit(self):
        if self.region:
            self.regions.append(self.region)
            self.region = []
        nc = self.nc
        sems = self.sems
        ops = self.ops
        cnt = {e: 0 for e in ENGS}
        seen = {e: {} for e in ENGS}
        prog = {e: [] for e in ENGS}
        dcount = {}
        self.makespans = []

        def waits_for(eng, toks):
            need = {}
            for k, v in toks:
                if k == eng and eng in ("pe", "sp"):
                    continue
                if seen[eng].get(k, 0) >= v:
                    continue
                if need.get(k, 0) < v:
                    need[k] = v
            for k, v in need.items():
                seen[eng][k] = v
            return list(need.items())

        for ids in self.regions:
            order = self._schedule(ids)
            self.makespans.append(getattr(self, "makespan", 0.0))
            if not hasattr(self, "busy"):
                self.busy = []
            bz = {e: 0.0 for e in ENGS}
            for i in ids:
                bz[ops[i].eng] += ops[i].cost
            self.busy.append({e: round(v / 1e3) for e, v in bz.items()})
            for i in order:
                o = ops[i]
                w = waits_for(o.eng, [ops[p].tok for p in o.preds])
                if o.dma:
                    k = o.owner.semkey
                    dcount[k] = dcount.get(k, 0) + 1
                    o.tok = (k, 16 * dcount[k])
                    prog[o.eng].append((w, o.fn, (k, 16)))
                else:
                    cnt[o.eng] += 1
                    o.tok = (o.eng, cnt[o.eng])
                    prog[o.eng].append((w, o.fn, (o.eng, 1)))
            toks = [(k, 16 * v) for k, v in dcount.items()]
            toks += [(e, cnt[e]) for e in ENGS if e != "sp" and cnt[e] > 0]
            w = waits_for("sp", toks)
            cnt["sp"] += 1
            tsp = ("sp", cnt["sp"])
            prog["sp"].append((w, (lambda e: e.nop()), ("sp", 1)))
            for e in ENGS:
                if e == "sp":
                    continue
                w = waits_for(e, [tsp] + toks)
                if w:
                    prog[e].append((w, None, None))

        import os as _os
        if _os.environ.get("KDEBUG"):
            print("KDEBUG counts", cnt, "max dma", max(dcount.values()) * 16, "nsems", len(sems), "makespans_us", [round(m / 1e3) for m in self.makespans])
            print("KDEBUG busy_us", self.busy)
            print("KDEBUG nwaits", {e: sum(len(w) for w, _, _ in prog[e]) for e in ENGS}, {e: len(prog[e]) for e in ENGS})

        def run(e, lst):
            for waits, fn, inc in lst:
                for k, v in waits:
                    e.wait_ge(sems[k], v)
                if fn is not None:
                    ins = fn(e)
                    if inc is not None:
                        ins.then_inc(sems[inc[0]], inc[1])

        with nc.Block() as block:
            @block.sync
            def _(e):
                run(e, prog["sp"])

            @block.tensor
            def _(e):
                run(e, prog["pe"])

            @block.scalar
            def _(e):
                run(e, prog["act"])

            @block.vector
            def _(e):
                run(e, prog["dve"])

            @block.gpsimd
            def _(e):
                run(e, prog["pool"])


def host_consts(S):
    J = S // 128
    bf = ml_dtypes.bfloat16
    c = {}
    c["ident"] = np.eye(128, dtype=np.float32).astype(bf)
    c["ones"] = np.ones((128, 128), dtype=np.float32).astype(bf)
    a = np.arange(64)
    ang = 2 * np.pi * np.outer(a, a) / 64.0
    C64 = np.cos(ang) / 8.0
    S64 = np.sin(ang) / 8.0
    z = np.zeros((64, 64))
    c["c2"] = np.block([[C64, z], [z, C64]]).astype(np.float32)
    c["s2n"] = np.block([[-S64, z], [z, -S64]]).astype(np.float32)
    q = np.arange(128)[:, None, None]
    j = np.arange(J)[None, :, None]
    k1 = np.arange(128)[None, None, :]
    m = (k1 * (J * q + j)) % S
    th = 2 * np.pi * m / S
    sc = 1.0 / np.sqrt(128.0)
    c["tc"] = (np.cos(th) * sc).reshape(128, J * 128).astype(np.float32).astype(bf)
    c["ts"] = (np.sin(th) * sc).reshape(128, J * 128).astype(np.float32).astype(bf)
    c["tsn"] = (-np.sin(th) * sc).reshape(128, J * 128).astype(np.float32).astype(bf)
    jj = np.arange(J)[:, None]
    k2 = np.arange(J)[None, :]
    th3 = 2 * np.pi * jj * k2 / J
    cs3 = np.zeros((J, 2, J))
    cs3[:, 0, :] = np.cos(th3) / np.sqrt(J)
    cs3[:, 1, :] = np.sin(th3) / np.sqrt(J)
    c["cs3"] = cs3.reshape(2 * J, J).astype(np.float32).astype(bf)
    ic = np.zeros((128, 2, 2, 8), dtype=np.float32)
    for mch in range(2):
        for half in range(2):
            w = (2, 4, 8, 16)[mch * 2 + half]
            for t in range(8):
                lo = max(t - w // 2, 0)
                hi = min(t + w // 2, S)
                ic[half * 64:(half + 1) * 64, mch, 0, t] = 1.0 / (hi - lo)
                tt = S - 8 + t
                lo = max(tt - w // 2, 0)
                hi = min(tt + w // 2, S)
                ic[half * 64:(half + 1) * 64, mch, 1, t] = 1.0 / (hi - lo)
    c["icnt"] = ic.reshape(128, 32)
    return c


CONST_SPECS = lambda J: {
    "ident": ([128, 128], BF16), "ones": ([128, 128], BF16),
    "c2": ([128, 128], F32), "s2n": ([128, 128], F32),
    "tc": ([128, J * 128], BF16), "ts": ([128, J * 128], BF16), "tsn": ([128, J * 128], BF16),
    "cs3": ([2 * J, J], BF16), "icnt": ([128, 32], F32),
}

PARAM_SPECS = lambda L: {
    "pre_mix_gain": [L, D], "post_mix_gain": [L, D], "pre_ffn_gain": [L, D], "post_ffn_gain": [L, D],
    "w_in": [L, D, PW], "conv_w": [L, 3, 256], "pool_w": [L, 4, 64, 64], "pool_scale": [L, 256],
    "fourier_w": [L, 4, 64, 64], "spatial_w": [L, 4, 128, 128], "spatial_b": [L, 4, 128],
    "group_norm_gain": [L, D], "w_out": [L, D, D], "w_gate": [L, D, DFF], "w_up": [L, D, DFF],
    "w_down": [L, DFF, D],
}


CFG = dict(xt=2, hb=1, cv=1, pl=1, yfm=1, gn=1, gm=1, ntf=1, cpools={0: [0, 1], 1: [2], 2: [3], 3: [0, 1]})


def build(S=8192, L=2, dbg=None):
    J = S // 128
    NB = S // 512
    J2 = 2 * J
    nc = bass.Bass("TRN2", target_bir_lowering=False)
    stack = ExitStack()
    with stack:
        dram = {}
        dram["x"] = nc.dram_tensor("x", [S, D], F32, kind="ExternalInput").ap()
        for k, shp in PARAM_SPECS(L).items():
            dram[k] = nc.dram_tensor(k, shp, F32, kind="ExternalInput").ap()
        for k, (shp, dt_) in CONST_SPECS(J).items():
            dram[k] = nc.dram_tensor(k, shp, dt_, kind="ExternalInput").ap()
        out_d = nc.dram_tensor("out", [S, D], F32, kind="ExternalOutput").ap()
        xmid = [nc.dram_tensor(f"xmid{l}", [S, D], F32, kind="Internal").ap() for l in range(max(L - 1, 1))]
        ysc = nc.dram_tensor("ysc", [NB, 128, 8, 512], BF16, kind="Internal").ap()
        wgs = nc.dram_tensor("wgs", [L, NDC, 128, 1024], BF16, kind="Internal").ap()
        wus = nc.dram_tensor("wus", [L, NDC, 128, 1024], BF16, kind="Internal").ap()
        fsc = nc.dram_tensor("fsc", [128, 2, S], BF16, kind="Internal").ap()
        wds = nc.dram_tensor("wds", [L, NDC, 128, 1024], BF16, kind="Internal").ap()
        wos = nc.dram_tensor("wos", [L, 8, 128, 1024], BF16, kind="Internal").ap()
        dbg_out = {}
        if dbg:
            for k, shp, dt_ in dbg:
                dbg_out[k] = nc.dram_tensor(k, shp, dt_, kind="ExternalOutput").ap()

        sc = Sched(nc, stack)

        def sb(name, shape, dt_):
            return stack.enter_context(nc.sbuf_tensor("s_" + name, shape, dt_))

        ident = sb("ident", [128, 128], BF16); b_ident = Buf("ident")
        ones = sb("ones", [128, 128], BF16); b_ones = Buf("ones")
        epsc = sb("epsc", [128, 1], F32); b_eps = Buf("eps")
        cw = sb("cw", [128, L, 3, 2], F32); b_cw = Buf("cw")
        psc = sb("psc", [128, L, 2], F32); b_psc = Buf("psc")
        gng = sb("gng", [128, L, 8], F32); b_gng = Buf("gng")
        bsp = sb("bsp", [128, L, 4], F32); b_bsp = Buf("bsp")
        icnt = sb("icnt", [128, 2, 2, 8], F32); b_icnt = Buf("icnt")
        ggm = sb("ggm", [128, 256], F32); b_ggm = Buf("ggm")

        psum = [stack.enter_context(nc.psum_tensor(f"ps{i}", [128, 1024], F32)) for i in range(4)]
        b_ps = [[Buf(f"ps{i}a"), Buf(f"ps{i}b")] for i in range(4)]
        ps_rr = [0]

        ps_rr2 = {0: 0, 1: 0, "s0": 0, "s1": 0}

        class _Half:
            def __init__(self, t, off):
                self.t = t
                self.off = off

            def __getitem__(self, key):
                p, c = key
                assert c.start is not None and c.stop is not None and c.stop <= 512
                return self.t[p, c.start + self.off:c.stop + self.off]

        pools = {"cur": {0: [0, 1], 1: [2, 3]}}

        def set_pools(d):
            pools["cur"] = d
            for k in list(ps_rr2.keys()):
                ps_rr2[k] = 0

        def next_ps(pool=0):
            lst = pools["cur"][pool]
            k = ps_rr2.get(pool, 0)
            ps_rr2[pool] = k + 1
            i = lst[k % len(lst)]
            return psum[i], b_ps[i]

        def next_ps1(pool=0):
            lst = pools["cur"][pool]
            key = "s%d" % pool
            k = ps_rr2.get(key, 0)
            ps_rr2[key] = k + 1
            k = k % (2 * len(lst))
            pi = lst[k // 2]
            return _Half(psum[pi], (k % 2) * 512), [b_ps[pi][k % 2]]

        sc.dma(ident[:, :], dram["ident"], writes=[b_ident])
        sc.dma(ones[:, :], dram["ones"], writes=[b_ones])
        sc.dma(icnt[:, :, :, :].rearrange("p a b c -> p (a b c)"), dram["icnt"], writes=[b_icnt])
        sc.op("pool", lambda e: e.memset(epsc[:, :], EPS), writes=[b_eps])
        sc.dma(cw[:, :, :, :], dram["conv_w"].rearrange("l t (m p) -> p l t m", p=128), writes=[b_cw],
               allow_slow_non_contiguous=True)
        sc.dma(psc[:, :, :], dram["pool_scale"].rearrange("l (m p) -> p l m", p=128), writes=[b_psc],
               allow_slow_non_contiguous=True)
        sc.dma(gng[:, :, :], dram["group_norm_gain"].rearrange("l (k p) -> p l k", p=128), writes=[b_gng],
               allow_slow_non_contiguous=True)
        sc.dma(bsp[:, :, :], dram["spatial_b"].rearrange("l h p -> p l h"), writes=[b_bsp],
               allow_slow_non_contiguous=True)

        def convert_gate_up(l, stg, b_stg):
            n = 0
            for src, dst in ((dram["w_gate"], wgs), (dram["w_up"], wus)):
                for c0 in range(NDC):
                    s_ = n % len(stg)
                    n += 1
                    sc.dma(stg[s_][:, :].rearrange("p (k n) -> p k n", k=8),
                           src[l][:, c0 * 128:(c0 + 1) * 128].rearrange("(k p) n -> p k n", p=128),
                           writes=[b_stg[s_]], q="pool")
                    sc.dma(dst[l, c0], stg[s_][:, :], reads=[b_stg[s_]], q="pool")

        def convert_down_out(l, stg, b_stg):
            n = 0
            for src, dst, nch in ((dram["w_out"], wos, 8), (dram["w_down"], wds, NDC)):
                for c0 in range(nch):
                    s_ = n % len(stg)
                    n += 1
                    sc.dma(stg[s_][:, :], src[l][c0 * 128:(c0 + 1) * 128, :], writes=[b_stg[s_]], q="pool")
                    sc.dma(dst[l, c0], stg[s_][:, :], reads=[b_stg[s_]], q="pool")

        def rstd_small(out_ap, in_ap, n, b_in, b_out, b_tmp, tmp_ap, lnexp=True):
            if lnexp:
                sc.op("act", lambda e: e.activation(out=tmp_ap, in_=in_ap, func=AF.Ln, scale=1.0 / n, bias=epsc[:, 0:1]),
                      reads=[b_in, b_eps], writes=[b_tmp])
                sc.op("act", lambda e: e.activation(out=out_ap, in_=tmp_ap, func=AF.Exp, scale=-0.5), reads=[b_tmp], writes=[b_out])
            else:
                sc.op("act", lambda e: e.activation(out=tmp_ap, in_=in_ap, func=AF.Sqrt, scale=1.0 / n, bias=epsc[:, 0:1]),
                      reads=[b_in, b_eps], writes=[b_tmp])
                sc.op("dve", lambda e: e.reciprocal(out=out_ap, in_=tmp_ap), reads=[b_tmp], writes=[b_out])

        def layer(l):
            x_in = dram["x"] if l == 0 else xmid[l - 1]
            x_out = out_d if l == L - 1 else xmid[l]
            sc.dma(ggm[:, :], dram["group_norm_gain"][l:l + 1, 768:1024].partition_broadcast(128), writes=[b_ggm])
            fstack = ExitStack()
            fT = fstack.enter_context(nc.sbuf_tensor(f"fT_{l}", [128, 2, S], BF16)); b_fT = Buf(f"fT{l}")
            fTflat = fT
            Dh = fstack.enter_context(nc.sbuf_tensor(f"Dh_{l}", [128, 2, 256], BF16)); b_Dh = Buf(f"Dh{l}")

            set_pools({0: [0, 1], 1: [2, 3]})
            with ExitStack() as pa:
                def sa(name, shape, dt_):
                    return pa.enter_context(nc.sbuf_tensor(f"a_{name}_{l}", shape, dt_))
                w_in_sb = sa("w_in", [128, 8, PW], BF16); b_win = Buf(f"win{l}")
                gA = sa("gA", [128, D], F32); b_gA = Buf(f"gA{l}")
                sc.dma(gA[:, :], dram["pre_mix_gain"][l:l + 1, :].partition_broadcast(128), writes=[b_gA])
                sc.dma(w_in_sb[:, :, :], dram["w_in"][l].rearrange("(k p) n -> p k n", p=128), writes=[b_win], q="pool")
                pwbd = sa("pwbd", [128, 2, 128], BF16); b_pwbd = Buf(f"pwbd{l}")
                sc.op("pool", lambda e: e.memset(pwbd[:, :, :], 0.0), writes=[b_pwbd])
                for g in range(4):
                    h0 = (g % 2) * 64
                    sc.dma(pwbd[h0:h0 + 64, g // 2, h0:h0 + 64], dram["pool_w"][l, g], writes=[b_pwbd], q="pool")
                wsn = sa("wsn", [128, 4, 128], BF16); b_wsn = Buf(f"wsn{l}")
                sc.dma(wsn[:, :, :], dram["spatial_w"][l].rearrange("h p q -> p h q"), writes=[b_wsn], q="pool")
                wsT = sa("wsT", [128, 4, 128], BF16); b_wsT = Buf(f"wsT{l}")
                pt_, bp_ = next_ps()

                def f_wsT(e, pt_=pt_):
                    for h in range(4):
                        ins = e.matmul(pt_[:, h * 128:(h + 1) * 128], lhsT=wsn[:, h, :], rhs=ident[:, :], start=True, stop=True)
                    return ins
                sc.op("pe", f_wsT, reads=[b_wsn, b_ident], writes=bp_[:1])
                sc.op("dve", lambda e, pt_=pt_: e.tensor_copy(out=wsT[:, :, :].rearrange("p h q -> p (h q)"), in_=pt_[:, 0:512]),
                      reads=bp_[:1], writes=[b_wsT])

                c2 = sa("c2", [128, 128], F32); b_c2 = Buf(f"c2{l}")
                s2n = sa("s2n", [128, 128], F32); b_s2n = Buf(f"s2n{l}")
                fw2 = sa("fw2", [128, 2, 128], F32); b_fw2 = Buf(f"fw2{l}")
                sc.dma(c2[:, :], dram["c2"], writes=[b_c2])
                sc.dma(s2n[:, :], dram["s2n"], writes=[b_s2n])
                sc.op("pool", lambda e: e.memset(fw2[:, :, :], 0.0), writes=[b_fw2])
                for h in range(4):
                    h0 = (h % 2) * 64
                    sc.dma(fw2[h0:h0 + 64, h // 2, h0:h0 + 64], dram["fourier_w"][l, h], writes=[b_fw2])
                for half in range(2):
                    pt, bp = next_ps1(1)

                    def f_D(e, pt=pt, half=half):
                        e.matmul(pt[:, 0:128], lhsT=c2[:, :], rhs=fw2[:, half, :], start=True, stop=True)
                        return e.matmul(pt[:, 128:256], lhsT=s2n[:, :], rhs=fw2[:, half, :], start=True, stop=True)
                    sc.op("pe", f_D, reads=[b_c2, b_s2n, b_fw2], writes=bp[:1])
                    sc.op("dve", lambda e, pt=pt, half=half: e.tensor_copy(out=Dh[:, half, :], in_=pt[:, 0:256]), reads=bp[:1], writes=[b_Dh])
                stg = [sa(f"stg{i}", [128, 1024], BF16) for i in range(2)]
                b_stg = [Buf(f"stgA{l}_{i}") for i in range(2)]
                convert_gate_up(l, stg, b_stg)

                def ring(name, shape, dt_, n):
                    return ([sa(f"{name}{k}", shape, dt_) for k in range(n)], [Buf(f"{name}{l}_{k}") for k in range(n)])

                def pick(r, idx):
                    return r[0][idx % len(r[0])], r[1][idx % len(r[1])]
                NX = CFG["xt"]
                xt, b_xt = ring("xt", [128, 4, D], F32, NX)
                ssx_r = ring("ssx", [128, 4], F32, 2); ssx2_r = ring("ssx2", [128, 4], F32, 2); rsx_r = ring("rsx", [128, 4], F32, 2)
                hb_r = ring("hb", [128, 4, D], BF16, CFG["hb"])
                hTx, b_hTx = ring("hTx", [128, 8, 528], BF16, 3)
                z_r = ring("z", [128, 528], F32, CFG["cv"]); cz_r = ring("cz", [128, 528], F32, CFG["cv"])
                bsb_r = ring("bsb", [128, 512], F32, CFG["cv"])
                acc_r = ring("acc", [128, 512], F32, CFG["cv"]); acc2_r = ring("acc2", [128, 512], F32, CFG["cv"])
                yfm_r = [ring(f"yfm{k}", [128, 512], F32, CFG["yfm"]) for k in range(4)]
                p_r = ring("p", [128, 528], F32, CFG["pl"]); Ra_r = ring("Ra", [128, 528], F32, CFG["pl"])
                Rb_r = ring("Rb", [128, 528], F32, CFG["pl"]); Rc_r = ring("Rc", [128, 528], F32, CFG["pl"])
                etmp = sa("etmp", [128, 8], F32); b_etmp = Buf("etmp")
                dpl_r = [ring(f"dpl{k}", [128, 512], BF16, CFG["pl"]) for k in range(2)]
                sq_r = [ring(f"sq{k}", [128, 512], BF16, CFG["gn"]) for k in range(2)]
                sd_r = ring("sd", [128, 512], F32, CFG["gn"]); rs_r = ring("rs", [128, 512], F32, CFG["gn"])
                yst, b_yst = ring("yst", [128, 6, 512], BF16, 2)
                uv_r = ring("uv", [128, 4, 512], F32, CFG["gm"]); sqv_r = ring("sqv", [128, 4, 256], F32, CFG["gm"])
                vst_r = [ring(f"vst{k}", [128, 16], F32, 2) for k in range(6)]
                vh_r = ring("vh", [128, 4, 256], BF16, CFG["gm"]); yg_r = ring("yg", [128, 4, 256], F32, CFG["gm"])
                ygn_r = ring("ygn", [128, 4, 256], BF16, CFG["gm"])
                gst_r = [ring(f"gst{k}", [128, 4], F32, 2) for k in range(3)]

                import os as _os2
                if _os2.environ.get("KDEBUG"):
                    print("KDEBUG sbuf remaining phase A", nc.sbuf_bytes_remaining)

                b_fTw = [Buf(f"fTw{l}_{i}") for i in range(NB)]

                def load_x(i):
                    s_ = i % NX
                    sc.dma(xt[s_][:, :, :], x_in[i * 512:(i + 1) * 512, :].rearrange("(c p) d -> p c d", p=128),
                           writes=[b_xt[s_]])

                def stageN(i):
                    s_ = i % NX
                    xs = xt[s_]
                    hs = i % 3
                    ssx, b_ssx = pick(ssx_r, i); ssx2, b_ssx2 = pick(ssx2_r, i); rsx, b_rsx = pick(rsx_r, i)
                    hb, b_hb = pick(hb_r, i)
                    for c in range(4):
                        sc.op("act", lambda e, c=c: e.activation(out=hb[:, c, :], in_=xs[:, c, :], func=AF.Square,
                                                                 accum_out=ssx[:, c:c + 1]),
                              reads=[b_xt[s_]], writes=[b_hb, b_ssx])
                    rstd_small(rsx[:, :], ssx[:, :], D, b_ssx, b_rsx, b_ssx2, ssx2[:, :])
                    for c in range(4):
                        sc.op("dve", lambda e, c=c: e.scalar_tensor_tensor(out=hb[:, c, :], in0=xs[:, c, :], scalar=rsx[:, c:c + 1],
                                                                         in1=gA[:, :], op0=ALU.mult, op1=ALU.mult),
                              reads=[b_xt[s_], b_rsx, b_gA], writes=[b_hb])
                    if i == 0:
                        sc.op("pool", lambda e: e.memset(hTx[hs][:, :, 0:8], 0.0), writes=[b_hTx[hs]])
                    if i == NB - 1:
                        sc.op("pool", lambda e: e.memset(hTx[hs][:, :, 520:528], 0.0), writes=[b_hTx[hs]])
                    for c in range(4):
                        pt, bp = next_ps()

                        def f_tr(e, c=c, pt=pt):
                            for k in range(8):
                                ins = e.matmul(pt[:, k * 128:(k + 1) * 128], lhsT=hb[:, c, k * 128:(k + 1) * 128], rhs=ident[:, :],
                                               start=True, stop=True)
                            return ins
                        sc.op("pe", f_tr, reads=[b_hb, b_ident], writes=bp)
                        sc.op("act", lambda e, c=c, pt=pt: e.activation(out=hTx[hs][:, :, 8 + c * 128:8 + (c + 1) * 128],
                                                                      in_=pt[:, :].rearrange("p (k n) -> p k n", k=8), func=AF.Copy),
                              reads=bp, writes=[b_hTx[hs]])
                    if i >= 1:
                        hp = (i - 1) % 3
                        sc.op("pool", lambda e: e.tensor_copy(out=hTx[hp][:, :, 520:528], in_=hTx[hs][:, :, 8:16]),
                              reads=[b_hTx[hs]], writes=[b_hTx[hp]])
                    if i + 1 < NB:
                        hn = (i + 1) % 3
                        sc.op("pool", lambda e: e.tensor_copy(out=hTx[hn][:, :, 0:8], in_=hTx[hs][:, :, 512:520]),
                              reads=[b_hTx[hs]], writes=[b_hTx[hn]])

                def proj_fm(i, m, halo):
                    hs = i % 3
                    pt, bp = next_ps() if halo else next_ps1()
                    if halo:
                        def f(e, pt=pt):
                            for k in range(8):
                                e.matmul(pt[:, 0:512], lhsT=w_in_sb[:, k, m * 128:(m + 1) * 128], rhs=hTx[hs][:, k, 0:512],
                                         start=(k == 0), stop=(k == 7))
                            for k in range(8):
                                ins = e.matmul(pt[:, 512:528], lhsT=w_in_sb[:, k, m * 128:(m + 1) * 128], rhs=hTx[hs][:, k, 512:528],
                                               start=(k == 0), stop=(k == 7))
                            return ins
                        sc.op("pe", f, reads=[b_win, b_hTx[hs]], writes=bp)
                    else:
                        def f(e, pt=pt):
                            for k in range(8):
                                ins = e.matmul(pt[:, 0:512], lhsT=w_in_sb[:, k, m * 128:(m + 1) * 128], rhs=hTx[hs][:, k, 8:520],
                                               start=(k == 0), stop=(k == 7))
                            return ins
                        sc.op("pe", f, reads=[b_win, b_hTx[hs]], writes=bp[:1])
                    return pt, bp

                def group_norm_fm(i, gidx, ys):
                    ri = 2 * i + gidx
                    sqs = [pick(sq_r[mm], ri) for mm in range(2)]
                    sd, b_sd = pick(sd_r, ri); rs, b_rs = pick(rs_r, ri)
                    yf = [pick(yfm_r[2 * gidx + mm], i) for mm in range(2)]
                    for mm in range(2):
                        sc.op("act", lambda e, mm=mm: e.activation(out=sqs[mm][0][:, :], in_=yf[mm][0][:, :], func=AF.Square),
                              reads=[yf[mm][1]], writes=[sqs[mm][1]])
                    pt, bp = next_ps1(1)

                    def f(e, pt=pt):
                        for mm in range(2):
                            ins = e.matmul(pt[:, 0:512], lhsT=ones[:, :], rhs=sqs[mm][0][:, :], start=(mm == 0), stop=(mm == 1))
                        return ins
                    sc.op("pe", f, reads=[b_ones, sqs[0][1], sqs[1][1]], writes=bp[:1])
                    sc.op("act", lambda e, pt=pt: e.activation(out=sd[:, :], in_=pt[:, 0:512], func=AF.Ln, scale=1.0 / 256,
                                                             bias=epsc[:, 0:1]), reads=bp[:1] + [b_eps], writes=[b_sd])
                    sc.op("act", lambda e: e.activation(out=rs[:, :], in_=sd[:, :], func=AF.Exp, scale=-0.5), reads=[b_sd], writes=[b_rs])
                    for mm in range(2):
                        kc = 2 * gidx + mm
                        sc.op("dve", lambda e, mm=mm, kc=kc: e.scalar_tensor_tensor(out=yst[ys][:, kc, :], in0=yf[mm][0][:, :],
                                                                                 scalar=gng[:, l, kc:kc + 1], in1=rs[:, :],
                                                                                 op0=ALU.mult, op1=ALU.mult),
                              reads=[yf[mm][1], b_gng, b_rs], writes=[b_yst[ys]])

                def conv_chunk(i, mm):
                    ri = 2 * i + mm
                    z_sb, b_z = pick(z_r, ri); cz, b_cz = pick(cz_r, ri); acc, b_acc = pick(acc_r, ri); acc2, b_acc2 = pick(acc2_r, ri)
                    yf, b_yf = pick(yfm_r[mm], i)
                    pz, bz = proj_fm(i, 4 + mm, True)
                    sc.op("act", lambda e: e.activation(out=z_sb[:, :], in_=pz[:, 0:528], func=AF.Copy), reads=bz, writes=[b_z])
                    pc, bc = proj_fm(i, 2 + mm, True)
                    sc.op("dve", lambda e: e.tensor_tensor(out=cz[:, :], in0=pc[:, 0:528], in1=z_sb[:, :], op=ALU.mult),
                          reads=bc + [b_z], writes=[b_cz])
                    bsb, b_bsb = pick(bsb_r, ri)
                    pb, bb = proj_fm(i, mm, False)
                    sc.op("act", lambda e: e.activation(out=bsb[:, :], in_=pb[:, 0:512], func=AF.Copy), reads=bb[:1], writes=[b_bsb])
                    sc.op("act", lambda e: e.activation(out=acc[:, :], in_=cz[:, 8:520], func=AF.Copy, scale=cw[:, l, 1, mm:mm + 1]),
                          reads=[b_cz, b_cw], writes=[b_acc])
                    sc.op("dve", lambda e: e.scalar_tensor_tensor(out=acc2[:, :], in0=cz[:, 7:519], scalar=cw[:, l, 0, mm:mm + 1],
                                                                  in1=acc[:, :], op0=ALU.mult, op1=ALU.add),
                          reads=[b_cz, b_cw, b_acc], writes=[b_acc2])
                    sc.op("dve", lambda e: e.scalar_tensor_tensor(out=acc[:, :], in0=cz[:, 9:521], scalar=cw[:, l, 2, mm:mm + 1],
                                                                  in1=acc2[:, :], op0=ALU.mult, op1=ALU.add),
                          reads=[b_cz, b_cw, b_acc2], writes=[b_acc])
                    sc.op("dve", lambda e: e.tensor_tensor(out=yf[:, :], in0=bsb[:, :], in1=acc[:, :], op=ALU.mult),
                          reads=[b_bsb, b_acc], writes=[b_yf])

                def pool_chunk(i, mm):
                    ri = 2 * i + mm
                    p_sb, b_p = pick(p_r, ri); Ra, b_Ra = pick(Ra_r, ri); Rb, b_Rb = pick(Rb_r, ri); Rc, b_Rc = pick(Rc_r, ri)
                    Rd, b_Rd = Ra, b_Ra
                    dpl, b_dpl = pick(dpl_r[mm], i)
                    pp, bpp = proj_fm(i, 6 + mm, True)
                    sc.op("act", lambda e: e.activation(out=p_sb[:, :], in_=pp[:, 0:528], func=AF.Copy), reads=bpp, writes=[b_p])
                    e0 = "pool" if mm == 0 else "dve"
                    sc.op(e0, lambda e: e.tensor_tensor(out=Ra[:, 0:527], in0=p_sb[:, 0:527], in1=p_sb[:, 1:528], op=ALU.add),
                          reads=[b_p], writes=[b_Ra])
                    sc.op(e0, lambda e: e.tensor_tensor(out=Rb[:, 0:525], in0=Ra[:, 0:525], in1=Ra[:, 2:527], op=ALU.add),
                          reads=[b_Ra], writes=[b_Rb])
                    if mm == 0:
                        wins = ((0, 64, Ra, b_Ra, 2), (64, 128, Rb, b_Rb, 4))
                    else:
                        sc.op("dve", lambda e: e.tensor_tensor(out=Rc[:, 0:521], in0=Rb[:, 0:521], in1=Rb[:, 4:525], op=ALU.add),
                              reads=[b_Rb], writes=[b_Rc])
                        sc.op("dve", lambda e: e.tensor_tensor(out=Rd[:, 0:513], in0=Rc[:, 0:513], in1=Rc[:, 8:521], op=ALU.add),
                              reads=[b_Rc], writes=[b_Rd])
                        wins = ((0, 64, Rc, b_Rc, 8), (64, 128, Rd, b_Rd, 16))
                    for (p0, p1, R, bR, w) in wins:
                        o = 8 - w // 2
                        sc.op("dve", lambda e, p0=p0, p1=p1, R=R, w=w, o=o: e.scalar_tensor_tensor(
                            out=dpl[p0:p1, :], in0=R[p0:p1, o:o + 512], scalar=1.0 / w, in1=p_sb[p0:p1, 8:520],
                            op0=ALU.mult, op1=ALU.subtract), reads=[bR, b_p], writes=[b_dpl])
                        for side, blk in ((0, 0), (1, NB - 1)):
                            if i != blk:
                                continue
                            c0 = 0 if side == 0 else 504
                            sc.op("dve", lambda e, p0=p0, p1=p1, R=R, o=o, c0=c0, side=side: e.tensor_tensor(
                                out=etmp[p0:p1, :], in0=R[p0:p1, o + c0:o + c0 + 8], in1=icnt[p0:p1, mm, side, :], op=ALU.mult),
                                reads=[bR, b_icnt], writes=[b_etmp])
                            sc.op("dve", lambda e, p0=p0, p1=p1, c0=c0: e.tensor_tensor(
                                out=dpl[p0:p1, c0:c0 + 8], in0=etmp[p0:p1, :], in1=p_sb[p0:p1, 8 + c0:16 + c0], op=ALU.subtract),
                                reads=[b_etmp, b_p], writes=[b_dpl])

                def stageP(i):
                    hs = i % 3
                    ys = i % 2
                    conv_chunk(i, 0)
                    conv_chunk(i, 1)
                    pool_chunk(i, 0)
                    pool_chunk(i, 1)
                    dp = [pick(dpl_r[mm], i) for mm in range(2)]
                    yfp = [pick(yfm_r[2 + mm], i) for mm in range(2)]
                    pt, bp = next_ps(1)

                    def f_pw(e, pt=pt):
                        for mm in range(2):
                            ins = e.matmul(pt[:, mm * 512:(mm + 1) * 512], lhsT=pwbd[:, mm, :], rhs=dp[mm][0][:, :], start=True, stop=True)
                        return ins
                    sc.op("pe", f_pw, reads=[b_pwbd, dp[0][1], dp[1][1]], writes=bp)
                    for mm in range(2):
                        sc.op("act", lambda e, mm=mm, pt=pt: e.activation(out=yfp[mm][0][:, :], in_=pt[:, mm * 512:(mm + 1) * 512],
                                                                        func=AF.Copy, scale=psc[:, l, mm:mm + 1]),
                              reads=[bp[mm], b_psc], writes=[yfp[mm][1]])
                    for mm in range(2):
                        pf, bf_ = proj_fm(i, 8 + mm, False)
                        sc.op("act", lambda e, mm=mm, pf=pf: e.activation(out=fT[:, mm, i * 512:(i + 1) * 512], in_=pf[:, 0:512], func=AF.Copy),
                              reads=bf_[:1], writes=[b_fTw[i]])
                    uv, b_uv = pick(uv_r, i); sqv, b_sqv = pick(sqv_r, i); vh, b_vh = pick(vh_r, i)
                    yg, b_yg = pick(yg_r, i); ygn, b_ygn = pick(ygn_r, i)
                    vstp = [pick(vst_r[k], i) for k in range(6)]
                    vst = [t[0] for t in vstp]; b_vst = [t[1] for t in vstp]
                    gstp = [pick(gst_r[k], i) for k in range(3)]
                    gst = [t[0] for t in gstp]; b_gst = [t[1] for t in gstp]
                    for c in range(4):
                        pt, bp = next_ps1()

                        def f_uv(e, c=c, pt=pt):
                            for k in range(8):
                                ins = e.matmul(pt[:, 0:512], lhsT=hTx[hs][:, k, 8 + c * 128:8 + (c + 1) * 128], rhs=w_in_sb[:, k, 1280:1792],
                                               start=(k == 0), stop=(k == 7))
                            return ins
                        sc.op("pe", f_uv, reads=[b_win, b_hTx[hs]], writes=bp[:1])
                        sc.op("act", lambda e, c=c, pt=pt: e.activation(out=uv[:, c, :], in_=pt[:, 0:512], func=AF.Copy),
                              reads=bp[:1], writes=[b_uv])
                    group_norm_fm(i, 0, ys)
                    group_norm_fm(i, 1, ys)
                    v4 = uv[:, :, 256:512].rearrange("p n (h c) -> p n h c", h=4)
                    nh = lambda t: t[:, :].rearrange("p (n h) -> p n h", n=4)
                    sc.op("dve", lambda e: e.tensor_reduce(out=nh(vst[0]), in_=v4, axis=AX.X, op=ALU.add),
                          reads=[b_uv], writes=[b_vst[0]])
                    sc.op("act", lambda e: e.activation(out=sqv[:, :, :], in_=uv[:, :, 256:512], func=AF.Square), reads=[b_uv], writes=[b_sqv])
                    sc.op("dve", lambda e: e.tensor_reduce(out=nh(vst[1]), in_=sqv[:, :, :].rearrange("p n (h c) -> p n h c", h=4), axis=AX.X, op=ALU.add),
                          reads=[b_sqv], writes=[b_vst[1]])
                    sc.op("pool", lambda e: e.tensor_scalar(out=vst[2][:, :], in0=vst[0][:, :], scalar1=1.0 / 64, scalar2=None, op0=ALU.mult),
                          reads=[b_vst[0]], writes=[b_vst[2]])
                    sc.op("pool", lambda e: e.tensor_tensor(out=vst[3][:, :], in0=vst[2][:, :], in1=vst[2][:, :], op=ALU.mult),
                          reads=[b_vst[2]], writes=[b_vst[3]])
                    sc.op("dve", lambda e: e.scalar_tensor_tensor(out=vst[4][:, :], in0=vst[1][:, :], scalar=1.0 / 64, in1=vst[3][:, :],
                                                                  op0=ALU.mult, op1=ALU.subtract), reads=[b_vst[1], b_vst[3]], writes=[b_vst[4]])
                    sc.op("act", lambda e: e.activation(out=vst[3][:, :], in_=vst[4][:, :], func=AF.Ln, scale=1.0, bias=epsc[:, 0:1]),
                          reads=[b_vst[4], b_eps], writes=[b_vst[3]])
                    sc.op("act", lambda e: e.activation(out=vst[5][:, :], in_=vst[3][:, :], func=AF.Exp, scale=-0.5), reads=[b_vst[3]], writes=[b_vst[5]])
                    mean_b = nh(vst[2]).unsqueeze(3).to_broadcast([128, 4, 4, 64])
                    rstd_b = nh(vst[5]).unsqueeze(3).to_broadcast([128, 4, 4, 64])
                    sqv4 = sqv[:, :, :].rearrange("p n (h c) -> p n h c", h=4)
                    sc.op("dve", lambda e: e.tensor_tensor(out=sqv4, in0=v4, in1=mean_b, op=ALU.subtract),
                          reads=[b_uv, b_vst[2]], writes=[b_sqv])
                    sc.op("dve", lambda e: e.tensor_tensor(out=vh[:, :, :].rearrange("p n (h c) -> p n h c", h=4), in0=sqv4, in1=rstd_b, op=ALU.mult),
                          reads=[b_sqv, b_vst[5]], writes=[b_vh])
                    pt, bp = next_ps(1)

                    def f_sp(e, pt=pt):
                        for h in range(4):
                            ins = e.matmul(pt[:, h * 256:(h + 1) * 256], lhsT=wsT[:, h, :], rhs=vh[:, :, h * 64:(h + 1) * 64], start=True, stop=True)
                        return ins
                    sc.op("pe", f_sp, reads=[b_wsT, b_vh], writes=bp)
                    for h in range(4):
                        sc.op("dve", lambda e, h=h, pt=pt: e.scalar_tensor_tensor(
                            out=yg[:, :, h * 64:(h + 1) * 64], in0=pt[:, h * 256:(h + 1) * 256].rearrange("p (n c) -> p n c", n=4),
                            scalar=bsp[:, l, h:h + 1], in1=uv[:, :, h * 64:(h + 1) * 64], op0=ALU.add, op1=ALU.mult),
                            reads=[bp[h // 2], b_bsp, b_uv], writes=[b_yg])
                    sc.op("act", lambda e: e.activation(out=sqv[:, :, :], in_=yg[:, :, :], func=AF.Square), reads=[b_yg], writes=[b_sqv])
                    sc.op("dve", lambda e: e.tensor_reduce(out=gst[0][:, :], in_=sqv[:, :, :], axis=AX.X, op=ALU.add), reads=[b_sqv], writes=[b_gst[0]])
                    rstd_small(gst[2][:, :], gst[0][:, :], 256, b_gst[0], b_gst[2], b_gst[1], gst[1][:, :])
                    for n in range(4):
                        sc.op("dve", lambda e, n=n: e.scalar_tensor_tensor(out=ygn[:, n, :], in0=yg[:, n, :], scalar=gst[2][:, n:n + 1], in1=ggm[:, :],
                                                                          op0=ALU.mult, op1=ALU.mult), reads=[b_yg, b_gst[2], b_ggm], writes=[b_ygn])
                    pt, bp = next_ps(1)

                    def f_gt(e, pt=pt):
                        for mm in range(2):
                            for n in range(4):
                                ins = e.matmul(pt[:, mm * 512 + n * 128: mm * 512 + (n + 1) * 128], lhsT=ygn[:, n, mm * 128:(mm + 1) * 128],
                                               rhs=ident[:, :], start=True, stop=True)
                        return ins
                    sc.op("pe", f_gt, reads=[b_ygn, b_ident], writes=bp)
                    sc.op("act", lambda e, pt=pt: e.activation(out=yst[ys][:, 4:6, :], in_=pt[:, :].rearrange("p (m t) -> p m t", m=2), func=AF.Copy),
                          reads=bp, writes=[b_yst[ys]])
                    sc.dma(ysc[i, :, 0:6, :], yst[ys][:, :, :], reads=[b_yst[ys]])

                for i0 in range(min(NX, NB)):
                    load_x(i0)
                for i in range(NB + 1):
                    if i < NB:
                        stageN(i)
                        if i + NX < NB:
                            load_x(i + NX)
                    if i >= 1:
                        stageP(i - 1)
                sc.barrier()
            set_pools({0: [0, 1, 2, 3], 1: [0, 1, 2, 3]})
            with ExitStack() as pb_:
                def sbb(name, shape, dt_):
                    return pb_.enter_context(nc.sbuf_tensor(f"b_{name}_{l}", shape, dt_))
                tcs = sbb("tc", [128, J, 128], BF16); b_tc = Buf(f"tc{l}")
                tss = sbb("ts", [128, J, 128], BF16); b_ts = Buf(f"ts{l}")
                tsn = sbb("tsn", [128, J, 128], BF16); b_tsn = Buf(f"tsn{l}")
                cs3 = sbb("cs3", [J2, J], BF16); b_cs3 = Buf(f"cs3{l}")
                G = sbb("G", [128, 2, J, 256], BF16); b_G = [Buf(f"G0{l}"), Buf(f"G1{l}")]
                A = sbb("A", [128, J, 2, 128], BF16); b_A = Buf(f"A{l}")
                Ap_t = [sbb(f"Ap{i}", [J2, 128, 128], BF16) for i in range(2)] if J * 256 < 128 * 128 else None
                stgB = [sbb(f"stgB{i}", [128, 1024], BF16) for i in range(3)]
                b_stgB = [Buf(f"stgB{l}_{i}") for i in range(3)]
                convert_down_out(l, stgB, b_stgB)
                import os as _os4
                if _os4.environ.get("KDEBUG"):
                    print("KDEBUG sbuf remaining phase B", nc.sbuf_bytes_remaining)
                sc.dma(tcs[:, :, :].rearrange("p j k -> p (j k)"), dram["tc"], writes=[b_tc])
                sc.dma(tss[:, :, :].rearrange("p j k -> p (j k)"), dram["ts"], writes=[b_ts])
                sc.dma(tsn[:, :, :].rearrange("p j k -> p (j k)"), dram["tsn"], writes=[b_tsn])
                sc.dma(cs3[:, :], dram["cs3"], writes=[b_cs3])
                for j in range(J):
                    pt, bp = next_ps1()

                    def f_s1(e, pt=pt, j=j):
                        for half in range(2):
                            ins = e.matmul(pt[:, half * 256:(half + 1) * 256], lhsT=fT[:, half, :].rearrange("p (q j) -> p j q", j=J)[:, j, :], rhs=Dh[:, half, :], start=True, stop=True)
                        return ins
                    sc.op("pe", f_s1, reads=[b_fT, b_Dh], writes=bp[:1])
                    eng = "act" if j % 2 == 0 else "dve"
                    if eng == "act":
                        sc.op("act", lambda e, pt=pt, j=j: e.activation(out=G[:, :, j, :], in_=pt[:, 0:512].rearrange("p (h c) -> p h c", h=2), func=AF.Copy),
                              reads=bp[:1], writes=b_G)
                    else:
                        sc.op("dve", lambda e, pt=pt, j=j: e.tensor_copy(out=G[:, :, j, :], in_=pt[:, 0:512].rearrange("p (h c) -> p h c", h=2)),
                              reads=bp[:1], writes=b_G)
                yT4 = fTflat
                for half in range(2):
                    for j0 in range(0, J, 2):
                        pt, bp = next_ps1()

                        def f_s2(e, pt=pt, j0=j0, half=half):
                            for jj in range(2):
                                j = j0 + jj
                                o = jj * 256
                                e.matmul(pt[:, o:o + 128], lhsT=tcs[:, j, :], rhs=G[:, half, j, 0:128], start=True, stop=False)
                                e.matmul(pt[:, o:o + 128], lhsT=tss[:, j, :], rhs=G[:, half, j, 128:256], start=False, stop=True)
                                e.matmul(pt[:, o + 128:o + 256], lhsT=tcs[:, j, :], rhs=G[:, half, j, 128:256], start=True, stop=False)
                                ins = e.matmul(pt[:, o + 128:o + 256], lhsT=tsn[:, j, :], rhs=G[:, half, j, 0:128], start=False, stop=True)
                            return ins
                        sc.op("pe", f_s2, reads=[b_tc, b_ts, b_tsn, b_G[half]], writes=bp[:1])
                        eng = "act" if (j0 // 2) % 2 == 0 else "dve"
                        o_ap = A[:, j0:j0 + 2, :, :]
                        i_ap = lambda pt: pt[:, 0:512].rearrange("p (j r c) -> p j r c", j=2, r=2)
                        if eng == "act":
                            sc.op("act", lambda e, pt=pt, o_ap=o_ap: e.activation(out=o_ap, in_=i_ap(pt), func=AF.Copy), reads=bp[:1], writes=[b_A])
                        else:
                            sc.op("dve", lambda e, pt=pt, o_ap=o_ap: e.tensor_copy(out=o_ap, in_=i_ap(pt)), reads=bp[:1], writes=[b_A])
                    if J * 256 >= 128 * 128:
                        Ap = G[0:J2, half, :, :].rearrange("p j c -> p (j c)")[:, 0:128 * 128].rearrange("p (c k) -> p c k", c=128)
                    else:
                        Ap = Ap_t[half][:, :, :]
                    for c0 in range(0, 128, 8):
                        pt, bp = next_ps()

                        def f_tr(e, pt=pt, c0=c0):
                            for cc in range(8):
                                ins = e.matmul(pt[0:J2, cc * 128:(cc + 1) * 128], lhsT=A[:, :, :, c0 + cc].rearrange("p j r -> p (j r)"), rhs=ident[:, :], start=True, stop=True)
                            return ins
                        sc.op("pe", f_tr, reads=[b_A, b_ident], writes=bp)
                        eng = "act" if (c0 // 8) % 2 == 0 else "dve"
                        o_ap = Ap[:, c0:c0 + 8, :]
                        if eng == "act":
                            sc.op("act", lambda e, pt=pt, o_ap=o_ap: e.activation(out=o_ap, in_=pt[0:J2, :].rearrange("p (c k) -> p c k", c=8), func=AF.Copy),
                                  reads=bp, writes=[b_G[half]])
                        else:
                            sc.op("dve", lambda e, pt=pt, o_ap=o_ap: e.tensor_copy(out=o_ap, in_=pt[0:J2, :].rearrange("p (c k) -> p c k", c=8)),
                                  reads=bp, writes=[b_G[half]])
                    KB = min(128, 512 // J)
                    for k0 in range(0, 128, KB):
                        pt, bp = next_ps1()

                        def f_s3(e, pt=pt, k0=k0, Ap=Ap):
                            for kk in range(KB):
                                ins = e.matmul(pt[:, kk * J:(kk + 1) * J], lhsT=Ap[:, :, k0 + kk], rhs=cs3[:, :], start=True, stop=True)
                            return ins
                        sc.op("pe", f_s3, reads=[b_G[half], b_cs3], writes=bp[:1])
                        eng = "act" if (k0 // KB) % 2 == 0 else "dve"
                        o_ap = yT4[:, half, :].rearrange("p (k2 k1) -> p k2 k1", k1=128)[:, :, k0:k0 + KB]
                        if eng == "act":
                            sc.op("act", lambda e, pt=pt, o_ap=o_ap: e.activation(out=o_ap, in_=pt[:, 0:KB * J].rearrange("p (k a) -> p a k", k=KB), func=AF.Copy),
                                  reads=bp[:1], writes=[b_fT])
                        else:
                            sc.op("dve", lambda e, pt=pt, o_ap=o_ap: e.tensor_copy(out=o_ap, in_=pt[:, 0:KB * J].rearrange("p (k a) -> p a k", k=KB)),
                                  reads=bp[:1], writes=[b_fT])
                if dbg and "dbg_y4" in dbg_out and l == 0:
                    sc.barrier()
                    sc.dma(dbg_out["dbg_y4"], yT4, reads=[b_fT])
                    sc.barrier()
                for i in range(NB):
                    sc.dma(ysc[i, :, 6:8, :], yT4[:, :, i * 512:(i + 1) * 512], reads=[b_fT])
                sc.barrier()
            fstack.close()
            set_pools(CFG["cpools"])
            with ExitStack() as pc_:
                def sbc(name, shape, dt_):
                    return pc_.enter_context(nc.sbuf_tensor(f"c_{name}_{l}", shape, dt_))
                w_out_sb = sbc("w_out", [128, 8, D], BF16); b_wout = Buf(f"wout{l}")
                gC = sbc("gC", [128, 3, D], F32); b_gC = [Buf(f"gC{l}_{i}") for i in range(3)]
                for gi, gk in enumerate(("post_mix_gain", "pre_ffn_gain", "post_ffn_gain")):
                    sc.dma(gC[:, gi, :], dram[gk][l:l + 1, :].partition_broadcast(128), writes=[b_gC[gi]])
                w_dn_sb = sbc("w_dn", [128, NDC, D], BF16); b_wdn = Buf(f"wdn{l}")
                sc.dma(w_out_sb[:, :, :], wos[l].rearrange("k p n -> p k n"), writes=[b_wout])
                b_wdn4 = [Buf(f"wdn{l}_{i}") for i in range(2)]
                sc.dma(w_dn_sb[:, 0:NDC // 2, :], wds[l, 0:NDC // 2].rearrange("k p n -> p k n"), writes=[b_wdn4[0]])
                sc.dma(w_dn_sb[:, NDC // 2:NDC, :], wds[l, NDC // 2:NDC].rearrange("k p n -> p k n"), writes=[b_wdn4[1]])
                NW = 3
                wg_sb = [sbc(f"wg{i}", [128, 8, 128], BF16) for i in range(NW)]; b_wg = [Buf(f"wg{l}_{i}") for i in range(NW)]
                wu_sb = [sbc(f"wu{i}", [128, 8, 128], BF16) for i in range(NW)]; b_wu = [Buf(f"wu{l}_{i}") for i in range(NW)]
                xtc = [sbc(f"xtc{i}", [128, 4, D], F32) for i in range(2)]; b_xtc = [Buf(f"xtC{l}_{i}") for i in range(2)]
                yl = [sbc(f"yl{i}", [128, 8, 512], BF16) for i in range(2)]; b_yl = [Buf(f"yl{l}_{i}") for i in range(2)]
                junkf = sbc("junkf", [128, D], BF16); b_junkc = Buf("junkc")
                sqC = sbc("sqC", [128, 2, 512], BF16); b_sqC = [Buf("sqC0"), Buf("sqC1")]
                sdC = sbc("sdC", [128, 512], F32); b_sdC = Buf("sdC")
                rsC = sbc("rsC", [128, 512], F32); b_rsC = Buf("rsC")
                NTF = CFG["ntf"]
                tctr = [0, 0]
                tmpf = [sbc(f"tmpf{i}", [128, D], F32) for i in range(2 * NTF)]; b_tmpf = [Buf(f"tmpf{i}") for i in range(2 * NTF)]
                st = sbc("st", [128, 9, 4], F32); b_st = [[Buf(f"st{i}_{c}") for c in range(4)] for i in range(9)]
                h2 = sbc("h2", [128, 4, D], BF16); b_h2 = Buf("h2")
                h2Ts = [sbc(f"h2T{i}", [128, 8, 512], BF16) for i in range(2)]; b_h2Ts = [Buf(f"h2T{i}") for i in range(2)]
                actT = sbc("actT", [128, NDC, 512], BF16); b_act = Buf("actT")
                sg = [sbc(f"sg{i}", [128, 512], F32) for i in range(2)]; b_sg = [Buf(f"sg{i}") for i in range(2)]
                wctr = [0]

                import os as _os3
                if _os3.environ.get("KDEBUG"):
                    print("KDEBUG sbuf remaining phase C", nc.sbuf_bytes_remaining)

                def load_blk(i):
                    s_ = i % 2
                    sc.dma(xtc[s_][:, :, :], x_in[i * 512:(i + 1) * 512, :].rearrange("(c p) d -> p c d", p=128), writes=[b_xtc[s_]])
                    sc.dma(yl[s_][:, :, :], ysc[i], writes=[b_yl[s_]])

                def load_w(dc):
                    s_ = wctr[0] % NW
                    wctr[0] += 1
                    sc.dma(wg_sb[s_][:, :, :], wgs[l, dc].rearrange("p (k n) -> p k n", k=8), writes=[b_wg[s_]])
                    sc.dma(wu_sb[s_][:, :, :], wus[l, dc].rearrange("p (k n) -> p k n", k=8), writes=[b_wu[s_]])
                    return s_

                def resid_norm(i, c, pt, bp, gi, s0, final):
                    s_ = i % 2
                    xs = xtc[s_]
                    ring_id = 1 if final else 0
                    k = ring_id * NTF + tctr[ring_id] % NTF
                    tctr[ring_id] += 1
                    tf = tmpf[k]
                    b_tf = b_tmpf[k]
                    sc.op("act", lambda e: e.activation(out=tf[:, :], in_=pt[:, :], func=AF.Copy), reads=bp, writes=[b_tf])
                    sc.op("act", lambda e: e.activation(out=junkf[:, :], in_=tf[:, :], func=AF.Square, accum_out=st[:, s0, c:c + 1]),
                          reads=[b_tf], writes=[b_st[s0][c]])
                    rstd_small(st[:, s0 + 2, c:c + 1], st[:, s0, c:c + 1], D, b_st[s0][c], b_st[s0 + 2][c], b_st[s0 + 1][c], st[:, s0 + 1, c:c + 1], lnexp=False)
                    sc.op("dve", lambda e: e.scalar_tensor_tensor(out=tf[:, :], in0=tf[:, :], scalar=st[:, s0 + 2, c:c + 1], in1=gC[:, gi, :],
                                                                  op0=ALU.mult, op1=ALU.mult), reads=[b_st[s0 + 2][c], b_gC[gi]], writes=[b_tf])
                    sc.op("pool" if final else "dve", lambda e: e.tensor_tensor(out=xs[:, c, :], in0=tf[:, :], in1=xs[:, c, :], op=ALU.add),
                          reads=[b_tf], writes=[b_xtc[s_]])

                def four_norm(i):
                    s_ = i % 2
                    for mm in range(2):
                        sc.op("act", lambda e, mm=mm: e.activation(out=sqC[:, mm, :], in_=yl[s_][:, 6 + mm, :], func=AF.Square),
                              reads=[b_yl[s_]], writes=[b_sqC[mm]])
                    pt, bp = next_ps1(1)

                    def f_st(e, pt=pt):
                        for mm in range(2):
                            ins = e.matmul(pt[:, 0:512], lhsT=ones[:, :], rhs=sqC[:, mm, :], start=(mm == 0), stop=(mm == 1))
                        return ins
                    sc.op("pe", f_st, reads=[b_ones] + b_sqC, writes=bp[:1])
                    sc.op("act", lambda e, pt=pt: e.activation(out=sdC[:, :], in_=pt[:, 0:512], func=AF.Sqrt, scale=1.0 / 256, bias=epsc[:, 0:1]),
                          reads=bp[:1] + [b_eps], writes=[b_sdC])
                    sc.op("dve", lambda e: e.reciprocal(out=rsC[:, :], in_=sdC[:, :]), reads=[b_sdC], writes=[b_rsC])
                    for mm in range(2):
                        sc.op("dve", lambda e, mm=mm: e.scalar_tensor_tensor(out=yl[s_][:, 6 + mm, :], in0=yl[s_][:, 6 + mm, :],
                                                                            scalar=gng[:, l, 4 + mm:5 + mm], in1=rsC[:, :], op0=ALU.mult, op1=ALU.mult),
                              reads=[b_gng, b_rsC], writes=[b_yl[s_]])

                def stageC1(i):
                    s_ = i % 2
                    four_norm(i)
                    h2T = h2Ts[i % 2]; b_h2T = b_h2Ts[i % 2]
                    xs = xtc[s_]
                    for c in range(4):
                        pt, bp = next_ps(1)

                        def f_o(e, pt=pt, c=c):
                            for hf in range(2):
                                for kc in range(8):
                                    wk = (0, 1, 2, 3, 6, 7, 4, 5)[kc]
                                    ins = e.matmul(pt[:, hf * 512:(hf + 1) * 512], lhsT=yl[s_][:, kc, c * 128:(c + 1) * 128],
                                                   rhs=w_out_sb[:, wk, hf * 512:(hf + 1) * 512], start=(kc == 0), stop=(kc == 7))
                            return ins
                        sc.op("pe", f_o, reads=[b_yl[s_], b_wout], writes=bp)
                        resid_norm(i, c, pt, bp, 0, 0, False)
                        sc.op("act", lambda e, c=c: e.activation(out=junkf[:, :], in_=xs[:, c, :], func=AF.Square, accum_out=st[:, 3, c:c + 1]),
                              reads=[b_xtc[s_]], writes=[b_st[3][c]])
                        rstd_small(st[:, 5, c:c + 1], st[:, 3, c:c + 1], D, b_st[3][c], b_st[5][c], b_st[4][c], st[:, 4, c:c + 1], lnexp=False)
                        sc.op("dve", lambda e, c=c: e.scalar_tensor_tensor(out=h2[:, c, :], in0=xs[:, c, :], scalar=st[:, 5, c:c + 1], in1=gC[:, 1, :],
                                                                          op0=ALU.mult, op1=ALU.mult), reads=[b_xtc[s_], b_st[5][c], b_gC[1]], writes=[b_h2])
                    for c in range(4):
                        pt, bp = next_ps(3)

                        def f_tr(e, pt=pt, c=c):
                            for k in range(8):
                                ins = e.matmul(pt[:, k * 128:(k + 1) * 128], lhsT=h2[:, c, k * 128:(k + 1) * 128], rhs=ident[:, :], start=True, stop=True)
                            return ins
                        sc.op("pe", f_tr, reads=[b_h2, b_ident], writes=bp)
                        if c % 2 == 0:
                            sc.op("act", lambda e, pt=pt, c=c: e.activation(out=h2T[:, :, c * 128:(c + 1) * 128], in_=pt[:, :].rearrange("p (k n) -> p k n", k=8), func=AF.Copy),
                                  reads=bp, writes=[b_h2T])
                        else:
                            sc.op("dve", lambda e, pt=pt, c=c: e.tensor_copy(out=h2T[:, :, c * 128:(c + 1) * 128], in_=pt[:, :].rearrange("p (k n) -> p k n", k=8)),
                                  reads=bp, writes=[b_h2T])

                def stageC2(i, slots):
                    h2T = h2Ts[i % 2]; b_h2T = b_h2Ts[i % 2]
                    for dc in range(NDC):
                        ws = slots.pop(0)
                        if dc + NW - 1 < NDC:
                            slots.append(load_w(dc + NW - 1))
                        pt, bp = next_ps()

                        def f_g(e, pt=pt, ws=ws):
                            for k in range(8):
                                e.matmul(pt[:, 0:512], lhsT=wg_sb[ws][:, k, :], rhs=h2T[:, k, :], start=(k == 0), stop=(k == 7))
                            for k in range(8):
                                ins = e.matmul(pt[:, 512:1024], lhsT=wu_sb[ws][:, k, :], rhs=h2T[:, k, :], start=(k == 0), stop=(k == 7))
                            return ins
                        sc.op("pe", f_g, reads=[b_wg[ws], b_wu[ws], b_h2T], writes=bp)
                        sc.op("act", lambda e, pt=pt, dc=dc: e.activation(out=sg[dc % 2][:, :], in_=pt[:, 0:512], func=AF.Silu),
                              reads=bp[:1], writes=[b_sg[dc % 2]])
                        sc.op("dve", lambda e, pt=pt, dc=dc: e.tensor_tensor(out=actT[:, dc, :], in0=pt[:, 512:1024], in1=sg[dc % 2][:, :], op=ALU.mult),
                              reads=[bp[1], b_sg[dc % 2]], writes=[b_act])

                def stageC3(i):
                    s_ = i % 2
                    for c in range(4):
                        pt, bp = next_ps(2)

                        def f_d(e, pt=pt, c=c):
                            for hf in range(2):
                                for dc in range(NDC):
                                    ins = e.matmul(pt[:, hf * 512:(hf + 1) * 512], lhsT=actT[:, dc, c * 128:(c + 1) * 128],
                                                   rhs=w_dn_sb[:, dc, hf * 512:(hf + 1) * 512], start=(dc == 0), stop=(dc == NDC - 1))
                            return ins
                        sc.op("pe", f_d, reads=[b_act] + b_wdn4, writes=bp)
                        resid_norm(i, c, pt, bp, 2, 6, True)
                    sc.dma(x_out[i * 512:(i + 1) * 512, :].rearrange("(c p) d -> p c d", p=128), xtc[s_][:, :, :], reads=[b_xtc[s_]])

                load_blk(0)
                if NB > 1:
                    load_blk(1)
                for i in range(NB):
                    slots = [load_w(dc) for dc in range(NW - 1)]
                    stageC1(i)
                    stageC2(i, slots)
                    stageC3(i)
                    if i + 2 < NB:
                        load_blk(i + 2)
                sc.barrier()
        for l_ in range(L):
            layer(l_)
        sc.emit()
    return nc


_CACHE = {}


def kernel(**inputs):
    S = inputs["x"].shape[1]
    L = inputs["w_in"].shape[0]
    B = inputs["x"].shape[0]
    key = (S, L)
    if key not in _CACHE:
        _CACHE[key] = (build(S, L), host_consts(S))
    nc, consts = _CACHE[key]
    shared = {k: np.ascontiguousarray(np.asarray(inputs[k], dtype=np.float32)) for k in PARAM_SPECS(L)}
    shared.update(consts)
    x = np.asarray(inputs["x"], dtype=np.float32)
    in_maps = []
    for b in range(B):
        m = dict(shared)
        m["x"] = np.ascontiguousarray(x[b])
        in_maps.append(m)
    res = run_bass_kernel_spmd(nc, in_maps, core_ids=list(range(B)))
    return np.stack([np.asarray(r["out"], dtype=np.float32) for r in res.results], axis=0)
```

```python
from contextlib import ExitStack
import numpy as np
import ml_dtypes
import concourse.bass as bass
import concourse.mybir as mybir
from concourse.bass_utils import run_bass_kernel_spmd

F32 = mybir.dt.float32
BF16 = mybir.dt.bfloat16
AF = mybir.ActivationFunctionType
ALU = mybir.AluOpType
AX = mybir.AxisListType

D = 1024
DFF = 2816
NDC = DFF // 128
PW = 1792
EPS = 1e-6
ENGS = ("pe", "act", "dve", "pool", "sp")


class Buf:
    __slots__ = ("name", "w", "r", "semkey", "n", "last")

    def __init__(self, name):
        self.name = name
        self.w = None
        self.r = []
        self.semkey = None
        self.n = 0


class _FakeIns:
    def then_inc(self, *a, **k):
        return self


class _FakeEng:
    def __init__(self):
        self.calls = []

    def __getattr__(self, name):
        def f(*a, **k):
            self.calls.append((name, a, k))
            return _FakeIns()
        return f


def _free_size(ap):
    n = 1
    for d in ap.shape[1:]:
        n *= d
    return n


def _est_cost(eng, fn):
    fe = _FakeEng()
    fn(fe)
    t = 0.0
    for name, a, k in fe.calls:
        out = k.get("out", a[0] if a else None)
        F = _free_size(out)
        if name == "matmul":
            t += max(F, 48) * 0.43 + 14.0
        elif eng == "act":
            t += F * 0.87 + 200.0
        elif eng == "dve":
            t += F * 1.15 + 120.0
        elif eng == "pool":
            t += F * 3.5 + 250.0
        else:
            t += 60.0
    return t


SYNC_LAT = 150.0
DMA_LAT = 2200.0
DMA_BW = 120.0


class _Op:
    __slots__ = ("id", "eng", "fn", "preds", "cost", "dma", "owner", "nbytes", "tok", "finish", "prio", "succ", "npend", "ready", "tag")

    def __init__(self, id_, eng, fn, preds, cost, dma=False, owner=None, nbytes=0):
        self.id = id_
        self.eng = eng
        self.fn = fn
        self.preds = preds
        self.cost = cost
        self.dma = dma
        self.owner = owner
        self.nbytes = nbytes
        self.tok = None
        self.finish = 0.0
        self.prio = 0.0
        self.succ = []
        self.npend = 0
        self.ready = 0.0


class Sched:
    def __init__(self, nc, stack, reorder=True):
        self.nc = nc
        self.stack = stack
        self.reorder = reorder
        self.sems = {}
        self.ops = []
        self.region = []
        self.regions = []
        self.dma_bufs = []
        self.free_semkeys = []
        self._mark = 0
        for e in ENGS:
            self.sems[e] = stack.enter_context(nc.semaphore("c_" + e))

    def _preds(self, reads, writes):
        p = set()
        for b in reads:
            if b.w is not None:
                p.add(b.w)
        for b in writes:
            if b.w is not None:
                p.add(b.w)
            p.update(b.r)
        return p

    def _commit(self, op, reads, writes):
        self.ops.append(op)
        self.region.append(op.id)
        for b in reads:
            b.r.append(op.id)
        for b in writes:
            b.w = op.id
            b.r = []
        return op.id

    def op(self, eng, fn, reads=(), writes=()):
        o = _Op(len(self.ops), eng, fn, self._preds(reads, writes), _est_cost(eng, fn))
        import sys as _sys
        o.tag = _sys._getframe(1).f_lineno
        return self._commit(o, reads, writes)

    def dma(self, out, in_, reads=(), writes=(), q="sp", **kw):
        owner = writes[0] if writes else reads[0]
        if owner.semkey is None:
            if self.free_semkeys:
                owner.semkey = self.free_semkeys.pop()
            else:
                owner.semkey = f"d{len(self.sems)}_" + owner.name
                self.sems[owner.semkey] = self.stack.enter_context(self.nc.semaphore(owner.semkey))
            self.dma_bufs.append(owner)
            owner.n = 0
            owner.last = None
        preds = self._preds(reads, writes)
        if owner.last is not None:
            preds.add(owner.last)
        nbytes = _free_size(out) * out.shape[0] * (2 if out.dtype == BF16 else 4)
        o = _Op(len(self.ops), q, (lambda e: e.dma_start(out=out, in_=in_, **kw)), preds,
                60.0 if q == "sp" else 1000.0, dma=True, owner=owner, nbytes=nbytes)
        owner.last = o.id
        return self._commit(o, reads, writes)

    def barrier(self):
        self.regions.append(self.region)
        self.region = []

    def mark(self):
        self._mark = len(self.dma_bufs)

    def retire(self):
        for b in self.dma_bufs[self._mark:]:
            self.free_semkeys.append(b.semkey)
        del self.dma_bufs[self._mark:]

    def _schedule(self, ids):
        ops = self.ops
        inreg = set(ids)
        if not self.reorder:
            return list(ids)
        for i in ids:
            o = ops[i]
            o.succ = []
            o.npend = 0
            o.ready = 0.0
        for i in ids:
            o = ops[i]
            for p in o.preds:
                if p in inreg:
                    ops[p].succ.append(i)
                    o.npend += 1
        for i in reversed(ids):
            o = ops[i]
            m = 0.0
            for sidx in o.succ:
                if ops[sidx].prio > m:
                    m = ops[sidx].prio
            o.prio = m + o.cost + (DMA_LAT if o.dma else 0.0)
        free = {e: 0.0 for e in ENGS}
        ready = {e: [] for e in ENGS}
        import os as _os
        self.trace = {} if _os.environ.get("KTRACE") else None
        self.last_on = {}
        for i in ids:
            if ops[i].npend == 0:
                ready[ops[i].eng].append(i)
        order = []
        n = len(ids)
        while len(order) < n:
            best = None
            bkey = None
            for e in ENGS:
                fe = free[e]
                for i in ready[e]:
                    o = ops[i]
                    st = o.ready if o.ready > fe else fe
                    key = (st, -o.prio, i)
                    if bkey is None or key < bkey:
                        bkey = key
                        best = i
            o = ops[best]
            ready[o.eng].remove(best)
            st = bkey[0]
            if self.trace is not None:
                self.trace[best] = (st, free[o.eng], o.ready, self.last_on.get(o.eng))
                self.last_on[o.eng] = best
            free[o.eng] = st + o.cost
            if o.dma:
                o.finish = st + o.cost + DMA_LAT + o.nbytes / DMA_BW
            else:
                o.finish = st + o.cost
            order.append(best)
            for sidx in o.succ:
                so = ops[sidx]
                t = o.finish + SYNC_LAT
                if t > so.ready:
                    so.ready = t
                so.npend -= 1
                if so.npend == 0:
                    ready[so.eng].append(sidx)
        self.makespan = max(free.values())
        self._nreg = getattr(self, "_nreg", -1) + 1
        if self.trace is not None and self._nreg == int(_os.environ.get("KREGION", "0")):
            cur = max(ids, key=lambda i: ops[i].finish)
            chain = []
            while cur is not None and len(chain) < 400:
                st, fe, rd, prev = self.trace[cur]
                o = ops[cur]
                if rd >= fe and o.preds:
                    cand = [p for p in o.preds if p in inreg]
                    if not cand:
                        break
                    pbest = max(cand, key=lambda p: ops[p].finish)
                    chain.append((cur, o.eng, round(st), round(o.cost), "dep", getattr(o, "tag", "")))
                    cur = pbest
                else:
                    chain.append((cur, o.eng, round(st), round(o.cost), "eng", getattr(o, "tag", "")))
                    cur = prev
            for c in chain[:int(_os.environ.get("KTRACE"))]:
                print("KCHAIN", c)
        return order

    def emit(self):
        if self.region:
            self.regions.append(self.region)
            self.region = []
        nc = self.nc
        sems = self.sems
        ops = self.ops
        cnt = {e: 0 for e in ENGS}
        seen = {e: {} for e in ENGS}
        prog = {e: [] for e in ENGS}
        dcount = {}
        self.makespans = []

        def waits_for(eng, toks):
            need = {}
            for k, v in toks:
                if k == eng and eng in ("pe", "sp"):
                    continue
                if seen[eng].get(k, 0) >= v:
                    continue
                if need.get(k, 0) < v:
                    need[k] = v
            for k, v in need.items():
                seen[eng][k] = v
            return list(need.items())

        for ids in self.regions:
            order = self._schedule(ids)
            self.makespans.append(getattr(self, "makespan", 0.0))
            if not hasattr(self, "busy"):
                self.busy = []
            bz = {e: 0.0 for e in ENGS}
            for i in ids:
                bz[ops[i].eng] += ops[i].cost
            self.busy.append({e: round(v / 1e3) for e, v in bz.items()})
            for i in order:
                o = ops[i]
                w = waits_for(o.eng, [ops[p].tok for p in o.preds])
                if o.dma:
                    k = o.owner.semkey
                    dcount[k] = dcount.get(k, 0) + 1
                    o.tok = (k, 16 * dcount[k])
                    prog[o.eng].append((w, o.fn, (k, 16)))
                else:
                    cnt[o.eng] += 1
                    o.tok = (o.eng, cnt[o.eng])
                    prog[o.eng].append((w, o.fn, (o.eng, 1)))
            toks = [(k, 16 * v) for k, v in dcount.items()]
            toks += [(e, cnt[e]) for e in ENGS if e != "sp" and cnt[e] > 0]
            w = waits_for("sp", toks)
            cnt["sp"] += 1
            tsp = ("sp", cnt["sp"])
            prog["sp"].append((w, (lambda e: e.nop()), ("sp", 1)))
            for e in ENGS:
                if e == "sp":
                    continue
                w = waits_for(e, [tsp] + toks)
                if w:
                    prog[e].append((w, None, None))

        import os as _os
        if _os.environ.get("KDEBUG"):
            print("KDEBUG counts", cnt, "max dma", max(dcount.values()) * 16, "nsems", len(sems), "makespans_us", [round(m / 1e3) for m in self.makespans])
            print("KDEBUG busy_us", self.busy)
            print("KDEBUG nwaits", {e: sum(len(w) for w, _, _ in prog[e]) for e in ENGS}, {e: len(prog[e]) for e in ENGS})

        def run(e, lst):
            for waits, fn, inc in lst:
                for k, v in waits:
                    e.wait_ge(sems[k], v)
                if fn is not None:
                    ins = fn(e)
                    if inc is not None:
                        ins.then_inc(sems[inc[0]], inc[1])

        with nc.Block() as block:
            @block.sync
            def _(e):
                run(e, prog["sp"])

            @block.tensor
            def _(e):
                run(e, prog["pe"])

            @block.scalar
            def _(e):
                run(e, prog["act"])

            @block.vector
            def _(e):
                run(e, prog["dve"])

            @block.gpsimd
            def _(e):
                run(e, prog["pool"])


def host_consts(S):
    J = S // 128
    bf = ml_dtypes.bfloat16
    c = {}
    c["ident"] = np.eye(128, dtype=np.float32).astype(bf)
    c["ones"] = np.ones((128, 128), dtype=np.float32).astype(bf)
    a = np.arange(64)
    ang = 2 * np.pi * np.outer(a, a) / 64.0
    C64 = np.cos(ang) / 8.0
    S64 = np.sin(ang) / 8.0
    z = np.zeros((64, 64))
    c["c2"] = np.block([[C64, z], [z, C64]]).astype(np.float32)
    c["s2n"] = np.block([[-S64, z], [z, -S64]]).astype(np.float32)
    q = np.arange(128)[:, None, None]
    j = np.arange(J)[None, :, None]
    k1 = np.arange(128)[None, None, :]
    m = (k1 * (J * q + j)) % S
    th = 2 * np.pi * m / S
    sc = 1.0 / np.sqrt(128.0)
    c["tc"] = (np.cos(th) * sc).reshape(128, J * 128).astype(np.float32).astype(bf)
    c["ts"] = (np.sin(th) * sc).reshape(128, J * 128).astype(np.float32).astype(bf)
    c["tsn"] = (-np.sin(th) * sc).reshape(128, J * 128).astype(np.float32).astype(bf)
    jj = np.arange(J)[:, None]
    k2 = np.arange(J)[None, :]
    th3 = 2 * np.pi * jj * k2 / J
    cs3 = np.zeros((J, 2, J))
    cs3[:, 0, :] = np.cos(th3) / np.sqrt(J)
    cs3[:, 1, :] = np.sin(th3) / np.sqrt(J)
    c["cs3"] = cs3.reshape(2 * J, J).astype(np.float32).astype(bf)
    ic = np.zeros((128, 2, 2, 8), dtype=np.float32)
    for mch in range(2):
        for half in range(2):
            w = (2, 4, 8, 16)[mch * 2 + half]
            for t in range(8):
                lo = max(t - w // 2, 0)
                hi = min(t + w // 2, S)
                ic[half * 64:(half + 1) * 64, mch, 0, t] = 1.0 / (hi - lo)
                tt = S - 8 + t
                lo = max(tt - w // 2, 0)
                hi = min(tt + w // 2, S)
                ic[half * 64:(half + 1) * 64, mch, 1, t] = 1.0 / (hi - lo)
    c["icnt"] = ic.reshape(128, 32)
    return c


CONST_SPECS = lambda J: {
    "ident": ([128, 128], BF16), "ones": ([128, 128], BF16),
    "c2": ([128, 128], F32), "s2n": ([128, 128], F32),
    "tc": ([128, J * 128], BF16), "ts": ([128, J * 128], BF16), "tsn": ([128, J * 128], BF16),
    "cs3": ([2 * J, J], BF16), "icnt": ([128, 32], F32),
}

PARAM_SPECS = lambda L: {
    "pre_mix_gain": [L, D], "post_mix_gain": [L, D], "pre_ffn_gain": [L, D], "post_ffn_gain": [L, D],
    "w_in": [L, D, PW], "conv_w": [L, 3, 256], "pool_w": [L, 4, 64, 64], "pool_scale": [L, 256],
    "fourier_w": [L, 4, 64, 64], "spatial_w": [L, 4, 128, 128], "spatial_b": [L, 4, 128],
    "group_norm_gain": [L, D], "w_out": [L, D, D], "w_gate": [L, D, DFF], "w_up": [L, D, DFF],
    "w_down": [L, DFF, D],
}


CFG = dict(xt=2, hb=1, cv=1, pl=1, yfm=1, gn=1, gm=1, ntf=1, cpools={0: [0, 1], 1: [2], 2: [3], 3: [0, 1]})


def build(S=8192, L=2, dbg=None):
    J = S // 128
    NB = S // 512
    J2 = 2 * J
    nc = bass.Bass("TRN2", target_bir_lowering=False)
    stack = ExitStack()
    with stack:
        dram = {}
        dram["x"] = nc.dram_tensor("x", [S, D], F32, kind="ExternalInput").ap()
        for k, shp in PARAM_SPECS(L).items():
            dram[k] = nc.dram_tensor(k, shp, F32, kind="ExternalInput").ap()
        for k, (shp, dt_) in CONST_SPECS(J).items():
            dram[k] = nc.dram_tensor(k, shp, dt_, kind="ExternalInput").ap()
        out_d = nc.dram_tensor("out", [S, D], F32, kind="ExternalOutput").ap()
        xmid = [nc.dram_tensor(f"xmid{l}", [S, D], F32, kind="Internal").ap() for l in range(max(L - 1, 1))]
        ysc = nc.dram_tensor("ysc", [NB, 128, 8, 512], BF16, kind="Internal").ap()
        wgs = nc.dram_tensor("wgs", [L, NDC, 128, 1024], BF16, kind="Internal").ap()
        wus = nc.dram_tensor("wus", [L, NDC, 128, 1024], BF16, kind="Internal").ap()
        fsc = nc.dram_tensor("fsc", [128, 2, S], BF16, kind="Internal").ap()
        wds = nc.dram_tensor("wds", [L, NDC, 128, 1024], BF16, kind="Internal").ap()
        wos = nc.dram_tensor("wos", [L, 8, 128, 1024], BF16, kind="Internal").ap()
        dbg_out = {}
        if dbg:
            for k, shp, dt_ in dbg:
                dbg_out[k] = nc.dram_tensor(k, shp, dt_, kind="ExternalOutput").ap()

        sc = Sched(nc, stack)

        def sb(name, shape, dt_):
            return stack.enter_context(nc.sbuf_tensor("s_" + name, shape, dt_))

        ident = sb("ident", [128, 128], BF16); b_ident = Buf("ident")
        ones = sb("ones", [128, 128], BF16); b_ones = Buf("ones")
        epsc = sb("epsc", [128, 1], F32); b_eps = Buf("eps")
        cw = sb("cw", [128, L, 3, 2], F32); b_cw = Buf("cw")
        psc = sb("psc", [128, L, 2], F32); b_psc = Buf("psc")
        gng = sb("gng", [128, L, 8], F32); b_gng = Buf("gng")
        bsp = sb("bsp", [128, L, 4], F32); b_bsp = Buf("bsp")
        icnt = sb("icnt", [128, 2, 2, 8], F32); b_icnt = Buf("icnt")
        ggm = sb("ggm", [128, 256], F32); b_ggm = Buf("ggm")

        psum = [stack.enter_context(nc.psum_tensor(f"ps{i}", [128, 1024], F32)) for i in range(4)]
        b_ps = [[Buf(f"ps{i}a"), Buf(f"ps{i}b")] for i in range(4)]
        ps_rr = [0]

        ps_rr2 = {0: 0, 1: 0, "s0": 0, "s1": 0}

        class _Half:
            def __init__(self, t, off):
                self.t = t
                self.off = off

            def __getitem__(self, key):
                p, c = key
                assert c.start is not None and c.stop is not None and c.stop <= 512
                return self.t[p, c.start + self.off:c.stop + self.off]

        pools = {"cur": {0: [0, 1], 1: [2, 3]}}

        def set_pools(d):
            pools["cur"] = d
            for k in list(ps_rr2.keys()):
                ps_rr2[k] = 0

        def next_ps(pool=0):
            lst = pools["cur"][pool]
            k = ps_rr2.get(pool, 0)
            ps_rr2[pool] = k + 1
            i = lst[k % len(lst)]
            return psum[i], b_ps[i]

        def next_ps1(pool=0):
            lst = pools["cur"][pool]
            key = "s%d" % pool
            k = ps_rr2.get(key, 0)
            ps_rr2[key] = k + 1
            k = k % (2 * len(lst))
            pi = lst[k // 2]
            return _Half(psum[pi], (k % 2) * 512), [b_ps[pi][k % 2]]

        sc.dma(ident[:, :], dram["ident"], writes=[b_ident])
        sc.dma(ones[:, :], dram["ones"], writes=[b_ones])
        sc.dma(icnt[:, :, :, :].rearrange("p a b c -> p (a b c)"), dram["icnt"], writes=[b_icnt])
        sc.op("pool", lambda e: e.memset(epsc[:, :], EPS), writes=[b_eps])
        sc.dma(cw[:, :, :, :], dram["conv_w"].rearrange("l t (m p) -> p l t m", p=128), writes=[b_cw],
               allow_slow_non_contiguous=True)
        sc.dma(psc[:, :, :], dram["pool_scale"].rearrange("l (m p) -> p l m", p=128), writes=[b_psc],
               allow_slow_non_contiguous=True)
        sc.dma(gng[:, :, :], dram["group_norm_gain"].rearrange("l (k p) -> p l k", p=128), writes=[b_gng],
               allow_slow_non_contiguous=True)
        sc.dma(bsp[:, :, :], dram["spatial_b"].rearrange("l h p -> p l h"), writes=[b_bsp],
               allow_slow_non_contiguous=True)

        def convert_gate_up(l, stg, b_stg):
            n = 0
            for src, dst in ((dram["w_gate"], wgs), (dram["w_up"], wus)):
                for c0 in range(NDC):
                    s_ = n % len(stg)
                    n += 1
                    sc.dma(stg[s_][:, :].rearrange("p (k n) -> p k n", k=8),
                           src[l][:, c0 * 128:(c0 + 1) * 128].rearrange("(k p) n -> p k n", p=128),
                           writes=[b_stg[s_]], q="pool")
                    sc.dma(dst[l, c0], stg[s_][:, :], reads=[b_stg[s_]], q="pool")

        def convert_down_out(l, stg, b_stg):
            n = 0
            for src, dst, nch in ((dram["w_out"], wos, 8), (dram["w_down"], wds, NDC)):
                for c0 in range(nch):
                    s_ = n % len(stg)
                    n += 1
                    sc.dma(stg[s_][:, :], src[l][c0 * 128:(c0 + 1) * 128, :], writes=[b_stg[s_]], q="pool")
                    sc.dma(dst[l, c0], stg[s_][:, :], reads=[b_stg[s_]], q="pool")

        def rstd_small(out_ap, in_ap, n, b_in, b_out, b_tmp, tmp_ap, lnexp=True):
            if lnexp:
                sc.op("act", lambda e: e.activation(out=tmp_ap, in_=in_ap, func=AF.Ln, scale=1.0 / n, bias=epsc[:, 0:1]),
                      reads=[b_in, b_eps], writes=[b_tmp])
                sc.op("act", lambda e: e.activation(out=out_ap, in_=tmp_ap, func=AF.Exp, scale=-0.5), reads=[b_tmp], writes=[b_out])
            else:
                sc.op("act", lambda e: e.activation(out=tmp_ap, in_=in_ap, func=AF.Sqrt, scale=1.0 / n, bias=epsc[:, 0:1]),
                      reads=[b_in, b_eps], writes=[b_tmp])
                sc.op("dve", lambda e: e.reciprocal(out=out_ap, in_=tmp_ap), reads=[b_tmp], writes=[b_out])

        def layer(l):
            x_in = dram["x"] if l == 0 else xmid[l - 1]
            x_out = out_d if l == L - 1 else xmid[l]
            b_ggm = Buf(f"ggm{l}")
            sc.dma(ggm[:, :], dram["group_norm_gain"][l:l + 1, 768:1024].partition_broadcast(128), writes=[b_ggm])
            fstack = ExitStack()
            fT = fstack.enter_context(nc.sbuf_tensor(f"fT_{l}", [128, 2, S], BF16)); b_fT = Buf(f"fT{l}")
            fTflat = fT
            Dh = fstack.enter_context(nc.sbuf_tensor(f"Dh_{l}", [128, 2, 256], BF16)); b_Dh = Buf(f"Dh{l}")

            set_pools({0: [0, 1], 1: [2, 3]})
            with ExitStack() as pa:
                def sa(name, shape, dt_):
                    return pa.enter_context(nc.sbuf_tensor(f"a_{name}_{l}", shape, dt_))
                w_in_sb = sa("w_in", [128, 8, PW], BF16); b_win = Buf(f"win{l}")
                gA = sa("gA", [128, D], F32); b_gA = Buf(f"gA{l}")
                sc.dma(gA[:, :], dram["pre_mix_gain"][l:l + 1, :].partition_broadcast(128), writes=[b_gA])
                sc.dma(w_in_sb[:, :, :], dram["w_in"][l].rearrange("(k p) n -> p k n", p=128), writes=[b_win], q="pool")
                pwbd = sa("pwbd", [128, 2, 128], BF16); b_pwbd = Buf(f"pwbd{l}")
                sc.op("pool", lambda e: e.memset(pwbd[:, :, :], 0.0), writes=[b_pwbd])
                for g in range(4):
                    h0 = (g % 2) * 64
                    sc.dma(pwbd[h0:h0 + 64, g // 2, h0:h0 + 64], dram["pool_w"][l, g], writes=[b_pwbd], q="pool")
                wsn = sa("wsn", [128, 4, 128], BF16); b_wsn = Buf(f"wsn{l}")
                sc.dma(wsn[:, :, :], dram["spatial_w"][l].rearrange("h p q -> p h q"), writes=[b_wsn], q="pool")
                wsT = sa("wsT", [128, 4, 128], BF16); b_wsT = Buf(f"wsT{l}")
                pt_, bp_ = next_ps()

                def f_wsT(e, pt_=pt_):
                    for h in range(4):
                        ins = e.matmul(pt_[:, h * 128:(h + 1) * 128], lhsT=wsn[:, h, :], rhs=ident[:, :], start=True, stop=True)
                    return ins
                sc.op("pe", f_wsT, reads=[b_wsn, b_ident], writes=bp_[:1])
                sc.op("dve", lambda e, pt_=pt_: e.tensor_copy(out=wsT[:, :, :].rearrange("p h q -> p (h q)"), in_=pt_[:, 0:512]),
                      reads=bp_[:1], writes=[b_wsT])

                c2 = sa("c2", [128, 128], F32); b_c2 = Buf(f"c2{l}")
                s2n = sa("s2n", [128, 128], F32); b_s2n = Buf(f"s2n{l}")
                fw2 = sa("fw2", [128, 2, 128], F32); b_fw2 = Buf(f"fw2{l}")
                sc.dma(c2[:, :], dram["c2"], writes=[b_c2])
                sc.dma(s2n[:, :], dram["s2n"], writes=[b_s2n])
                sc.op("pool", lambda e: e.memset(fw2[:, :, :], 0.0), writes=[b_fw2])
                for h in range(4):
                    h0 = (h % 2) * 64
                    sc.dma(fw2[h0:h0 + 64, h // 2, h0:h0 + 64], dram["fourier_w"][l, h], writes=[b_fw2])
                for half in range(2):
                    pt, bp = next_ps1(1)

                    def f_D(e, pt=pt, half=half):
                        e.matmul(pt[:, 0:128], lhsT=c2[:, :], rhs=fw2[:, half, :], start=True, stop=True)
                        return e.matmul(pt[:, 128:256], lhsT=s2n[:, :], rhs=fw2[:, half, :], start=True, stop=True)
                    sc.op("pe", f_D, reads=[b_c2, b_s2n, b_fw2], writes=bp[:1])
                    sc.op("dve", lambda e, pt=pt, half=half: e.tensor_copy(out=Dh[:, half, :], in_=pt[:, 0:256]), reads=bp[:1], writes=[b_Dh])
                stg = [sa(f"stg{i}", [128, 1024], BF16) for i in range(2)]
                b_stg = [Buf(f"stgA{l}_{i}") for i in range(2)]
                convert_gate_up(l, stg, b_stg)

                def ring(name, shape, dt_, n):
                    return ([sa(f"{name}{k}", shape, dt_) for k in range(n)], [Buf(f"{name}{l}_{k}") for k in range(n)])

                def pick(r, idx):
                    return r[0][idx % len(r[0])], r[1][idx % len(r[1])]
                NX = CFG["xt"]
                xt, b_xt = ring("xt", [128, 4, D], F32, NX)
                ssx_r = ring("ssx", [128, 4], F32, 2); ssx2_r = ring("ssx2", [128, 4], F32, 2); rsx_r = ring("rsx", [128, 4], F32, 2)
                hb_r = ring("hb", [128, 4, D], BF16, CFG["hb"])
                hTx, b_hTx = ring("hTx", [128, 8, 528], BF16, 3)
                z_r = ring("z", [128, 528], F32, CFG["cv"]); cz_r = ring("cz", [128, 528], F32, CFG["cv"])
                bsb_r = ring("bsb", [128, 512], F32, CFG["cv"])
                acc_r = ring("acc", [128, 512], F32, CFG["cv"]); acc2_r = ring("acc2", [128, 512], F32, CFG["cv"])
                yfm_r = [ring(f"yfm{k}", [128, 512], F32, CFG["yfm"]) for k in range(4)]
                p_r = ring("p", [128, 528], F32, CFG["pl"]); Ra_r = ring("Ra", [128, 528], F32, CFG["pl"])
                Rb_r = ring("Rb", [128, 528], F32, CFG["pl"]); Rc_r = ring("Rc", [128, 528], F32, CFG["pl"])
                etmp = sa("etmp", [128, 8], F32); b_etmp = Buf("etmp")
                dpl_r = [ring(f"dpl{k}", [128, 512], BF16, CFG["pl"]) for k in range(2)]
                sq_r = [ring(f"sq{k}", [128, 512], BF16, CFG["gn"]) for k in range(2)]
                sd_r = ring("sd", [128, 512], F32, CFG["gn"]); rs_r = ring("rs", [128, 512], F32, CFG["gn"])
                yst, b_yst = ring("yst", [128, 6, 512], BF16, 2)
                uv_r = ring("uv", [128, 4, 512], F32, CFG["gm"]); sqv_r = ring("sqv", [128, 4, 256], F32, CFG["gm"])
                vst_r = [ring(f"vst{k}", [128, 16], F32, 2) for k in range(6)]
                vh_r = ring("vh", [128, 4, 256], BF16, CFG["gm"]); yg_r = ring("yg", [128, 4, 256], F32, CFG["gm"])
                ygn_r = ring("ygn", [128, 4, 256], BF16, CFG["gm"])
                gst_r = [ring(f"gst{k}", [128, 4], F32, 2) for k in range(3)]

                import os as _os2
                if _os2.environ.get("KDEBUG"):
                    print("KDEBUG sbuf remaining phase A", nc.sbuf_bytes_remaining)

                b_fTw = [Buf(f"fTw{l}_{i}") for i in range(NB)]

                def load_x(i):
                    s_ = i % NX
                    sc.dma(xt[s_][:, :, :], x_in[i * 512:(i + 1) * 512, :].rearrange("(c p) d -> p c d", p=128),
                           writes=[b_xt[s_]])

                def stageN(i):
                    s_ = i % NX
                    xs = xt[s_]
                    hs = i % 3
                    ssx, b_ssx = pick(ssx_r, i); ssx2, b_ssx2 = pick(ssx2_r, i); rsx, b_rsx = pick(rsx_r, i)
                    hb, b_hb = pick(hb_r, i)
                    for c in range(4):
                        sc.op("act", lambda e, c=c: e.activation(out=hb[:, c, :], in_=xs[:, c, :], func=AF.Square,
                                                                 accum_out=ssx[:, c:c + 1]),
                              reads=[b_xt[s_]], writes=[b_hb, b_ssx])
                    rstd_small(rsx[:, :], ssx[:, :], D, b_ssx, b_rsx, b_ssx2, ssx2[:, :])
                    for c in range(4):
                        sc.op("dve", lambda e, c=c: e.scalar_tensor_tensor(out=hb[:, c, :], in0=xs[:, c, :], scalar=rsx[:, c:c + 1],
                                                                         in1=gA[:, :], op0=ALU.mult, op1=ALU.mult),
                              reads=[b_xt[s_], b_rsx, b_gA], writes=[b_hb])
                    if i == 0:
                        sc.op("pool", lambda e: e.memset(hTx[hs][:, :, 0:8], 0.0), writes=[b_hTx[hs]])
                    if i == NB - 1:
                        sc.op("pool", lambda e: e.memset(hTx[hs][:, :, 520:528], 0.0), writes=[b_hTx[hs]])
                    for c in range(4):
                        pt, bp = next_ps()

                        def f_tr(e, c=c, pt=pt):
                            for k in range(8):
                                ins = e.matmul(pt[:, k * 128:(k + 1) * 128], lhsT=hb[:, c, k * 128:(k + 1) * 128], rhs=ident[:, :],
                                               start=True, stop=True)
                            return ins
                        sc.op("pe", f_tr, reads=[b_hb, b_ident], writes=bp)
                        sc.op("act", lambda e, c=c, pt=pt: e.activation(out=hTx[hs][:, :, 8 + c * 128:8 + (c + 1) * 128],
                                                                      in_=pt[:, :].rearrange("p (k n) -> p k n", k=8), func=AF.Copy),
                              reads=bp, writes=[b_hTx[hs]])
                    if i >= 1:
                        hp = (i - 1) % 3
                        sc.op("pool", lambda e: e.tensor_copy(out=hTx[hp][:, :, 520:528], in_=hTx[hs][:, :, 8:16]),
                              reads=[b_hTx[hs]], writes=[b_hTx[hp]])
                    if i + 1 < NB:
                        hn = (i + 1) % 3
                        sc.op("pool", lambda e: e.tensor_copy(out=hTx[hn][:, :, 0:8], in_=hTx[hs][:, :, 512:520]),
                              reads=[b_hTx[hs]], writes=[b_hTx[hn]])

                def proj_fm(i, m, halo):
                    hs = i % 3
                    pt, bp = next_ps() if halo else next_ps1()
                    if halo:
                        def f(e, pt=pt):
                            for k in range(8):
                                e.matmul(pt[:, 0:512], lhsT=w_in_sb[:, k, m * 128:(m + 1) * 128], rhs=hTx[hs][:, k, 0:512],
                                         start=(k == 0), stop=(k == 7))
                            for k in range(8):
                                ins = e.matmul(pt[:, 512:528], lhsT=w_in_sb[:, k, m * 128:(m + 1) * 128], rhs=hTx[hs][:, k, 512:528],
                                               start=(k == 0), stop=(k == 7))
                            return ins
                        sc.op("pe", f, reads=[b_win, b_hTx[hs]], writes=bp)
                    else:
                        def f(e, pt=pt):
                            for k in range(8):
                                ins = e.matmul(pt[:, 0:512], lhsT=w_in_sb[:, k, m * 128:(m + 1) * 128], rhs=hTx[hs][:, k, 8:520],
                                               start=(k == 0), stop=(k == 7))
                            return ins
                        sc.op("pe", f, reads=[b_win, b_hTx[hs]], writes=bp[:1])
                    return pt, bp

                def group_norm_fm(i, gidx, ys):
                    ri = 2 * i + gidx
                    sqs = [pick(sq_r[mm], ri) for mm in range(2)]
                    sd, b_sd = pick(sd_r, ri); rs, b_rs = pick(rs_r, ri)
                    yf = [pick(yfm_r[2 * gidx + mm], i) for mm in range(2)]
                    for mm in range(2):
                        sc.op("act", lambda e, mm=mm: e.activation(out=sqs[mm][0][:, :], in_=yf[mm][0][:, :], func=AF.Square),
                              reads=[yf[mm][1]], writes=[sqs[mm][1]])
                    pt, bp = next_ps1(1)

                    def f(e, pt=pt):
                        for mm in range(2):
                            ins = e.matmul(pt[:, 0:512], lhsT=ones[:, :], rhs=sqs[mm][0][:, :], start=(mm == 0), stop=(mm == 1))
                        return ins
                    sc.op("pe", f, reads=[b_ones, sqs[0][1], sqs[1][1]], writes=bp[:1])
                    sc.op("act", lambda e, pt=pt: e.activation(out=sd[:, :], in_=pt[:, 0:512], func=AF.Ln, scale=1.0 / 256,
                                                             bias=epsc[:, 0:1]), reads=bp[:1] + [b_eps], writes=[b_sd])
                    sc.op("act", lambda e: e.activation(out=rs[:, :], in_=sd[:, :], func=AF.Exp, scale=-0.5), reads=[b_sd], writes=[b_rs])
                    for mm in range(2):
                        kc = 2 * gidx + mm
                        sc.op("dve", lambda e, mm=mm, kc=kc: e.scalar_tensor_tensor(out=yst[ys][:, kc, :], in0=yf[mm][0][:, :],
                                                                                 scalar=gng[:, l, kc:kc + 1], in1=rs[:, :],
                                                                                 op0=ALU.mult, op1=ALU.mult),
                              reads=[yf[mm][1], b_gng, b_rs], writes=[b_yst[ys]])

                def conv_chunk(i, mm):
                    ri = 2 * i + mm
                    z_sb, b_z = pick(z_r, ri); cz, b_cz = pick(cz_r, ri); acc, b_acc = pick(acc_r, ri); acc2, b_acc2 = pick(acc2_r, ri)
                    yf, b_yf = pick(yfm_r[mm], i)
                    pz, bz = proj_fm(i, 4 + mm, True)
                    sc.op("act", lambda e: e.activation(out=z_sb[:, :], in_=pz[:, 0:528], func=AF.Copy), reads=bz, writes=[b_z])
                    pc, bc = proj_fm(i, 2 + mm, True)
                    sc.op("dve", lambda e: e.tensor_tensor(out=cz[:, :], in0=pc[:, 0:528], in1=z_sb[:, :], op=ALU.mult),
                          reads=bc + [b_z], writes=[b_cz])
                    bsb, b_bsb = pick(bsb_r, ri)
                    pb, bb = proj_fm(i, mm, False)
                    sc.op("act", lambda e: e.activation(out=bsb[:, :], in_=pb[:, 0:512], func=AF.Copy), reads=bb[:1], writes=[b_bsb])
                    sc.op("act", lambda e: e.activation(out=acc[:, :], in_=cz[:, 8:520], func=AF.Copy, scale=cw[:, l, 1, mm:mm + 1]),
                          reads=[b_cz, b_cw], writes=[b_acc])
                    sc.op("dve", lambda e: e.scalar_tensor_tensor(out=acc2[:, :], in0=cz[:, 7:519], scalar=cw[:, l, 0, mm:mm + 1],
                                                                  in1=acc[:, :], op0=ALU.mult, op1=ALU.add),
                          reads=[b_cz, b_cw, b_acc], writes=[b_acc2])
                    sc.op("dve", lambda e: e.scalar_tensor_tensor(out=acc[:, :], in0=cz[:, 9:521], scalar=cw[:, l, 2, mm:mm + 1],
                                                                  in1=acc2[:, :], op0=ALU.mult, op1=ALU.add),
                          reads=[b_cz, b_cw, b_acc2], writes=[b_acc])
                    sc.op("dve", lambda e: e.tensor_tensor(out=yf[:, :], in0=bsb[:, :], in1=acc[:, :], op=ALU.mult),
                          reads=[b_bsb, b_acc], writes=[b_yf])

                def pool_chunk(i, mm):
                    ri = 2 * i + mm
                    p_sb, b_p = pick(p_r, ri); Ra, b_Ra = pick(Ra_r, ri); Rb, b_Rb = pick(Rb_r, ri); Rc, b_Rc = pick(Rc_r, ri)
                    Rd, b_Rd = Ra, b_Ra
                    dpl, b_dpl = pick(dpl_r[mm], i)
                    pp, bpp = proj_fm(i, 6 + mm, True)
                    sc.op("act", lambda e: e.activation(out=p_sb[:, :], in_=pp[:, 0:528], func=AF.Copy), reads=bpp, writes=[b_p])
                    e0 = "pool" if mm == 0 else "dve"
                    sc.op(e0, lambda e: e.tensor_tensor(out=Ra[:, 0:527], in0=p_sb[:, 0:527], in1=p_sb[:, 1:528], op=ALU.add),
                          reads=[b_p], writes=[b_Ra])
                    sc.op(e0, lambda e: e.tensor_tensor(out=Rb[:, 0:525], in0=Ra[:, 0:525], in1=Ra[:, 2:527], op=ALU.add),
                          reads=[b_Ra], writes=[b_Rb])
                    if mm == 0:
                        wins = ((0, 64, Ra, b_Ra, 2), (64, 128, Rb, b_Rb, 4))
                    else:
                        sc.op("dve", lambda e: e.tensor_tensor(out=Rc[:, 0:521], in0=Rb[:, 0:521], in1=Rb[:, 4:525], op=ALU.add),
                              reads=[b_Rb], writes=[b_Rc])
                        sc.op("dve", lambda e: e.tensor_tensor(out=Rd[:, 0:513], in0=Rc[:, 0:513], in1=Rc[:, 8:521], op=ALU.add),
                              reads=[b_Rc], writes=[b_Rd])
                        wins = ((0, 64, Rc, b_Rc, 8), (64, 128, Rd, b_Rd, 16))
                    for (p0, p1, R, bR, w) in wins:
                        o = 8 - w // 2
                        sc.op("dve", lambda e, p0=p0, p1=p1, R=R, w=w, o=o: e.scalar_tensor_tensor(
                            out=dpl[p0:p1, :], in0=R[p0:p1, o:o + 512], scalar=1.0 / w, in1=p_sb[p0:p1, 8:520],
                            op0=ALU.mult, op1=ALU.subtract), reads=[bR, b_p], writes=[b_dpl])
                        for side, blk in ((0, 0), (1, NB - 1)):
                            if i != blk:
                                continue
                            c0 = 0 if side == 0 else 504
                            sc.op("dve", lambda e, p0=p0, p1=p1, R=R, o=o, c0=c0, side=side: e.tensor_tensor(
                                out=etmp[p0:p1, :], in0=R[p0:p1, o + c0:o + c0 + 8], in1=icnt[p0:p1, mm, side, :], op=ALU.mult),
                                reads=[bR, b_icnt], writes=[b_etmp])
                            sc.op("dve", lambda e, p0=p0, p1=p1, c0=c0: e.tensor_tensor(
                                out=dpl[p0:p1, c0:c0 + 8], in0=etmp[p0:p1, :], in1=p_sb[p0:p1, 8 + c0:16 + c0], op=ALU.subtract),
                                reads=[b_etmp, b_p], writes=[b_dpl])

                def stageP(i):
                    hs = i % 3
                    ys = i % 2
                    conv_chunk(i, 0)
                    conv_chunk(i, 1)
                    pool_chunk(i, 0)
                    pool_chunk(i, 1)
                    dp = [pick(dpl_r[mm], i) for mm in range(2)]
                    yfp = [pick(yfm_r[2 + mm], i) for mm in range(2)]
                    pt, bp = next_ps(1)

                    def f_pw(e, pt=pt):
                        for mm in range(2):
                            ins = e.matmul(pt[:, mm * 512:(mm + 1) * 512], lhsT=pwbd[:, mm, :], rhs=dp[mm][0][:, :], start=True, stop=True)
                        return ins
                    sc.op("pe", f_pw, reads=[b_pwbd, dp[0][1], dp[1][1]], writes=bp)
                    for mm in range(2):
                        sc.op("act", lambda e, mm=mm, pt=pt: e.activation(out=yfp[mm][0][:, :], in_=pt[:, mm * 512:(mm + 1) * 512],
                                                                        func=AF.Copy, scale=psc[:, l, mm:mm + 1]),
                              reads=[bp[mm], b_psc], writes=[yfp[mm][1]])
                    for mm in range(2):
                        pf, bf_ = proj_fm(i, 8 + mm, False)
                        sc.op("act", lambda e, mm=mm, pf=pf: e.activation(out=fT[:, mm, i * 512:(i + 1) * 512], in_=pf[:, 0:512], func=AF.Copy),
                              reads=bf_[:1], writes=[b_fTw[i]])
                    uv, b_uv = pick(uv_r, i); sqv, b_sqv = pick(sqv_r, i); vh, b_vh = pick(vh_r, i)
                    yg, b_yg = pick(yg_r, i); ygn, b_ygn = pick(ygn_r, i)
                    vstp = [pick(vst_r[k], i) for k in range(6)]
                    vst = [t[0] for t in vstp]; b_vst = [t[1] for t in vstp]
                    gstp = [pick(gst_r[k], i) for k in range(3)]
                    gst = [t[0] for t in gstp]; b_gst = [t[1] for t in gstp]
                    for c in range(4):
                        pt, bp = next_ps1()

                        def f_uv(e, c=c, pt=pt):
                            for k in range(8):
                                ins = e.matmul(pt[:, 0:512], lhsT=hTx[hs][:, k, 8 + c * 128:8 + (c + 1) * 128], rhs=w_in_sb[:, k, 1280:1792],
                                               start=(k == 0), stop=(k == 7))
                            return ins
                        sc.op("pe", f_uv, reads=[b_win, b_hTx[hs]], writes=bp[:1])
                        sc.op("act", lambda e, c=c, pt=pt: e.activation(out=uv[:, c, :], in_=pt[:, 0:512], func=AF.Copy),
                              reads=bp[:1], writes=[b_uv])
                    group_norm_fm(i, 0, ys)
                    group_norm_fm(i, 1, ys)
                    v4 = uv[:, :, 256:512].rearrange("p n (h c) -> p n h c", h=4)
                    nh = lambda t: t[:, :].rearrange("p (n h) -> p n h", n=4)
                    sc.op("dve", lambda e: e.tensor_reduce(out=nh(vst[0]), in_=v4, axis=AX.X, op=ALU.add),
                          reads=[b_uv], writes=[b_vst[0]])
                    sc.op("act", lambda e: e.activation(out=sqv[:, :, :], in_=uv[:, :, 256:512], func=AF.Square), reads=[b_uv], writes=[b_sqv])
                    sc.op("dve", lambda e: e.tensor_reduce(out=nh(vst[1]), in_=sqv[:, :, :].rearrange("p n (h c) -> p n h c", h=4), axis=AX.X, op=ALU.add),
                          reads=[b_sqv], writes=[b_vst[1]])
                    sc.op("pool", lambda e: e.tensor_scalar(out=vst[2][:, :], in0=vst[0][:, :], scalar1=1.0 / 64, scalar2=None, op0=ALU.mult),
                          reads=[b_vst[0]], writes=[b_vst[2]])
                    sc.op("pool", lambda e: e.tensor_tensor(out=vst[3][:, :], in0=vst[2][:, :], in1=vst[2][:, :], op=ALU.mult),
                          reads=[b_vst[2]], writes=[b_vst[3]])
                    sc.op("dve", lambda e: e.scalar_tensor_tensor(out=vst[4][:, :], in0=vst[1][:, :], scalar=1.0 / 64, in1=vst[3][:, :],
                                                                  op0=ALU.mult, op1=ALU.subtract), reads=[b_vst[1], b_vst[3]], writes=[b_vst[4]])
                    sc.op("act", lambda e: e.activation(out=vst[3][:, :], in_=vst[4][:, :], func=AF.Ln, scale=1.0, bias=epsc[:, 0:1]),
                          reads=[b_vst[4], b_eps], writes=[b_vst[3]])
                    sc.op("act", lambda e: e.activation(out=vst[5][:, :], in_=vst[3][:, :], func=AF.Exp, scale=-0.5), reads=[b_vst[3]], writes=[b_vst[5]])
                    mean_b = nh(vst[2]).unsqueeze(3).to_broadcast([128, 4, 4, 64])
                    rstd_b = nh(vst[5]).unsqueeze(3).to_broadcast([128, 4, 4, 64])
                    sqv4 = sqv[:, :, :].rearrange("p n (h c) -> p n h c", h=4)
                    sc.op("dve", lambda e: e.tensor_tensor(out=sqv4, in0=v4, in1=mean_b, op=ALU.subtract),
                          reads=[b_uv, b_vst[2]], writes=[b_sqv])
                    sc.op("dve", lambda e: e.tensor_tensor(out=vh[:, :, :].rearrange("p n (h c) -> p n h c", h=4), in0=sqv4, in1=rstd_b, op=ALU.mult),
                          reads=[b_sqv, b_vst[5]], writes=[b_vh])
                    pt, bp = next_ps(1)

                    def f_sp(e, pt=pt):
                        for h in range(4):
                            ins = e.matmul(pt[:, h * 256:(h + 1) * 256], lhsT=wsT[:, h, :], rhs=vh[:, :, h * 64:(h + 1) * 64], start=True, stop=True)
                        return ins
                    sc.op("pe", f_sp, reads=[b_wsT, b_vh], writes=bp)
                    for h in range(4):
                        sc.op("dve", lambda e, h=h, pt=pt: e.scalar_tensor_tensor(
                            out=yg[:, :, h * 64:(h + 1) * 64], in0=pt[:, h * 256:(h + 1) * 256].rearrange("p (n c) -> p n c", n=4),
                            scalar=bsp[:, l, h:h + 1], in1=uv[:, :, h * 64:(h + 1) * 64], op0=ALU.add, op1=ALU.mult),
                            reads=[bp[h // 2], b_bsp, b_uv], writes=[b_yg])
                    sc.op("act", lambda e: e.activation(out=sqv[:, :, :], in_=yg[:, :, :], func=AF.Square), reads=[b_yg], writes=[b_sqv])
                    sc.op("dve", lambda e: e.tensor_reduce(out=gst[0][:, :], in_=sqv[:, :, :], axis=AX.X, op=ALU.add), reads=[b_sqv], writes=[b_gst[0]])
                    rstd_small(gst[2][:, :], gst[0][:, :], 256, b_gst[0], b_gst[2], b_gst[1], gst[1][:, :])
                    for n in range(4):
                        sc.op("dve", lambda e, n=n: e.scalar_tensor_tensor(out=ygn[:, n, :], in0=yg[:, n, :], scalar=gst[2][:, n:n + 1], in1=ggm[:, :],
                                                                          op0=ALU.mult, op1=ALU.mult), reads=[b_yg, b_gst[2], b_ggm], writes=[b_ygn])
                    pt, bp = next_ps(1)

                    def f_gt(e, pt=pt):
                        for mm in range(2):
                            for n in range(4):
                                ins = e.matmul(pt[:, mm * 512 + n * 128: mm * 512 + (n + 1) * 128], lhsT=ygn[:, n, mm * 128:(mm + 1) * 128],
                                               rhs=ident[:, :], start=True, stop=True)
                        return ins
                    sc.op("pe", f_gt, reads=[b_ygn, b_ident], writes=bp)
                    sc.op("act", lambda e, pt=pt: e.activation(out=yst[ys][:, 4:6, :], in_=pt[:, :].rearrange("p (m t) -> p m t", m=2), func=AF.Copy),
                          reads=bp, writes=[b_yst[ys]])
                    sc.dma(ysc[i, :, 0:6, :], yst[ys][:, :, :], reads=[b_yst[ys]])

                for i0 in range(min(NX, NB)):
                    load_x(i0)
                for i in range(NB + 1):
                    if i < NB:
                        stageN(i)
                        if i + NX < NB:
                            load_x(i + NX)
                    if i >= 1:
                        stageP(i - 1)
                sc.barrier()
            set_pools({0: [0, 1, 2, 3], 1: [0, 1, 2, 3]})
            with ExitStack() as pb_:
                def sbb(name, shape, dt_):
                    return pb_.enter_context(nc.sbuf_tensor(f"b_{name}_{l}", shape, dt_))
                tcs = sbb("tc", [128, J, 128], BF16); b_tc = Buf(f"tc{l}")
                tss = sbb("ts", [128, J, 128], BF16); b_ts = Buf(f"ts{l}")
                tsn = sbb("tsn", [128, J, 128], BF16); b_tsn = Buf(f"tsn{l}")
                cs3 = sbb("cs3", [J2, J], BF16); b_cs3 = Buf(f"cs3{l}")
                G = sbb("G", [128, 2, J, 256], BF16); b_G = [[Buf(f"G{h_}_{j_}_{l}") for j_ in range(J)] for h_ in range(2)]
                A = sbb("A", [128, J, 2, 128], BF16); b_A = [Buf(f"A{jp_}_{l}") for jp_ in range(J // 2)]
                b_Y = [[Buf(f"Y{h_}_{g_}_{l}") for g_ in range(128)] for h_ in range(2)]
                b_y4st = [Buf(f"y4st{l}_{i_}") for i_ in range(4)]
                Ap_t = [sbb(f"Ap{i}", [J2, 128, 128], BF16) for i in range(2)] if J * 256 < 128 * 128 else None
                stgB = [sbb(f"stgB{i}", [128, 1024], BF16) for i in range(3)]
                b_stgB = [Buf(f"stgB{l}_{i}") for i in range(3)]
                convert_down_out(l, stgB, b_stgB)
                import os as _os4
                if _os4.environ.get("KDEBUG"):
                    print("KDEBUG sbuf remaining phase B", nc.sbuf_bytes_remaining)
                sc.dma(tcs[:, :, :].rearrange("p j k -> p (j k)"), dram["tc"], writes=[b_tc])
                sc.dma(tss[:, :, :].rearrange("p j k -> p (j k)"), dram["ts"], writes=[b_ts])
                sc.dma(tsn[:, :, :].rearrange("p j k -> p (j k)"), dram["tsn"], writes=[b_tsn])
                sc.dma(cs3[:, :], dram["cs3"], writes=[b_cs3])
                for j in range(J):
                    pt, bp = next_ps1()

                    def f_s1(e, pt=pt, j=j):
                        for half in range(2):
                            ins = e.matmul(pt[:, half * 256:(half + 1) * 256], lhsT=fT[:, half, :].rearrange("p (q j) -> p j q", j=J)[:, j, :], rhs=Dh[:, half, :], start=True, stop=True)
                        return ins
                    sc.op("pe", f_s1, reads=[b_fT, b_Dh], writes=bp[:1])
                    eng = "act" if j % 2 == 0 else "dve"
                    if eng == "act":
                        sc.op("act", lambda e, pt=pt, j=j: e.activation(out=G[:, :, j, :], in_=pt[:, 0:512].rearrange("p (h c) -> p h c", h=2), func=AF.Copy),
                              reads=bp[:1], writes=[b_G[0][j], b_G[1][j]])
                    else:
                        sc.op("dve", lambda e, pt=pt, j=j: e.tensor_copy(out=G[:, :, j, :], in_=pt[:, 0:512].rearrange("p (h c) -> p h c", h=2)),
                              reads=bp[:1], writes=[b_G[0][j], b_G[1][j]])
                yT4 = fTflat
                for half in range(2):
                    for j0 in range(0, J, 2):
                        pt, bp = next_ps1()

                        def f_s2(e, pt=pt, j0=j0, half=half):
                            for jj in range(2):
                                j = j0 + jj
                                o = jj * 256
                                e.matmul(pt[:, o:o + 128], lhsT=tcs[:, j, :], rhs=G[:, half, j, 0:128], start=True, stop=False)
                                e.matmul(pt[:, o:o + 128], lhsT=tss[:, j, :], rhs=G[:, half, j, 128:256], start=False, stop=True)
                                e.matmul(pt[:, o + 128:o + 256], lhsT=tcs[:, j, :], rhs=G[:, half, j, 128:256], start=True, stop=False)
                                ins = e.matmul(pt[:, o + 128:o + 256], lhsT=tsn[:, j, :], rhs=G[:, half, j, 0:128], start=False, stop=True)
                            return ins
                        sc.op("pe", f_s2, reads=[b_tc, b_ts, b_tsn, b_G[half][j0], b_G[half][j0 + 1]], writes=bp[:1])
                        eng = "act" if (j0 // 2) % 2 == 0 else "dve"
                        o_ap = A[:, j0:j0 + 2, :, :]
                        i_ap = lambda pt: pt[:, 0:512].rearrange("p (j r c) -> p j r c", j=2, r=2)
                        if eng == "act":
                            sc.op("act", lambda e, pt=pt, o_ap=o_ap: e.activation(out=o_ap, in_=i_ap(pt), func=AF.Copy), reads=bp[:1], writes=[b_A[j0 // 2]])
                        else:
                            sc.op("dve", lambda e, pt=pt, o_ap=o_ap: e.tensor_copy(out=o_ap, in_=i_ap(pt)), reads=bp[:1], writes=[b_A[j0 // 2]])
                    if J * 256 >= 128 * 128:
                        Ap = G[0:J2, half, :, :].rearrange("p j c -> p (j c)")[:, 0:128 * 128].rearrange("p (c k) -> p c k", c=128)
                    else:
                        Ap = Ap_t[half][:, :, :]
                    for c0 in range(0, 128, 8):
                        pt, bp = next_ps()

                        def f_tr(e, pt=pt, c0=c0):
                            for cc in range(8):
                                ins = e.matmul(pt[0:J2, cc * 128:(cc + 1) * 128], lhsT=A[:, :, :, c0 + cc].rearrange("p j r -> p (j r)"), rhs=ident[:, :], start=True, stop=True)
                            return ins
                        sc.op("pe", f_tr, reads=b_A + [b_ident], writes=bp)
                        eng = "act" if (c0 // 8) % 2 == 0 else "dve"
                        gw = [b_G[half][c0 // 2 + t_] for t_ in range(4)] if J * 256 >= 128 * 128 else b_G[half]
                        o_ap = Ap[:, c0:c0 + 8, :]
                        if eng == "act":
                            sc.op("act", lambda e, pt=pt, o_ap=o_ap: e.activation(out=o_ap, in_=pt[0:J2, :].rearrange("p (c k) -> p c k", c=8), func=AF.Copy),
                                  reads=bp, writes=gw)
                        else:
                            sc.op("dve", lambda e, pt=pt, o_ap=o_ap: e.tensor_copy(out=o_ap, in_=pt[0:J2, :].rearrange("p (c k) -> p c k", c=8)),
                                  reads=bp, writes=gw)
                    KB = min(128, 512 // J)
                    for k0 in range(0, 128, KB):
                        pt, bp = next_ps1()

                        def f_s3(e, pt=pt, k0=k0, Ap=Ap):
                            for kk in range(KB):
                                ins = e.matmul(pt[:, kk * J:(kk + 1) * J], lhsT=Ap[:, :, k0 + kk], rhs=cs3[:, :], start=True, stop=True)
                            return ins
                        sc.op("pe", f_s3, reads=b_G[half] + [b_cs3], writes=bp[:1])
                        eng = "act" if (k0 // KB) % 2 == 0 else "dve"
                        o_ap = yT4[:, half, :].rearrange("p (k2 k1) -> p k2 k1", k1=128)[:, :, k0:k0 + KB]
                        if eng == "act":
                            sc.op("act", lambda e, pt=pt, o_ap=o_ap: e.activation(out=o_ap, in_=pt[:, 0:KB * J].rearrange("p (k a) -> p a k", k=KB), func=AF.Copy),
                                  reads=bp[:1], writes=[b_Y[half][k0 // KB]])
                        else:
                            sc.op("dve", lambda e, pt=pt, o_ap=o_ap: e.tensor_copy(out=o_ap, in_=pt[:, 0:KB * J].rearrange("p (k a) -> p a k", k=KB)),
                                  reads=bp[:1], writes=[b_Y[half][k0 // KB]])
                if dbg and "dbg_y4" in dbg_out and l == 0:
                    sc.barrier()
                    sc.dma(dbg_out["dbg_y4"], yT4, reads=[b_fT])
                    sc.barrier()
                for i in range(NB):
                    sc.dma(ysc[i, :, 6:8, :], yT4[:, :, i * 512:(i + 1) * 512], reads=[b_y4st[i % 4], b_fT] + [b for hh in b_Y for b in hh if b.w is not None])
                sc.barrier()
            fstack.close()
            set_pools(CFG["cpools"])
            with ExitStack() as pc_:
                def sbc(name, shape, dt_):
                    return pc_.enter_context(nc.sbuf_tensor(f"c_{name}_{l}", shape, dt_))
                w_out_sb = sbc("w_out", [128, 8, D], BF16); b_wout = Buf(f"wout{l}")
                gC = sbc("gC", [128, 3, D], F32); b_gC = [Buf(f"gC{l}_{i}") for i in range(3)]
                for gi, gk in enumerate(("post_mix_gain", "pre_ffn_gain", "post_ffn_gain")):
                    sc.dma(gC[:, gi, :], dram[gk][l:l + 1, :].partition_broadcast(128), writes=[b_gC[gi]])
                w_dn_sb = sbc("w_dn", [128, NDC, D], BF16); b_wdn = Buf(f"wdn{l}")
                sc.dma(w_out_sb[:, :, :], wos[l].rearrange("k p n -> p k n"), writes=[b_wout])
                b_wdn4 = [Buf(f"wdn{l}_{i}") for i in range(2)]
                sc.dma(w_dn_sb[:, 0:NDC // 2, :], wds[l, 0:NDC // 2].rearrange("k p n -> p k n"), writes=[b_wdn4[0]])
                sc.dma(w_dn_sb[:, NDC // 2:NDC, :], wds[l, NDC // 2:NDC].rearrange("k p n -> p k n"), writes=[b_wdn4[1]])
                NW = 3
                wg_sb = [sbc(f"wg{i}", [128, 8, 128], BF16) for i in range(NW)]; b_wg = [Buf(f"wg{l}_{i}") for i in range(NW)]
                wu_sb = [sbc(f"wu{i}", [128, 8, 128], BF16) for i in range(NW)]; b_wu = [Buf(f"wu{l}_{i}") for i in range(NW)]
                xtc = [sbc(f"xtc{i}", [128, 4, D], F32) for i in range(2)]; b_xtc = [Buf(f"xtC{l}_{i}") for i in range(2)]
                yl = [sbc(f"yl{i}", [128, 8, 512], BF16) for i in range(2)]; b_yl = [Buf(f"yl{l}_{i}") for i in range(2)]
                junkf = sbc("junkf", [128, D], BF16); b_junkc = Buf("junkc")
                sqC = sbc("sqC", [128, 2, 512], BF16); b_sqC = [Buf("sqC0"), Buf("sqC1")]
                sdC = sbc("sdC", [128, 512], F32); b_sdC = Buf("sdC")
                rsC = sbc("rsC", [128, 512], F32); b_rsC = Buf("rsC")
                NTF = CFG["ntf"]
                tctr = [0, 0]
                tmpf = [sbc(f"tmpf{i}", [128, D], F32) for i in range(2 * NTF)]; b_tmpf = [Buf(f"tmpf{i}") for i in range(2 * NTF)]
                st = sbc("st", [128, 9, 4], F32); b_st = [[Buf(f"st{i}_{c}") for c in range(4)] for i in range(9)]
                h2 = sbc("h2", [128, 4, D], BF16); b_h2 = Buf("h2")
                h2Ts = [sbc(f"h2T{i}", [128, 8, 512], BF16) for i in range(2)]; b_h2Ts = [Buf(f"h2T{i}") for i in range(2)]
                actT = sbc("actT", [128, NDC, 512], BF16); b_act = Buf("actT")
                sg = [sbc(f"sg{i}", [128, 512], F32) for i in range(2)]; b_sg = [Buf(f"sg{i}") for i in range(2)]
                wctr = [0]

                import os as _os3
                if _os3.environ.get("KDEBUG"):
                    print("KDEBUG sbuf remaining phase C", nc.sbuf_bytes_remaining)

                def load_blk(i):
                    s_ = i % 2
                    sc.dma(xtc[s_][:, :, :], x_in[i * 512:(i + 1) * 512, :].rearrange("(c p) d -> p c d", p=128), writes=[b_xtc[s_]])
                    sc.dma(yl[s_][:, :, :], ysc[i], writes=[b_yl[s_]])

                def load_w(dc):
                    s_ = wctr[0] % NW
                    wctr[0] += 1
                    sc.dma(wg_sb[s_][:, :, :], wgs[l, dc].rearrange("p (k n) -> p k n", k=8), writes=[b_wg[s_]])
                    sc.dma(wu_sb[s_][:, :, :], wus[l, dc].rearrange("p (k n) -> p k n", k=8), writes=[b_wu[s_]])
                    return s_

                def resid_norm(i, c, pt, bp, gi, s0, final):
                    s_ = i % 2
                    xs = xtc[s_]
                    ring_id = 1 if final else 0
                    k = ring_id * NTF + tctr[ring_id] % NTF
                    tctr[ring_id] += 1
                    tf = tmpf[k]
                    b_tf = b_tmpf[k]
                    sc.op("act", lambda e: e.activation(out=tf[:, :], in_=pt[:, :], func=AF.Copy), reads=bp, writes=[b_tf])
                    sc.op("act", lambda e: e.activation(out=junkf[:, :], in_=tf[:, :], func=AF.Square, accum_out=st[:, s0, c:c + 1]),
                          reads=[b_tf], writes=[b_st[s0][c]])
                    rstd_small(st[:, s0 + 2, c:c + 1], st[:, s0, c:c + 1], D, b_st[s0][c], b_st[s0 + 2][c], b_st[s0 + 1][c], st[:, s0 + 1, c:c + 1], lnexp=False)
                    sc.op("dve", lambda e: e.scalar_tensor_tensor(out=tf[:, :], in0=tf[:, :], scalar=st[:, s0 + 2, c:c + 1], in1=gC[:, gi, :],
                                                                  op0=ALU.mult, op1=ALU.mult), reads=[b_st[s0 + 2][c], b_gC[gi]], writes=[b_tf])
                    sc.op("pool" if final else "dve", lambda e: e.tensor_tensor(out=xs[:, c, :], in0=tf[:, :], in1=xs[:, c, :], op=ALU.add),
                          reads=[b_tf], writes=[b_xtc[s_]])

                def four_norm(i):
                    s_ = i % 2
                    for mm in range(2):
                        sc.op("act", lambda e, mm=mm: e.activation(out=sqC[:, mm, :], in_=yl[s_][:, 6 + mm, :], func=AF.Square),
                              reads=[b_yl[s_]], writes=[b_sqC[mm]])
                    pt, bp = next_ps1(1)

                    def f_st(e, pt=pt):
                        for mm in range(2):
                            ins = e.matmul(pt[:, 0:512], lhsT=ones[:, :], rhs=sqC[:, mm, :], start=(mm == 0), stop=(mm == 1))
                        return ins
                    sc.op("pe", f_st, reads=[b_ones] + b_sqC, writes=bp[:1])
                    sc.op("act", lambda e, pt=pt: e.activation(out=sdC[:, :], in_=pt[:, 0:512], func=AF.Sqrt, scale=1.0 / 256, bias=epsc[:, 0:1]),
                          reads=bp[:1] + [b_eps], writes=[b_sdC])
                    sc.op("dve", lambda e: e.reciprocal(out=rsC[:, :], in_=sdC[:, :]), reads=[b_sdC], writes=[b_rsC])
                    for mm in range(2):
                        sc.op("dve", lambda e, mm=mm: e.scalar_tensor_tensor(out=yl[s_][:, 6 + mm, :], in0=yl[s_][:, 6 + mm, :],
                                                                            scalar=gng[:, l, 4 + mm:5 + mm], in1=rsC[:, :], op0=ALU.mult, op1=ALU.mult),
                              reads=[b_gng, b_rsC], writes=[b_yl[s_]])

                def stageC1(i):
                    s_ = i % 2
                    four_norm(i)
                    h2T = h2Ts[i % 2]; b_h2T = b_h2Ts[i % 2]
                    xs = xtc[s_]
                    for c in range(4):
                        pt, bp = next_ps(1)

                        def f_o(e, pt=pt, c=c):
                            for hf in range(2):
                                for kc in range(8):
                                    wk = (0, 1, 2, 3, 6, 7, 4, 5)[kc]
                                    ins = e.matmul(pt[:, hf * 512:(hf + 1) * 512], lhsT=yl[s_][:, kc, c * 128:(c + 1) * 128],
                                                   rhs=w_out_sb[:, wk, hf * 512:(hf + 1) * 512], start=(kc == 0), stop=(kc == 7))
                            return ins
                        sc.op("pe", f_o, reads=[b_yl[s_], b_wout], writes=bp)
                        resid_norm(i, c, pt, bp, 0, 0, False)
                        sc.op("act", lambda e, c=c: e.activation(out=junkf[:, :], in_=xs[:, c, :], func=AF.Square, accum_out=st[:, 3, c:c + 1]),
                              reads=[b_xtc[s_]], writes=[b_st[3][c]])
                        rstd_small(st[:, 5, c:c + 1], st[:, 3, c:c + 1], D, b_st[3][c], b_st[5][c], b_st[4][c], st[:, 4, c:c + 1], lnexp=False)
                        sc.op("dve", lambda e, c=c: e.scalar_tensor_tensor(out=h2[:, c, :], in0=xs[:, c, :], scalar=st[:, 5, c:c + 1], in1=gC[:, 1, :],
                                                                          op0=ALU.mult, op1=ALU.mult), reads=[b_xtc[s_], b_st[5][c], b_gC[1]], writes=[b_h2])
                    for c in range(4):
                        pt, bp = next_ps(3)

                        def f_tr(e, pt=pt, c=c):
                            for k in range(8):
                                ins = e.matmul(pt[:, k * 128:(k + 1) * 128], lhsT=h2[:, c, k * 128:(k + 1) * 128], rhs=ident[:, :], start=True, stop=True)
                            return ins
                        sc.op("pe", f_tr, reads=[b_h2, b_ident], writes=bp)
                        if c % 2 == 0:
                            sc.op("act", lambda e, pt=pt, c=c: e.activation(out=h2T[:, :, c * 128:(c + 1) * 128], in_=pt[:, :].rearrange("p (k n) -> p k n", k=8), func=AF.Copy),
                                  reads=bp, writes=[b_h2T])
                        else:
                            sc.op("dve", lambda e, pt=pt, c=c: e.tensor_copy(out=h2T[:, :, c * 128:(c + 1) * 128], in_=pt[:, :].rearrange("p (k n) -> p k n", k=8)),
                                  reads=bp, writes=[b_h2T])

                def stageC2(i, slots):
                    h2T = h2Ts[i % 2]; b_h2T = b_h2Ts[i % 2]
                    for dc in range(NDC):
                        ws = slots.pop(0)
                        if dc + NW - 1 < NDC:
                            slots.append(load_w(dc + NW - 1))
                        pt, bp = next_ps()

                        def f_g(e, pt=pt, ws=ws):
                            for k in range(8):
                                e.matmul(pt[:, 0:512], lhsT=wg_sb[ws][:, k, :], rhs=h2T[:, k, :], start=(k == 0), stop=(k == 7))
                            for k in range(8):
                                ins = e.matmul(pt[:, 512:1024], lhsT=wu_sb[ws][:, k, :], rhs=h2T[:, k, :], start=(k == 0), stop=(k == 7))
                            return ins
                        sc.op("pe", f_g, reads=[b_wg[ws], b_wu[ws], b_h2T], writes=bp)
                        sc.op("act", lambda e, pt=pt, dc=dc: e.activation(out=sg[dc % 2][:, :], in_=pt[:, 0:512], func=AF.Silu),
                              reads=bp[:1], writes=[b_sg[dc % 2]])
                        sc.op("dve", lambda e, pt=pt, dc=dc: e.tensor_tensor(out=actT[:, dc, :], in0=pt[:, 512:1024], in1=sg[dc % 2][:, :], op=ALU.mult),
                              reads=[bp[1], b_sg[dc % 2]], writes=[b_act])

                def stageC3(i):
                    s_ = i % 2
                    for c in range(4):
                        pt, bp = next_ps(2)

                        def f_d(e, pt=pt, c=c):
                            for hf in range(2):
                                for dc in range(NDC):
                                    ins = e.matmul(pt[:, hf * 512:(hf + 1) * 512], lhsT=actT[:, dc, c * 128:(c + 1) * 128],
                                                   rhs=w_dn_sb[:, dc, hf * 512:(hf + 1) * 512], start=(dc == 0), stop=(dc == NDC - 1))
                            return ins
                        sc.op("pe", f_d, reads=[b_act] + b_wdn4, writes=bp)
                        resid_norm(i, c, pt, bp, 2, 6, True)
                    sc.dma(x_out[i * 512:(i + 1) * 512, :].rearrange("(c p) d -> p c d", p=128), xtc[s_][:, :, :], reads=[b_xtc[s_]])

                load_blk(0)
                if NB > 1:
                    load_blk(1)
                for i in range(NB):
                    slots = [load_w(dc) for dc in range(NW - 1)]
                    stageC1(i)
                    stageC2(i, slots)
                    stageC3(i)
                    if i + 2 < NB:
                        load_blk(i + 2)
                sc.barrier()
        for l_ in range(L):
            sc.mark()
            layer(l_)
            sc.retire()
        sc.emit()
    return nc


_CACHE = {}


def kernel(**inputs):
    S = inputs["x"].shape[1]
    L = inputs["w_in"].shape[0]
    B = inputs["x"].shape[0]
    key = (S, L)
    if key not in _CACHE:
        _CACHE[key] = (build(S, L), host_consts(S))
    nc, consts = _CACHE[key]
    shared = {k: np.ascontiguousarray(np.asarray(inputs[k], dtype=np.float32)) for k in PARAM_SPECS(L)}
    shared.update(consts)
    x = np.asarray(inputs["x"], dtype=np.float32)
    in_maps = []
    for b in range(B):
        m = dict(shared)
        m["x"] = np.ascontiguousarray(x[b])
        in_maps.append(m)
    res = run_bass_kernel_spmd(nc, in_maps, core_ids=list(range(B)))
    return np.stack([np.asarray(r["out"], dtype=np.float32) for r in res.results], axis=0)
```
